# Optimizing a Trainium2 kernel written in Bass

```python
import math
import jax, jax.numpy as jnp
from jax import lax
import numpy as np

D_MODEL = 1024
BATCH = 2
SEQ = 8192
DEPTH = 2
DEC_BATCH = 128
DEC_SEQ = 8
PAST_LEN = 2048
PAGE_SIZE = 128

RET_HEADS = 4
RET_DK = 128
RET_DV = 256
RET_QK = RET_HEADS * RET_DK
RET_V = RET_HEADS * RET_DV
RET_CHUNK = 128
ROPE_BASE = 10000.0
POOL_WINDOWS = (2, 4, 8, 16)
POOL_GROUPS = 4
POOL_GDIM = 128
POOL_WIDTH = POOL_GROUPS * POOL_GDIM
POOL_BUF = 15
NSA_HEADS = 8
NSA_KV_HEADS = 2
NSA_GROUP = NSA_HEADS // NSA_KV_HEADS
NSA_DH = 64
NSA_WIDTH = NSA_HEADS * NSA_DH
NSA_KV = NSA_KV_HEADS * NSA_DH
L_CMP = 32
L_SEL = 64
N_SEL = 16
WINDOW = 512
Q_BLOCK = 128
FORCE_SCORE = 1e4
N_BUCKETS = 32
MAX_DIST = 128
EPS = 1e-6
IN_SIZES = (RET_QK, RET_QK, RET_V, RET_V,
            POOL_WIDTH, POOL_WIDTH,
            NSA_WIDTH, NSA_KV, NSA_KV, NSA_KV, NSA_KV, NSA_KV, NSA_KV, 3 * NSA_HEADS, NSA_WIDTH,
            3 * D_MODEL)
D_IN = sum(IN_SIZES)

kernel_name = "hybrid_retention_pool_nsa_decoder_step"


def rms_norm(x, g):
    xf = x.astype(jnp.float32)
    y = xf * lax.rsqrt(jnp.mean(xf * xf, axis=-1, keepdims=True) + EPS) * g.astype(jnp.float32)
    return y.astype(x.dtype)


def masked_softmax(s, valid):
    s = jnp.where(valid, s, -jnp.inf)
    m = jnp.max(s, axis=-1, keepdims=True)
    m = jnp.where(jnp.isfinite(m), m, 0.0)
    e = jnp.where(valid, jnp.exp(s - m), 0.0)
    return e / jnp.maximum(jnp.sum(e, axis=-1, keepdims=True), 1e-30)


def t5_bucket(dist):
    n = jnp.maximum(dist, 0)
    exact = N_BUCKETS // 2
    nf = jnp.maximum(n, 1).astype(jnp.float32)
    large = exact + (jnp.log(nf / exact) / math.log(MAX_DIST / exact) * (N_BUCKETS - exact)).astype(jnp.int32)
    large = jnp.minimum(large, N_BUCKETS - 1)
    return jnp.where(n < exact, n, large)


def rotary(x, pos):
    d = x.shape[-1]
    inv = ROPE_BASE ** (-jnp.arange(0, d, 2, dtype=jnp.float32) / d)
    ang = pos.astype(jnp.float32)[:, None] * inv[None, :]
    cos = jnp.cos(ang)[None, :, None, :]
    sin = jnp.sin(ang)[None, :, None, :]
    x1, x2 = x[..., 0::2], x[..., 1::2]
    return jnp.stack([x1 * cos - x2 * sin, x1 * sin + x2 * cos], axis=-1).reshape(x.shape)


def retention(q, k, v, s0, pos0):
    n, t, h, _ = q.shape
    f32 = jnp.float32
    pos = pos0 + jnp.arange(t)
    q = rotary(q.astype(f32), pos)
    k = rotary(k.astype(f32), pos) * (RET_DK ** -0.5)
    v = v.astype(f32)
    c = RET_CHUNK if t % RET_CHUNK == 0 else t
    nc = t // c
    log_g = jnp.log1p(-jnp.exp2(-5.0 - jnp.arange(h, dtype=f32)))
    idx = jnp.arange(c, dtype=f32)
    rel = idx[:, None] - idx[None, :]
    dmask = jnp.where(rel[None] >= 0, jnp.exp(jnp.maximum(rel, 0.0)[None] * log_g[:, None, None]), 0.0)
    cross = jnp.exp((idx + 1.0)[None, :] * log_g[:, None])
    tail = jnp.exp((c - 1.0 - idx)[None, :] * log_g[:, None])
    chunk_decay = jnp.exp(c * log_g)

    def to_chunks(a):
        return a.reshape(n, nc, c, h, a.shape[-1]).transpose(1, 0, 3, 2, 4)

    def step(s, xs):
        qc, kc, vc = xs
        att = jnp.einsum('nhid,nhjd->nhij', qc, kc) * dmask
        o = jnp.einsum('nhij,nhje->nhie', att, vc) + jnp.einsum('nhid,nhde->nhie', qc, s) * cross[None, :, :, None]
        s = s * chunk_decay[None, :, None, None] + jnp.einsum('nhjd,nhje->nhde', kc * tail[None, :, :, None], vc)
        return s, o

    s_new, o = lax.scan(step, s0.astype(f32), (to_chunks(q), to_chunks(k), to_chunks(v)))
    o = o.transpose(1, 0, 3, 2, 4).reshape(n, t, h, RET_DV)
    o = o * lax.rsqrt(jnp.mean(o * o, axis=-1, keepdims=True) + EPS)
    return o.reshape(n, t, h * RET_DV), s_new


def pool_mix(u, buf, pos0, w_pool, pool_scale):
    n, t, cw = u.shape
    ext = jnp.concatenate([buf.astype(u.dtype), u], axis=1)
    extf = ext.astype(jnp.float32)
    cs = jnp.pad(jnp.cumsum(extf, axis=1), ((0, 0), (1, 0), (0, 0)))
    cnt_pos = pos0 + jnp.arange(t)
    means = []
    for g, w in enumerate(POOL_WINDOWS):
        sl = slice(g * POOL_GDIM, (g + 1) * POOL_GDIM)
        hi = cs[:, POOL_BUF + 1:POOL_BUF + 1 + t, sl]
        lo = cs[:, POOL_BUF + 1 - w:POOL_BUF + 1 - w + t, sl]
        cnt = jnp.minimum(cnt_pos + 1, w).astype(jnp.float32)
        means.append((hi - lo) / cnt[None, :, None])
    d = (jnp.concatenate(means, axis=-1) - extf[:, POOL_BUF:]).reshape(n, t, POOL_GROUPS, POOL_GDIM)
    y = jnp.einsum('ntgc,gce->ntge', d, w_pool.astype(jnp.float32)).reshape(n, t, cw)
    y = y * pool_scale.astype(jnp.float32)
    return y, ext[:, ext.shape[1] - POOL_BUF:]


def nsa_attention(q, kv_cmp, kv_sel, kv_win, pos0, w0, pe_cmp, w_ck, w_cv, rel_bias, br_gate):
    n, t, h, dh = q.shape
    L = kv_cmp.shape[1]
    f32 = jnp.float32
    qg = q.astype(f32).reshape(n, t, NSA_KV_HEADS, NSA_GROUP, dh) * (dh ** -0.5)
    q_pos = pos0 + jnp.arange(t)
    rb = rel_bias.astype(f32)
    bias_g = rb.reshape(N_BUCKETS, NSA_KV_HEADS, NSA_GROUP).transpose(1, 0, 2)

    nc = L // L_CMP
    blk = kv_cmp[:, :nc * L_CMP].astype(f32).reshape(n, nc, L_CMP, 2, NSA_KV_HEADS, dh) + pe_cmp.astype(f32)
    cm = jnp.mean(blk, axis=2)
    kc = jnp.einsum('ncgd,de->ncge', cm[:, :, 0], w_ck.astype(f32))
    vc = jnp.einsum('ncgd,de->ncge', cm[:, :, 1], w_cv.astype(f32))
    dist_c = q_pos[:, None] - (jnp.arange(nc) * L_CMP + L_CMP - 1)[None, :]
    bias_c = rb[t5_bucket(dist_c)].reshape(t, nc, NSA_KV_HEADS, NSA_GROUP).transpose(0, 2, 3, 1)
    s_c = jnp.einsum('ntgrd,ncgd->ntgrc', qg, kc) + bias_c[None]
    p_c = masked_softmax(s_c, (dist_c >= 0)[None, :, None, None, :])
    o_cmp = jnp.einsum('ntgrc,ncgd->ntgrd', p_c, vc)

    n_sel = -(-L // L_SEL)
    ratio = L_SEL // L_CMP
    imp = jnp.pad(jnp.sum(p_c, axis=3), ((0, 0), (0, 0), (0, 0), (0, n_sel * ratio - nc)))
    imp = imp.reshape(n, t, NSA_KV_HEADS, n_sel, ratio).sum(-1)
    blk_id = jnp.arange(n_sel)
    forced = (blk_id[None, :] == 0) | (blk_id[None, :] == (q_pos // L_SEL)[:, None])
    avail = blk_id[None, :] * L_SEL <= q_pos[:, None]
    imp = jnp.where(forced[None, :, None, :], FORCE_SCORE, imp)
    imp = jnp.where(avail[None, :, None, :], imp, -jnp.inf)
    k_top = min(N_SEL, n_sel)
    sel_idx = lax.top_k(imp, k_top)[1]

    lp = n_sel * L_SEL
    kvs = jnp.pad(kv_sel.astype(f32), ((0, 0), (0, lp - L), (0, 0), (0, 0), (0, 0)))
    kvs = kvs.reshape(n, n_sel, L_SEL, 2, NSA_KV_HEADS, dh).transpose(0, 4, 1, 2, 3, 5)
    kvw = jnp.pad(kv_win.astype(f32), ((0, 0), (WINDOW, 0), (0, 0), (0, 0), (0, 0)))
    qb_size = Q_BLOCK if t % Q_BLOCK == 0 else t
    nqb = t // qb_size
    span = WINDOW + qb_size
    gather_blocks = jax.vmap(jax.vmap(lambda kb, ib: kb[ib]))
    lookup = jax.vmap(lambda tab, b: tab[b], in_axes=(0, 1), out_axes=1)

    def block(args):
        qb, ib, i = args
        p0 = pos0 + i * qb_size
        qp = p0 + jnp.arange(qb_size)
        ibt = ib.transpose(0, 2, 1, 3)
        kvg = gather_blocks(kvs, ibt).reshape(n, NSA_KV_HEADS, qb_size, k_top * L_SEL, 2, dh)
        spos = (ibt[..., None] * L_SEL + jnp.arange(L_SEL)).reshape(n, NSA_KV_HEADS, qb_size, k_top * L_SEL)
        dist_s = qp[None, None, :, None] - spos
        bias_s = lookup(bias_g, t5_bucket(dist_s)).transpose(0, 1, 2, 4, 3)
        s_s = jnp.einsum('nqgrd,ngqsd->ngqrs', qb, kvg[..., 0, :]) + bias_s
        p_s = masked_softmax(s_s, (dist_s >= 0)[:, :, :, None, :])
        o_s = jnp.einsum('ngqrs,ngqsd->nqgrd', p_s, kvg[..., 1, :])
        kw = lax.dynamic_slice_in_dim(kvw, p0 - w0, span, axis=1)
        wpos = p0 - WINDOW + jnp.arange(span)
        dist_w = qp[:, None] - wpos[None, :]
        valid_w = (dist_w >= 0) & (dist_w < WINDOW) & (wpos >= max(w0, 0))[None, :]
        bias_w = rb[t5_bucket(dist_w)].reshape(qb_size, span, NSA_KV_HEADS, NSA_GROUP).transpose(0, 2, 3, 1)
        s_w = jnp.einsum('nqgrd,nsgd->nqgrs', qb, kw[:, :, 0]) + bias_w[None]
        p_w = masked_softmax(s_w, valid_w[None, :, None, None, :])
        o_w = jnp.einsum('nqgrs,nsgd->nqgrd', p_w, kw[:, :, 1])
        return o_s, o_w

    qblk = qg.reshape(n, nqb, qb_size, NSA_KV_HEADS, NSA_GROUP, dh).transpose(1, 0, 2, 3, 4, 5)
    iblk = sel_idx.reshape(n, nqb, qb_size, NSA_KV_HEADS, k_top).transpose(1, 0, 2, 3, 4)
    o_sel, o_win = lax.map(block, (qblk, iblk, jnp.arange(nqb)))

    def back(o):
        return o.transpose(1, 0, 2, 3, 4, 5).reshape(n, t, NSA_KV_HEADS, NSA_GROUP, dh)

    g = br_gate.astype(f32).reshape(n, t, 3, NSA_KV_HEADS, NSA_GROUP)[..., None]
    o = g[:, :, 0] * o_cmp + g[:, :, 1] * back(o_sel) + g[:, :, 2] * back(o_win)
    return o.reshape(n, t, h * dh)


def mixer_layer(x, pos0, s_ret, pool_buf, cmp_past, sel_past, win_past,
                gain, w_in, w_pool, pool_scale, pe_cmp, w_ck, w_cv,
                w_br_a, w_br_b, w_br_c, w_out, rel_bias):
    n, t, _ = x.shape
    f32 = jnp.float32
    z = rms_norm(x, gain) @ w_in
    split_at = np.cumsum(IN_SIZES)[:-1].tolist()
    (rq, rk, rv, rg, pu, pg, nq, ck, cv, sk, sv, wk, wv, nbg, ng, mg) = jnp.split(z, split_at, axis=-1)
    o_a, s_ret_new = retention(rq.reshape(n, t, RET_HEADS, RET_DK), rk.reshape(n, t, RET_HEADS, RET_DK),
                               rv.reshape(n, t, RET_HEADS, RET_DV), s_ret, pos0)
    b_a = (jax.nn.silu(rg.astype(f32)) * o_a).astype(x.dtype) @ w_br_a
    o_b, pool_new = pool_mix(pu, pool_buf, pos0, w_pool, pool_scale)
    b_b = (jax.nn.silu(pg.astype(f32)) * o_b).astype(x.dtype) @ w_br_b
    def kv(a, b):
        return jnp.stack([a.reshape(n, t, NSA_KV_HEADS, NSA_DH), b.reshape(n, t, NSA_KV_HEADS, NSA_DH)], axis=2)
    cmp_new, sel_new, win_new = kv(ck, cv), kv(sk, sv), kv(wk, wv)
    win_full = jnp.concatenate([win_past.astype(x.dtype), win_new], axis=1)
    o_c = nsa_attention(nq.reshape(n, t, NSA_HEADS, NSA_DH),
                        jnp.concatenate([cmp_past.astype(x.dtype), cmp_new], axis=1),
                        jnp.concatenate([sel_past.astype(x.dtype), sel_new], axis=1),
                        win_full, pos0, pos0 - win_past.shape[1], pe_cmp, w_ck, w_cv, rel_bias,
                        jax.nn.sigmoid(nbg.astype(f32)))
    b_c = (jax.nn.silu(ng.astype(f32)) * o_c).astype(x.dtype) @ w_br_c
    gates = jax.nn.sigmoid(mg.astype(f32)).reshape(n, t, 3, D_MODEL)
    merged = gates[:, :, 0] * b_a + gates[:, :, 1] * b_b + gates[:, :, 2] * b_c
    y = x + merged.astype(x.dtype) @ w_out
    keep = min(WINDOW, win_full.shape[1])
    return y, (s_ret_new, pool_new, win_full[:, win_full.shape[1] - keep:], cmp_new, sel_new)


def setup_inputs(seed: int = 0) -> dict:
    key = jax.random.key(seed)
    ks = jax.random.split(key, 24)
    f32 = jnp.float32
    n_pages = PAST_LEN // PAGE_SIZE
    n_used = DEC_BATCH * n_pages
    n_phys = n_used + max(1, n_used // 4)
    w_buf = min(WINDOW, PAST_LEN)

    def nrm(k, shape, s):
        return s * jax.random.normal(k, shape, f32)

    page_table = jax.random.permutation(ks[7], n_phys)[:n_used].reshape(DEC_BATCH, n_pages).astype(jnp.int32)
    return {
        "x_prompt": nrm(ks[0], (BATCH, SEQ, D_MODEL), 1.0),
        "x_sample": nrm(ks[1], (DEC_BATCH, DEC_SEQ, D_MODEL), 1.0),
        "state_ret": nrm(ks[2], (DEPTH, DEC_BATCH, RET_HEADS, RET_DK, RET_DV), 0.5),
        "state_pool": nrm(ks[3], (DEPTH, DEC_BATCH, POOL_BUF, POOL_WIDTH), 1.0),
        "cache_win": nrm(ks[4], (DEPTH, DEC_BATCH, w_buf, 2, NSA_KV_HEADS, NSA_DH), 1.0),
        "cache_cmp": nrm(ks[5], (DEPTH, n_phys, PAGE_SIZE, 2, NSA_KV_HEADS, NSA_DH), 1.0),
        "cache_sel": nrm(ks[6], (DEPTH, n_phys, PAGE_SIZE, 2, NSA_KV_HEADS, NSA_DH), 1.0),
        "page_table": page_table,
        "norm_gain": 1.0 + nrm(ks[8], (DEPTH, D_MODEL), 0.02),
        "w_in": nrm(ks[9], (DEPTH, D_MODEL, D_IN), D_MODEL ** -0.5),
        "w_pool": nrm(ks[10], (DEPTH, POOL_GROUPS, POOL_GDIM, POOL_GDIM), POOL_GDIM ** -0.5),
        "pool_scale": 1.0 + nrm(ks[11], (DEPTH, POOL_WIDTH), 0.02),
        "pe_cmp": nrm(ks[12], (DEPTH, L_CMP, 2, NSA_KV_HEADS, NSA_DH), 0.1),
        "w_ck": nrm(ks[13], (DEPTH, NSA_DH, NSA_DH), NSA_DH ** -0.5),
        "w_cv": nrm(ks[14], (DEPTH, NSA_DH, NSA_DH), NSA_DH ** -0.5),
        "w_br_a": nrm(ks[15], (DEPTH, RET_V, D_MODEL), RET_V ** -0.5),
        "w_br_b": nrm(ks[16], (DEPTH, POOL_WIDTH, D_MODEL), POOL_WIDTH ** -0.5),
        "w_br_c": nrm(ks[17], (DEPTH, NSA_WIDTH, D_MODEL), NSA_WIDTH ** -0.5),
        "w_out": nrm(ks[18], (DEPTH, D_MODEL, D_MODEL), D_MODEL ** -0.5),
        "rel_bias": nrm(ks[19], (N_BUCKETS, NSA_HEADS), 0.5),
        "final_gain": 1.0 + nrm(ks[20], (D_MODEL,), 0.02),
    }


def reference(x_prompt, x_sample, state_ret, state_pool, cache_win, cache_cmp, cache_sel, page_table,
              norm_gain, w_in, w_pool, pool_scale, pe_cmp, w_ck, w_cv, w_br_a, w_br_b, w_br_c, w_out,
              rel_bias, final_gain):
    b = x_prompt.shape[0]
    nb = x_sample.shape[0]
    past_len = page_table.shape[1] * PAGE_SIZE
    zero_kv = jnp.zeros((b, 0, 2, NSA_KV_HEADS, NSA_DH), x_prompt.dtype)
    hp, hs = x_prompt, x_sample
    new_p = [[] for _ in range(5)]
    new_s = [[] for _ in range(5)]
    for l in range(DEPTH):
        wl = (norm_gain[l], w_in[l], w_pool[l], pool_scale[l], pe_cmp[l], w_ck[l], w_cv[l],
              w_br_a[l], w_br_b[l], w_br_c[l], w_out[l], rel_bias)
        hp, st_p = mixer_layer(hp, 0, jnp.zeros((b, RET_HEADS, RET_DK, RET_DV), jnp.float32),
                               jnp.zeros((b, POOL_BUF, POOL_WIDTH), x_prompt.dtype),
                               zero_kv, zero_kv, zero_kv, *wl)
        past_cmp = cache_cmp[l][page_table].reshape(nb, past_len, 2, NSA_KV_HEADS, NSA_DH)
        past_sel = cache_sel[l][page_table].reshape(nb, past_len, 2, NSA_KV_HEADS, NSA_DH)
        hs, st_s = mixer_layer(hs, past_len, state_ret[l], state_pool[l],
                               past_cmp, past_sel, cache_win[l], *wl)
        for j in range(5):
            new_p[j].append(st_p[j])
            new_s[j].append(st_s[j])
    y_prompt = rms_norm(hp, final_gain)
    y_sample = rms_norm(hs, final_gain)
    return (y_prompt, y_sample,
            jnp.stack(new_p[0]), jnp.stack(new_s[0]),
            jnp.stack(new_p[1]), jnp.stack(new_s[1]),
            jnp.stack(new_p[2]), jnp.stack(new_s[2]),
            jnp.stack(new_p[3]), jnp.stack(new_s[3]),
            jnp.stack(new_p[4]), jnp.stack(new_s[4]))
```

```python
import math
from contextlib import ExitStack
import numpy as np
import ml_dtypes
import concourse.bass as bass
import concourse.mybir as mybir
from concourse.bass_utils import run_bass_kernel_spmd

F32 = mybir.dt.float32
BF16 = mybir.dt.bfloat16
I32 = mybir.dt.int32
AF = mybir.ActivationFunctionType
ALU = mybir.AluOpType
AX = mybir.AxisListType

D = 1024
SEQ = 8192
N_CORES = 8
NS = 16
TS = 8
PAST = 2048
NPAGE = 16
PAGE = 128
NPHYS = 2560
DEPTH = 2
EPS = 1e-6
RH, DK, DV = 4, 128, 256
NEG = -30000.0

O_RQ, O_RK, O_RV, O_RG, O_PU, O_PG, O_NQ, O_KV, O_NBG, O_NG, O_MG = (
    0, 512, 1024, 2048, 3072, 3584, 4096, 4608, 5376, 5400, 5912)
NFM = 56
TM_RV, TM_RG, TM_NG, TM_KVR, TM_PUT = 7168, 8192, 9216, 9728, 10520
NCA = 11032


def _swap(idx):
    return idx.reshape(-1, 2)[:, ::-1].reshape(-1)


def _col_index():
    ar = np.arange
    fm = []
    for h in range(4):
        fm += [O_RQ + h * 128 + ar(128), _swap(O_RQ + h * 128 + ar(128))]
    for h in range(4):
        fm += [O_RK + h * 128 + ar(128), _swap(O_RK + h * 128 + ar(128))]
    fm += [O_PU + g * 128 + ar(128) for g in range(4)]
    fm += [O_PG + g * 128 + ar(128) for g in range(4)]
    for c in range(4):
        fm += [np.concatenate([O_NQ + c * 64 + ar(64), O_NQ + (4 + c) * 64 + ar(64)])]
    fm += [O_KV + 0 + ar(128), O_KV + 128 + ar(128), O_KV + 256 + ar(128), O_KV + 512 + ar(128)]
    fm += [O_MG + j * 128 + ar(128) for j in range(24)]
    tm = [O_RV + ar(1024), O_RG + ar(1024), O_NG + ar(512), O_KV + ar(768), O_NBG + ar(24), O_PU + ar(512)]
    idx = np.concatenate(fm + tm)
    assert idx.shape[0] == NCA
    return idx


class Reg:
    __slots__ = ("w", "rs")

    def __init__(self):
        self.w = None
        self.rs = {}


class Sched:
    EPOCH = 30000

    def __init__(self, nc, n_dma_sems=40):
        self.nc = nc
        self.engs = {"pe": nc.tensor, "act": nc.scalar, "dve": nc.vector,
                     "pool": nc.gpsimd, "sp": nc.sync}
        self.sems = []
        self._ctx = []
        self.cur = {}
        self.cnt = {}
        self.last = {}
        for k in self.engs:
            self.cur[k] = self._new_sem("s_" + k)
            self.cnt[k] = 0
            self.last[k] = []
        self.dkeys, self.dval = [], []
        for i in range(n_dma_sems):
            self.dkeys.append(self._new_sem("d%d" % i))
            self.dval.append(0)
        self.dnext = 0
        self.known = {k: {} for k in self.engs}
        self.n_ins = 0
        self.n_wait = 0

    def _new_sem(self, name):
        cm = self.nc.semaphore(name + "_%d" % len(self.sems))
        self.sems.append(cm.__enter__())
        self._ctx.append(cm)
        return len(self.sems) - 1

    def close(self):
        for cm in reversed(self._ctx):
            cm.__exit__(None, None, None)

    def _wait(self, e, dep):
        if dep is None:
            return
        key, val = dep
        if key == self.cur[e] and e in ("pe", "sp"):
            return
        if self.known[e].get(key, 0) >= val:
            return
        self.engs[e].wait_ge(self.sems[key], val)
        self.known[e][key] = val
        self.n_wait += 1

    def _deps(self, e, r, w):
        for x in r:
            self._wait(e, x.w)
        for x in w:
            self._wait(e, x.w)
            for k, v in list(x.rs.items()):
                self._wait(e, (k, v))

    def _mark(self, tok, r, w):
        k, v = tok
        for x in r:
            if x.rs.get(k, 0) < v:
                x.rs[k] = v
        for x in w:
            x.w = tok
            x.rs = {}

    def op(self, e, fn, r=(), w=()):
        if self.cnt[e] >= self.EPOCH:
            self.last[e].append((self.cur[e], self.cnt[e]))
            self.cur[e] = self._new_sem("s_" + e)
            self.cnt[e] = 0
        self._deps(e, r, w)
        ins = fn(self.engs[e])
        self.cnt[e] += 1
        ins.then_inc(self.sems[self.cur[e]], 1)
        self._mark((self.cur[e], self.cnt[e]), r, w)
        self.n_ins += 1

    def dma(self, e, fn, r=(), w=()):
        i = self.dnext
        self.dnext = (self.dnext + 1) % len(self.dkeys)
        k = self.dkeys[i]
        if self.dval[i] > 0:
            self._wait(e, (k, self.dval[i]))
        self._deps(e, r, w)
        ins = fn(self.engs[e])
        self.dval[i] += 16
        ins.then_inc(self.sems[k], 16)
        self._mark((k, self.dval[i]), r, w)
        self.n_ins += 1

    def _all_tokens(self):
        toks = []
        for i, k in enumerate(self.dkeys):
            if self.dval[i] > 0:
                toks.append((k, self.dval[i]))
        for e in ("pe", "act", "dve", "pool"):
            toks += self.last[e]
            if self.cnt[e] > 0:
                toks.append((self.cur[e], self.cnt[e]))
        return toks

    def finish(self):
        for t in self._all_tokens():
            self._wait("sp", t)

    def barrier_all(self):
        toks = self._all_tokens()
        for e in ("sp", "pe", "act", "dve", "pool"):
            for t in toks:
                if t[0] != self.cur[e]:
                    self._wait(e, t)


class Tl:
    def __init__(self, h):
        self.h = h
        self.r = Reg()

    def __getitem__(self, k):
        return self.h[k]


class Vw:
    def __init__(self, base, ap):
        self.h = ap
        self.r = base.r

    def __getitem__(self, k):
        return self.h[k]


def _t5_bucket(dist):
    n = np.maximum(np.asarray(dist, np.int64), 0)
    nf = np.maximum(n, 1).astype(np.float32)
    lg = np.log(nf / np.float32(16)).astype(np.float32) / np.float32(math.log(128 / 16))
    large = 16 + (lg.astype(np.float32) * np.float32(16)).astype(np.int32)
    large = np.minimum(large, 31)
    return np.where(n < 16, n, large).astype(np.int32)


def _onehot_cols(dist, valid):
    b = _t5_bucket(np.asarray(dist, np.int32))
    oh = np.zeros((32,) + dist.shape, np.float32)
    for k in range(32):
        oh[k] = ((b == k) & valid).astype(np.float32)
    return oh


def _host_tables(seq):
    f32 = np.float32
    tb = {}
    inv = (10000.0 ** (-np.arange(0, DK, 2, dtype=f32) / DK)).astype(f32)

    def rot(pos):
        ang = pos.astype(f32)[None, :] * np.repeat(inv, 2)[:, None]
        c = np.cos(ang).astype(f32)
        s = np.sin(ang).astype(f32)
        sign = np.where(np.arange(DK) % 2 == 0, -1.0, 1.0).astype(f32)[:, None]
        ss = s * sign
        sc = f32(DK ** -0.5)
        return np.stack([c, ss, c * sc, ss * sc]).astype(f32)

    tb["rotp"] = rot(np.arange(seq))
    tb["rots"] = rot(PAST + (np.arange(128) % TS))
    log_g = np.log1p(-np.exp2(-5.0 - np.arange(RH, dtype=f32))).astype(f32)

    def ret_tabs(c, same):
        idx = np.arange(128)
        loc = idx % c
        rel = (loc[None, :] - loc[:, None]).astype(f32)
        dm = np.zeros((128, RH, 128), f32)
        cr = np.zeros((1, RH, 128), f32)
        tl = np.zeros((128, RH), f32)
        for h in range(RH):
            m = np.where((rel >= 0) & same, np.exp(np.maximum(rel, 0) * log_g[h]), 0.0)
            dm[:, h, :] = m
            cr[0, h, :] = np.exp((loc + 1.0) * log_g[h])
            tl[:, h] = np.exp((c - 1.0 - loc) * log_g[h])
        return dm, cr, tl

    allsame = np.ones((128, 128), bool)
    seqid = np.arange(128) // TS
    sames = seqid[:, None] == seqid[None, :]
    tb["dmt_p"], tb["cr_p"], tb["tail_p"] = ret_tabs(128, allsame)
    tb["dmt_s"], tb["cr_s"], tb["tail_s"] = ret_tabs(TS, sames)
    tb["gdec_p"] = [float(np.exp(128 * log_g[h])) for h in range(RH)]
    tb["gdec_s"] = [float(np.exp(TS * log_g[h])) for h in range(RH)]
    pin0 = np.zeros((1, 4, 128), f32)
    pin1 = np.zeros((1, 4, 128), f32)
    for g, w in enumerate((2, 4, 8, 16)):
        pin0[0, g] = 1.0 / np.minimum(np.arange(128) + 1, w)
        pin1[0, g] = 1.0 / w
    tb["pinv0"], tb["pinv1"] = pin0, pin1
    k = np.arange(128)
    cols, valid = [], []
    for dl in (0, 1):
        for q in range(128):
            d = 128 * dl + q - k
            cols.append(d); valid.append(d >= 0)
    for t in range(TS):
        d = 128 + t - k
        cols.append(d); valid.append(d >= 0)
    for c in range(128):
        d = (c % TS) - (k % TS)
        cols.append(d); valid.append((d >= 0) & ((c // TS) == (k // TS)))
    for x in range(8):
        d = k - 32 * (x - 4) - 31
        cols.append(d); valid.append(d >= 0)
    for x in range(4):
        d = PAST + (k % TS) - 32 * (60 + x) - 31
        cols.append(d); valid.append(d >= 0)
    cols = np.stack(cols); valid = np.stack(valid)
    tb["oh"] = _onehot_cols(cols, valid)
    tb["ohvalid"] = valid.T.astype(f32).copy()
    tb["mw4"] = np.tile((k[None, :] < k[:, None]).astype(f32), (1, 4))
    tb["mw0s"] = np.tile((k[:, None] > np.arange(TS)[None, :]).astype(f32), (1, 4))
    xm = np.where(seqid[:, None] == np.arange(NS)[None, :], 0.0, NEG).astype(f32)
    tb["xm"] = xm
    tb["sm"] = (seqid[:, None] == np.arange(NS)[None, :]).astype(f32)
    smrow = np.zeros((1, NS, 128), f32)
    for s in range(NS):
        smrow[0, s, s * TS:(s + 1) * TS] = 1.0
    tb["smrow"] = smrow
    rv0 = (k >= 31).astype(f32)[:, None]
    tb["rv0"] = rv0
    fixk = np.zeros((128, 2), f32); fixa = np.zeros((128, 2), f32)
    fixk[:, 0] = (k >= 64); fixa[:, 0] = np.where(k < 64, 2e4, 0.0)
    fixk[:, 1] = 0.0; fixa[:, 1] = np.where(k < 64, -1e30, 2e4)
    tb["fixk"], tb["fixa"] = fixk, fixa
    nkb = max(seq // 128, NPAGE)
    fexp = np.zeros((128, nkb * 128), f32)
    for m in range(nkb * 128):
        if m // 64 < 128:
            fexp[m // 64, m] = 1.0
    tb["fexp"] = fexp
    ind = np.zeros((128, 4), f32)
    for r in range(128):
        ind[r, r // 32] = 1.0 / 32
    tb["ind"] = ind
    tb["iota"] = np.arange(128, dtype=f32)[:, None].copy()
    tb["c32"] = np.full((32, 1), 1.0 / 32, f32)
    tb["ident"] = np.eye(128, dtype=f32)
    return tb


TABLE_SHAPES = lambda seq: {
    "rotp": [4, 128, seq], "rots": [4, 128, 128],
    "dmt_p": [128, 512], "dmt_s": [128, 512], "cr_p": [1, 512], "cr_s": [1, 512],
    "tail_p": [128, 4], "tail_s": [128, 4], "pinv0": [1, 512], "pinv1": [1, 512],
    "oh": [32, 404 * 128], "ohvalid": [128, 404], "mw4": [128, 512], "mw0s": [128, 32],
    "xm": [128, 16], "sm": [128, 16], "smrow": [1, 2048], "rv0": [128, 1],
    "fixk": [128, 2], "fixa": [128, 2], "fexp": [128, max(seq // 128, NPAGE) * 128],
    "ind": [128, 4], "iota": [128, 1], "c32": [32, 1], "ident": [128, 128],
}


def build(seq):
    NTP = seq // 128
    NKB = max(NTP, NPAGE)
    WKEEP = min(512, seq)
    tabs = _host_tables(seq)
    nc = bass.Bass("TRN2", target_bir_lowering=False)
    S = Sched(nc)
    es = ExitStack()

    def din(name, shape, dt=F32):
        return nc.dram_tensor(name, list(shape), dt, kind="ExternalInput").ap()

    def dout(name, shape):
        return nc.dram_tensor(name, list(shape), F32, kind="ExternalOutput").ap()

    def dscr(name, shape, dt):
        return nc.dram_tensor(name, list(shape), dt, kind="Internal").ap()

    def sb(name, shape, dt=F32):
        return Tl(es.enter_context(nc.sbuf_tensor(name, list(shape), dt)))

    def ps(name, shape, dt=F32):
        return Tl(es.enter_context(nc.psum_tensor(name, list(shape), dt)))

    xp = din("xp", [seq, D]); xs = din("xs", [128, D])
    st_ret = din("st_ret", [DEPTH, NS * RH * DK, DV])
    st_pool = din("st_pool", [DEPTH, NS * 15, 512])
    c_win = din("c_win", [DEPTH, NS * 512, 256])
    c_cmp = [din("c_cmp%d" % l, [NPHYS * PAGE, 256]) for l in range(DEPTH)]
    c_sel = [din("c_sel%d" % l, [NPHYS * PAGE, 256]) for l in range(DEPTH)]
    ptab = din("ptab", [1, NS * NPAGE], I32)
    w_all = din("w_all", [DEPTH, D, NCA]); w_b = din("w_b", [DEPTH, 3072, D])
    gains_fm = din("gains_fm", [128, 16]); fgain = din("fgain", [1, D])
    w_pool = din("w_pool", [DEPTH, 512, 128]); pscale_fm = din("pscale_fm", [128, 8])
    pe_cmp = din("pe_cmp", [DEPTH, 32, 256])
    w_ck = din("w_ck", [DEPTH, 64, 64]); w_cv = din("w_cv", [DEPTH, 64, 64])
    rel_bias = din("rel_bias", [32, 8])
    tin = {k: din("t_" + k, shp) for k, shp in TABLE_SHAPES(seq).items()}

    y_p = dout("y_p", [seq, D]); y_s = dout("y_s", [128, D])
    ret_p = dout("ret_p", [DEPTH, RH * DK, DV]); ret_s = dout("ret_s", [DEPTH, NS * RH * DK, DV])
    pool_p = dout("pool_p", [DEPTH, 15, 512]); pool_s = dout("pool_s", [DEPTH, NS * 15, 512])
    win_p = dout("win_p", [DEPTH, WKEEP, 256]); win_s = dout("win_s", [DEPTH, NS * 512, 256])
    cmp_p = dout("cmp_p", [DEPTH, seq, 256]); cmp_s = dout("cmp_s", [DEPTH, 128, 256])
    sel_p = dout("sel_p", [DEPTH, seq, 256]); sel_s = dout("sel_s", [DEPTH, 128, 256])

    y1p = dscr("y1p", [seq, D], F32); y1s = dscr("y1s", [128, D], F32)
    wa_bf = dscr("wa_bf", [DEPTH, D, NCA], BF16); wb_bf = dscr("wb_bf", [DEPTH, 3072, D], BF16)

    psA = ps("psA", [128, 512]); psB = ps("psB", [128, 512]); psT = ps("psT", [128, 1024], BF16)
    psS = ps("psS", [128, 512]); psM = ps("psM", [128, 512])
    psO = [ps("psO0", [128, 512]), ps("psO1", [128, 512])]; psX = ps("psX", [128, 512])

    def op(e, fn, r=(), w=()):
        S.op(e, fn, r=[t.r for t in r], w=[t.r for t in w])

    def dma(e, out, in_, r=(), w=()):
        S.dma(e, lambda q: q.dma_start(out=out, in_=in_), r=[t.r for t in r], w=[t.r for t in w])

    def cast_dma(dst, src, ncols, r=(), w=()):
        for c0 in range(0, ncols, 2048):
            n = min(2048, ncols - c0)
            dma("pool", dst[:, c0:c0 + n], src[:, c0:c0 + n], r=r, w=w)

    identf = sb("identf", [128, 128]); identb = sb("identb", [128, 128], BF16)
    gains = sb("gains", [128, 16]); pscl = sb("pscl", [128, 8])
    wpool = sb("wpool", [128, DEPTH * 4, 128], BF16)
    pemean = sb("pemean", [128, 4]); W2 = sb("W2", [128, 4, 128], BF16)
    E01 = sb("E01", [128, 4, 512], BF16)
    E15 = sb("E15", [128, 2, 32], BF16)
    ESD = sb("ESD", [128, 2 * 16, 32], BF16)
    BCT = sb("BCT", [128, 8, 8]); BCS = sb("BCS", [128, 8, 4])
    idx = sb("idx", [128, NS * NPAGE], I32)

    def ctab(name, shape, key, bcast=False, dt=F32):
        t = sb(name, shape, dt)
        src = tin[key]
        n = 1
        for x in shape[1:]:
            n *= x
        src2 = src[0:1, :].to_broadcast([128, n]) if bcast else src[:, :]
        dst = t[:, :] if len(shape) == 2 else t.h[:].rearrange("p a b -> p (a b)")
        if dt == F32:
            dma("sp", dst, src2, w=[t])
        else:
            cast_dma(dst, src2, n, w=[t])
        return t

    DMT = {"p": ctab("dmt_p", [128, 4, 128], "dmt_p"), "s": ctab("dmt_s", [128, 4, 128], "dmt_s")}
    CR = {"p": ctab("cr_p", [128, 4, 128], "cr_p", True), "s": ctab("cr_s", [128, 4, 128], "cr_s", True)}
    TAIL = {"p": ctab("tail_p", [128, 4], "tail_p"), "s": ctab("tail_s", [128, 4], "tail_s")}
    PINV0 = ctab("pinv0", [128, 4, 128], "pinv0", True); PINV1 = ctab("pinv1", [128, 4, 128], "pinv1", True)
    MW4 = ctab("mw4", [128, 512], "mw4", dt=BF16); MW0S = ctab("mw0s", [128, 32], "mw0s", dt=BF16)
    XM = ctab("xm", [128, 16], "xm"); SM = ctab("sm", [128, 16], "sm")
    RV0 = ctab("rv0", [128, 1], "rv0"); FIXK = ctab("fixk", [128, 2], "fixk"); FIXA = ctab("fixa", [128, 2], "fixa")
    FEXP = ctab("fexp", [128, NKB * 128], "fexp", dt=BF16)
    IND = ctab("ind", [128, 4], "ind")

    with ExitStack() as es2:
        stg = [Tl(es2.enter_context(nc.sbuf_tensor("stg%d" % i, [128, NCA], BF16))) for i in range(2)]
        i = 0
        for l in range(DEPTH):
            for rb in range(D // 128):
                t = stg[i % 2]; i += 1
                cast_dma(t, w_all[l, rb * 128:(rb + 1) * 128, :], NCA, w=[t])
                dma("sp", wa_bf[l, rb * 128:(rb + 1) * 128, :], t[:, :], r=[t])
            for rb in range(3):
                t = stg[i % 2]; i += 1
                for a8 in range(8):
                    cast_dma(t[:, a8 * 1024:(a8 + 1) * 1024], w_b[l, rb * 1024 + a8 * 128:rb * 1024 + (a8 + 1) * 128, :], 1024, w=[t])
                dma("sp", wb_bf[l, rb * 1024:(rb + 1) * 1024, :].rearrange("(a p) n -> p a n", p=128),
                    t[:, 0:8192].rearrange("p (a n) -> p a n", a=8), r=[t])
        S.barrier_all()
    WSCR = Tl(None)

    with ExitStack() as es3:
        def sbt(name, shape, dt=F32):
            return Tl(es3.enter_context(nc.sbuf_tensor(name, list(shape), dt)))
        dma("sp", identf[:, :], tin["ident"][:, :], w=[identf])
        op("dve", lambda e: e.tensor_copy(identb[:, :], identf[:, :]), [identf], [identb])
        dma("sp", gains[:, :], gains_fm[:, :], w=[gains])
        dma("sp", pscl[:, :], pscale_fm[:, :], w=[pscl])
        for l in range(DEPTH):
            dma("pool", wpool[:, l * 4:(l + 1) * 4, :], w_pool[l].rearrange("(g c) e -> c g e", c=128), w=[wpool])
        IOTA = sbt("iota", [128, 1]); dma("sp", IOTA[:, :], tin["iota"][:, :], w=[IOTA])
        OHV = sbt("ohv", [128, 404]); dma("sp", OHV[:, :], tin["ohvalid"][:, :], w=[OHV])
        c32 = sbt("c32", [32, 1]); dma("sp", c32[:, :], tin["c32"][:, :], w=[c32])
        pes = sbt("pes", [32, 256])
        for l in range(DEPTH):
            dma("sp", pes[:, :], pe_cmp[l], w=[pes])
            for kv in range(2):
                op("pe", lambda e: e.matmul(psX[:, 0:1], lhsT=pes[:, kv * 128:(kv + 1) * 128], rhs=c32[:, :], start=True, stop=True), [pes, c32], [psX])
                op("dve", lambda e: e.tensor_copy(pemean[:, l * 2 + kv:l * 2 + kv + 1], psX[:, 0:1]), [psX], [pemean])
        w2f = sbt("w2f", [128, 4, 128])
        op("dve", lambda e: e.memset(w2f.h[:], 0.0), [], [w2f])
        for l in range(DEPTH):
            for kv, wsrc in enumerate((w_ck, w_cv)):
                for g in range(2):
                    dma("sp", w2f[g * 64:(g + 1) * 64, l * 2 + kv, g * 64:(g + 1) * 64], wsrc[l], w=[w2f])
        op("dve", lambda e: e.tensor_copy(W2.h[:], w2f.h[:]), [w2f], [W2])
        rbs = sbt("rbs", [32, 8]); dma("sp", rbs[:, :], rel_bias[:, :], w=[rbs])
        rb31 = sbt("rb31", [128, 8]); dma("sp", rb31[:, :], rel_bias[31:32, :].to_broadcast([128, 8]), w=[rb31])
        TB = sbt("TB", [128, 404, 8])
        ohs = sbt("ohs", [32, 32, 128])
        NOH = 404
        for c0 in range(0, NOH, 32):
            n = min(32, NOH - c0)
            dma("sp", ohs.h[:].rearrange("b c k -> b (c k)")[:, 0:n * 128], tin["oh"][:, c0 * 128:(c0 + n) * 128], w=[ohs])
            for j in range(n):
                op("pe", lambda e: e.matmul(psA[:, j * 8:(j + 1) * 8], lhsT=ohs[:, j, :], rhs=rbs[:, :], start=True, stop=True), [ohs, rbs], [psA])
            op("dve", lambda e: e.tensor_copy(TB.h[:].rearrange("p c h -> p (c h)")[:, c0 * 8:(c0 + n) * 8], psA[:, 0:n * 8]), [psA], [TB])
        op("dve", lambda e: e.tensor_tensor(out=TB.h[:], in0=TB.h[:], in1=rb31[:, :].unsqueeze(1).to_broadcast([128, 404, 8]), op=ALU.subtract), [TB, rb31], [TB])
        op("dve", lambda e: e.tensor_tensor(out=TB.h[:], in0=TB.h[:], in1=OHV[:, :].unsqueeze(2).to_broadcast([128, 404, 8]), op=ALU.mult), [TB, OHV], [TB])
        TE = sbt("TE", [128, 392, 8])
        op("act", lambda e: e.activation(out=TE.h[:], in_=TB[:, 0:392, :], func=AF.Exp), [TB], [TE])
        op("dve", lambda e: e.tensor_tensor(out=TE.h[:], in0=TE.h[:], in1=OHV[:, 0:392].unsqueeze(2).to_broadcast([128, 392, 8]), op=ALU.mult), [TE, OHV], [TE])
        for g in range(2):
            for r in range(4):
                h = 4 * g + r
                for dl in range(2):
                    op("dve", lambda e: e.tensor_copy(E01[:, g * 2 + dl, r * 128:(r + 1) * 128], TE[:, dl * 128:(dl + 1) * 128, h]), [TE], [E01])
                op("dve", lambda e: e.tensor_copy(E15[:, g, r * 8:(r + 1) * 8], TE[:, 256:264, h]), [TE], [E15])
                op("dve", lambda e: e.tensor_copy(ESD[:, g * 16:(g + 1) * 16, r * 8:(r + 1) * 8],
                                                  TE[:, 264:392, h].rearrange("p (s t) -> p s t", t=8)), [TE], [ESD])
        negv = sbt("negv", [128, 12])
        op("dve", lambda e: e.tensor_scalar(out=negv[:, :], in0=OHV[:, 392:404], scalar1=-1.0, scalar2=-NEG, op0=ALU.add, op1=ALU.mult), [OHV], [negv])
        op("dve", lambda e: e.tensor_tensor(out=BCT.h[:], in0=TB[:, 392:400, :].rearrange("p c h -> p h c"),
                                            in1=negv[:, 0:8].unsqueeze(1).to_broadcast([128, 8, 8]), op=ALU.add), [TB, negv], [BCT])
        op("dve", lambda e: e.tensor_tensor(out=BCS.h[:], in0=TB[:, 400:404, :].rearrange("p c h -> p h c"),
                                            in1=negv[:, 8:12].unsqueeze(1).to_broadcast([128, 8, 4]), op=ALU.add), [TB, negv], [BCS])
        pti = sbt("pti", [128, NS * NPAGE], I32); ptf = sbt("ptf", [128, NS * NPAGE])
        dma("sp", pti[:, :], ptab[0:1, :].to_broadcast([128, NS * NPAGE]), w=[pti])
        op("dve", lambda e: e.tensor_copy(ptf[:, :], pti[:, :]), [pti], [ptf])
        op("dve", lambda e: e.tensor_scalar(out=ptf[:, :], in0=ptf[:, :], scalar1=128.0, scalar2=IOTA[:, 0:1], op0=ALU.mult, op1=ALU.add), [ptf, IOTA], [ptf])
        op("dve", lambda e: e.tensor_copy(idx[:, :], ptf[:, :]), [ptf], [idx])
        S.barrier_all()

    WB = [sb("wbuf%d" % i, [128, 8, 512], BF16) for i in range(2)]
    wbi = [0]

    def wload(src_rows, c0, w):
        t = WB[wbi[0] % 2]; wbi[0] += 1
        dma("sp", t[:, :, 0:w], src_rows[:, c0:c0 + w].rearrange("(a p) n -> p a n", p=128), r=[WSCR], w=[t])
        return t

    xt = sb("xt", [128, D])
    st1 = sb("st1", [128, 8]); xnT = sb("xnT", [128, 8, 128], BF16)
    ROT = sb("ROT", [128, 4, 128])
    t1 = sb("t1", [128, 128]); t2 = sb("t2", [128, 128])
    qrT = sb("qrT", [128, 4, 128], BF16); qcT = sb("qcT", [128, 4, 128], BF16); krT = sb("krT", [128, 4, 128], BF16)
    ktail = sb("ktail", [128, 512], BF16); vtok = sb("vtok", [128, 1024], BF16); rgs = sb("rgs", [128, 1024])
    pext = sb("pext", [128, 4, 143])
    pw1 = sb("pw1", [128, 16 * 23]); pw2 = sb("pw2", [128, 16 * 23])
    pgsT = sb("pgsT", [128, 4, 128], BF16); nqT = sb("nqT", [128, 4, 128], BF16)
    cms = sb("cms", [128, 2, 4]); cmb = sb("cmb", [128, 2, 4], BF16)
    mgs = sb("mgs", [128, 24, 128], BF16); ngs = sb("ngs", [128, 512]); kvrow = sb("kvrow", [128, 768])
    nbgs = sb("nbgs", [128, 24])
    attT = sb("attT", [128, 128], BF16); ga = sb("ga", [128, 1024], BF16); xn = Vw(ga, ga.h[:]); gaT = sb("gaT", [128, 8, 128], BF16)
    gbT = sb("gbT", [128, 4, 128], BF16); gc = sb("gc", [128, 512], BF16); gcT = sb("gcT", [128, 4, 128], BF16)
    dT = sb("dT", [128, 128], BF16); oc = sb("oc", [128, 512]); mT = sb("mT", [128, 8, 128], BF16)
    putok = Vw(oc, oc.h[:])
    m0 = Vw(t1, t1.h[:]); m1 = Vw(t2, t2.h[:])
    Sst = sb("Sst", [128, 4, 256]); Sbf = sb("Sbf", [128, 4, 256], BF16)
    selKT = sb("selKT", [128, NTP * 128], BF16); selV = sb("selV", [128, NTP, 130], BF16)
    winKT = sb("winKT", [128, 8, 128], BF16); winV = sb("winV", [128, 8, 130], BF16)
    kcT = sb("kcT", [128, 1024], BF16); vcT = sb("vcT", [128, 1024], BF16); vctok = sb("vctok", [128, 8, 128], BF16)
    sc = sb("sc", [128, 1024]); pb = sb("pb", [128, 1024], BF16); pT = sb("pT", [128, 8, 128], BF16)
    junk = Vw(pb, pb.h[:])
    scrB = sb("scrB", [128, 1024]); impf = Vw(scrB, scrB.h[:, 0:512]); imt = Vw(scrB, scrB.h[:, 512:1024]); impg = sb("impg", [128, 2, 128]); imp2 = sb("imp2", [128, 128])
    m8 = sb("m8", [128, 16]); selm = sb("selm", [128, 128], BF16); selT = [sb("selT0", [128, 128], BF16), sb("selT1", [128, 128], BF16)]
    pex = sb("pex", [128, 512], BF16); pmk = sb("pmk", [128, 512], BF16); oTs = sb("oTs", [65, 512]); wgt = sb("wgt", [128, 8])
    raw = [sb("raw0", [128, 8, 256]), sb("raw1", [128, 8, 256])]
    pexs = Vw(raw[1], raw[1].h[:].rearrange("p a b -> p (a b)")[:, 0:4 * 368].rearrange("p (g c) -> p g c", g=4))
    kpg = sb("kpg", [128, 8, 128], BF16); KTs = sb("KTs", [128, 16, 128], BF16); Vs = sb("Vs", [128, 16, 130], BF16)
    ownKs = sb("ownKs", [128, 128], BF16); ownKw = sb("ownKw", [128, 128], BF16)
    ownVs = sb("ownVs", [128, 130], BF16); ownVw = sb("ownVw", [128, 130], BF16)
    cmbs = Vw(KTs, KTs.h[:].rearrange("p a b -> p (a b)").rearrange("p (k n) -> p k n", k=2))
    qcm = Vw(sc, sc.h[:].bitcast(BF16).rearrange("p (s n) -> p s n", s=16)); ktm = Vw(scrB, scrB.h[:].bitcast(BF16).rearrange("p (s n) -> p s n", s=16))
    s0f = [sb("s0f%d" % i, [128, 256]) for i in range(2)]; s0b = [sb("s0b%d" % i, [128, 256], BF16) for i in range(2)]
    pbst = Vw(sc, sc.h[0:120, :].rearrange("p (a c) -> p a c", a=2))
    for t in (selV, winV, Vs):
        op("pool", lambda e: e.memset(t.h[:], 1.0), [], [t])
    for t in (ownVs, ownVw):
        op("pool", lambda e: e.memset(t[:, :], 1.0), [], [t])
    op("pool", lambda e: e.memset(pb[:, :], 0.0), [], [pb])

    def rmsnorm_rstd(src, col):
        op("act", lambda e: e.activation(out=junk[:, :], in_=src[:, :], func=AF.Square, accum_out=st1[:, col:col + 1]), [src], [junk, st1])
        op("act", lambda e: e.activation(out=st1[:, col + 1:col + 2], in_=st1[:, col:col + 1], func=AF.Sqrt, scale=1.0 / D, bias=EPS), [st1], [st1])
        op("dve", lambda e: e.reciprocal(st1[:, col + 2:col + 3], st1[:, col + 1:col + 2]), [st1], [st1])
        return st1[:, col + 2:col + 3]

    def transpose_to(dst_ap, dst_t, src_ap, src_t, eng="act"):
        op("pe", lambda e: e.transpose(psT[:, 0:128], src_ap, identb[:, :]), [src_t, identb], [psT])
        if eng == "act":
            op("act", lambda e: e.copy(dst_ap, psT[:, 0:128]), [psT], [dst_t])
        else:
            op("dve", lambda e: e.tensor_copy(dst_ap, psT[:, 0:128]), [psT], [dst_t])

    def finalize_branch(gate_col0, first):
        for g in range(2):
            op("act", lambda e: e.copy(oTs[0:65, :], psO[g][0:65, :]), [psO[g]], [oTs])
            for r in range(4):
                op("pe", lambda e: e.matmul(psX[:, r * 65:(r + 1) * 65], lhsT=oTs[0:65, r * 128:(r + 1) * 128], rhs=identf[0:65, 0:65], start=True, stop=True), [oTs, identf], [psX])
            den = psX[:, 0:260].rearrange("p (r c) -> p r c", c=65)[:, :, 64]
            op("dve", lambda e: e.tensor_scalar(out=wgt[:, 0:4], in0=den, scalar1=1e-30, scalar2=None, op0=ALU.max), [psX], [wgt])
            op("dve", lambda e: e.reciprocal(wgt[:, 4:8], wgt[:, 0:4]), [wgt], [wgt])
            op("dve", lambda e: e.tensor_tensor(out=wgt[:, 0:4], in0=wgt[:, 4:8], in1=nbgs[:, gate_col0 + 4 * g:gate_col0 + 4 * g + 4], op=ALU.mult), [wgt, nbgs], [wgt])
            for r in range(4):
                h = 4 * g + r
                if first:
                    op("dve", lambda e: e.tensor_scalar(out=oc[:, h * 64:(h + 1) * 64], in0=psX[:, r * 65:r * 65 + 64], scalar1=wgt[:, r:r + 1], scalar2=None, op0=ALU.mult), [psX, wgt], [oc])
                else:
                    op("dve", lambda e: e.scalar_tensor_tensor(out=oc[:, h * 64:(h + 1) * 64], in0=psX[:, r * 65:r * 65 + 64], scalar=wgt[:, r:r + 1], in1=oc[:, h * 64:(h + 1) * 64], op0=ALU.mult, op1=ALU.add), [psX, wgt, oc], [oc])

    def attn_slot(g, kt_ap, kt_t, q_ap, ncol, v_ap, v_t, masks, col0, start, stop, selexp=None):
        op("pe", lambda e: e.matmul(psS[:, 0:ncol].rearrange("p (r q) -> p r q", r=4), lhsT=kt_ap, rhs=q_ap, start=True, stop=True), [kt_t, nqT], [psS])
        op("act", lambda e: e.activation(out=pex[:, 0:ncol], in_=psS[:, 0:ncol], func=AF.Exp), [psS], [pex])
        cur = pex
        if selexp is not None:
            f_ap, st_ap, st_t, nq = selexp
            op("pe", lambda e: e.matmul(psM[:, 0:nq], lhsT=f_ap, rhs=st_ap, start=True, stop=True), [FEXP, st_t], [psM])
            op("dve", lambda e: e.tensor_tensor(out=pmk[:, 0:ncol].rearrange("p (r q) -> p r q", r=4), in0=pex[:, 0:ncol].rearrange("p (r q) -> p r q", r=4),
                                                in1=psM[:, 0:nq].unsqueeze(1).to_broadcast([128, 4, nq]), op=ALU.mult), [pex, psM], [pmk])
            cur = pmk
        for m_ap, m_t in masks:
            op("dve", lambda e: e.tensor_tensor(out=pmk[:, 0:ncol], in0=cur[:, 0:ncol], in1=m_ap, op=ALU.mult), [cur, m_t], [pmk])
            cur = pmk
        op("pe", lambda e: e.matmul(psO[g][0:65, col0:col0 + ncol], lhsT=v_ap, rhs=cur[:, 0:ncol], start=start, stop=stop), [v_t, cur], [psO[g]])

    def unit(l, kind, T):
        prm = (kind == "p")
        last_p = prm and T == NTP - 1
        pos0 = T * 128
        wa = wa_bf[l]; wb = wb_bf[l]
        if l == 0:
            src = xp[pos0:pos0 + 128, :] if prm else xs[:, :]
        else:
            src = y1p[pos0:pos0 + 128, :] if prm else y1s[:, :]
        dma("sp", xt[:, :], src, w=[xt])
        dma("sp", ROT.h[:], (tin["rotp"][:, :, pos0:pos0 + 128] if prm else tin["rots"][:, :, :]).rearrange("t d n -> d t n"), w=[ROT])
        rstd = rmsnorm_rstd(xt, 0)
        op("dve", lambda e: e.tensor_scalar(out=xn[:, :], in0=xt[:, :], scalar1=rstd, scalar2=None, op0=ALU.mult), [xt, st1], [xn])
        for c in range(8):
            op("pe", lambda e: e.transpose(psT[:, c * 128:(c + 1) * 128], xn[:, c * 128:(c + 1) * 128], identb[:, :]), [xn, identb], [psT])
        op("dve", lambda e: e.tensor_tensor(out=xnT.h[:], in0=psT[:, :].rearrange("p (c n) -> p c n", c=8),
                                            in1=gains[:, l * 8:(l + 1) * 8].unsqueeze(2).to_broadcast([128, 8, 128]), op=ALU.mult), [psT, gains], [xnT])
        kk = "p" if prm else "s"
        pp = [psA, psB]; pi = [0]
        for blk in range(NFM // 4):
            wt = wload(wa, blk * 512, 512)
            for j in range(4):
                m = blk * 4 + j
                if (m in (28, 29)) and not prm:
                    continue
                pair = m < 16
                if not (pair and m % 2 == 1):
                    pst = pp[pi[0] % 2]; pi[0] += 1
                off = 128 if (pair and m % 2 == 1) else 0
                for kc in range(8):
                    op("pe", lambda e: e.matmul(pst[:, off:off + 128], lhsT=wt[:, kc, j * 128:(j + 1) * 128], rhs=xnT[:, kc, :], start=(kc == 0), stop=(kc == 7)), [wt, xnT], [pst])
                if pair:
                    if m % 2 == 0:
                        continue
                    h = (m % 8) // 2
                    isq = m < 8
                    ta, tbb = (0, 1) if isq else (2, 3)
                    op("dve", lambda e: e.tensor_tensor(out=t1[:, :], in0=pst[:, 0:128], in1=ROT[:, ta, :], op=ALU.mult), [pst, ROT], [t1])
                    op("dve", lambda e: e.tensor_tensor(out=t2[:, :], in0=pst[:, 128:256], in1=ROT[:, tbb, :], op=ALU.mult), [pst, ROT], [t2])
                    if isq:
                        op("pool", lambda e: e.tensor_tensor(out=t1[:, :], in0=t1[:, :], in1=t2[:, :], op=ALU.add), [t1, t2], [t1])
                        op("act", lambda e: e.copy(qrT[:, h, :], t1[:, :]), [t1], [qrT])
                        op("dve", lambda e: e.tensor_tensor(out=qcT[:, h, :], in0=t1[:, :], in1=CR[kk][:, h, :], op=ALU.mult), [t1, CR[kk]], [qcT])
                    else:
                        op("pool", lambda e: e.tensor_tensor(out=krT[:, h, :], in0=t1[:, :], in1=t2[:, :], op=ALU.add), [t1, t2], [krT])
                elif m < 20:
                    g = m - 16
                    if prm:
                        op("act", lambda e: e.copy(pext[:, g, 15:143], pst[:, 0:128]), [pst], [pext])
                    else:
                        op("act", lambda e: e.copy(pexs[:, g, :].rearrange("p (s c) -> p s c", c=23)[:, :, 15:23], pst[:, 0:128].rearrange("p (s t) -> p s t", t=8)), [pst], [pexs])
                elif m < 24:
                    op("act", lambda e: e.activation(out=pgsT[:, m - 20, :], in_=pst[:, 0:128], func=AF.Silu), [pst], [pgsT])
                elif m < 28:
                    op("act", lambda e: e.activation(out=nqT[:, m - 24, :], in_=pst[:, 0:128], func=AF.Copy, scale=0.125), [pst], [nqT])
                elif m < 30:
                    op("dve", lambda e: e.tensor_reduce(out=cms[:, m - 28, :], in_=pst[:, 0:128].rearrange("p (b t) -> p b t", t=32), axis=AX.X, op=ALU.add), [pst], [cms])
                elif m == 30:
                    if prm:
                        op("act", lambda e: e.copy(selKT[:, pos0:pos0 + 128], pst[:, 0:128]), [pst], [selKT])
                    else:
                        op("act", lambda e: e.copy(ownKs[:, :], pst[:, 0:128]), [pst], [ownKs])
                elif m == 31:
                    if prm:
                        op("act", lambda e: e.copy(winKT[:, T % 8, :], pst[:, 0:128]), [pst], [winKT])
                    else:
                        op("act", lambda e: e.copy(ownKw[:, :], pst[:, 0:128]), [pst], [ownKw])
                else:
                    op("act", lambda e: e.activation(out=mgs[:, m - 32, :], in_=pst[:, 0:128], func=AF.Sigmoid), [pst], [mgs])
        need_put = (not prm) or last_p
        tmb = [(TM_RV, 512, "rv0"), (TM_RV + 512, 512, "rv1"), (TM_RG, 512, "rg0"), (TM_RG + 512, 512, "rg1"),
               (TM_NG, 512, "ng"), (TM_KVR, 512, "kv0"), (TM_KVR + 512, 280, "kv1")]
        if need_put:
            tmb.append((TM_PUT, 512, "put"))
        for c0, w, nm in tmb:
            wt = wload(wa, c0, w)
            pst = pp[pi[0] % 2]; pi[0] += 1
            for kc in range(8):
                op("pe", lambda e: e.matmul(pst[:, 0:w], lhsT=xnT[:, kc, :], rhs=wt[:, kc, 0:w], start=(kc == 0), stop=(kc == 7)), [wt, xnT], [pst])
            if nm[:2] == "rv":
                o = 512 * int(nm[2])
                op("act", lambda e: e.copy(vtok[:, o:o + 512], pst[:, :]), [pst], [vtok])
            elif nm[:2] == "rg":
                o = 512 * int(nm[2])
                op("act", lambda e: e.activation(out=rgs[:, o:o + 512], in_=pst[:, :], func=AF.Silu), [pst], [rgs])
            elif nm == "ng":
                op("act", lambda e: e.activation(out=ngs[:, :], in_=pst[:, :], func=AF.Silu), [pst], [ngs])
            elif nm == "kv0":
                op("dve", lambda e: e.tensor_copy(kvrow[:, 0:512], pst[:, :]), [pst], [kvrow])
            elif nm == "kv1":
                op("dve", lambda e: e.tensor_copy(kvrow[:, 512:768], pst[:, 0:256]), [pst], [kvrow])
                op("act", lambda e: e.activation(out=nbgs[:, :], in_=pst[:, 256:280], func=AF.Sigmoid), [pst], [nbgs])
            else:
                op("act", lambda e: e.copy(putok[:, :], pst[:, :]), [pst], [putok])
        if prm:
            dma("pool", cmp_p[l, pos0:pos0 + 128, :], kvrow[:, 0:256], r=[kvrow])
            dma("pool", sel_p[l, pos0:pos0 + 128, :], kvrow[:, 256:512], r=[kvrow])
            if pos0 + 128 > seq - WKEEP:
                dma("pool", win_p[l, pos0 - (seq - WKEEP):pos0 - (seq - WKEEP) + 128, :], kvrow[:, 512:768], r=[kvrow])
            op("pool", lambda e: e.tensor_copy(selV[:, T, :].rearrange("p (g c) -> p g c", c=65)[:, :, 0:64], kvrow[:, 384:512].rearrange("p (g c) -> p g c", c=64)), [kvrow], [selV])
            op("pool", lambda e: e.tensor_copy(winV[:, T % 8, :].rearrange("p (g c) -> p g c", c=65)[:, :, 0:64], kvrow[:, 640:768].rearrange("p (g c) -> p g c", c=64)), [kvrow], [winV])
        else:
            dma("pool", cmp_s[l, :, :], kvrow[:, 0:256], r=[kvrow])
            dma("pool", sel_s[l, :, :], kvrow[:, 256:512], r=[kvrow])
            for s_ in range(NS):
                dma("pool", win_s[l, s_ * 512 + 504:s_ * 512 + 512, :], kvrow[s_ * 8:(s_ + 1) * 8, 512:768], r=[kvrow])
            dma("pool", win_s[l].rearrange("(s r) c -> s r c", r=512)[:, 0:504, :], c_win[l].rearrange("(s r) c -> s r c", r=512)[:, 8:512, :])
            op("pool", lambda e: e.tensor_copy(ownVs[:, :].rearrange("p (g c) -> p g c", c=65)[:, :, 0:64], kvrow[:, 384:512].rearrange("p (g c) -> p g c", c=64)), [kvrow], [ownVs])
            op("pool", lambda e: e.tensor_copy(ownVw[:, :].rearrange("p (g c) -> p g c", c=65)[:, :, 0:64], kvrow[:, 640:768].rearrange("p (g c) -> p g c", c=64)), [kvrow], [ownVw])
            for s_ in range(NS):
                dma("pool", pool_s[l, s_ * 15 + 7:s_ * 15 + 15, :], putok[s_ * 8:(s_ + 1) * 8, :], r=[putok])
            dma("pool", pool_s[l].rearrange("(s r) c -> s r c", r=15)[:, 0:7, :], st_pool[l].rearrange("(s r) c -> s r c", r=15)[:, 8:15, :])
        if last_p:
            dma("pool", pool_p[l, :, :], putok[113:128, :], r=[putok])

        for h in range(4):
            op("pe", lambda e: e.transpose(psT[:, h * 128:(h + 1) * 128], krT[:, h, :], identb[:, :]), [krT, identb], [psT])
        for h in range(4):
            op("dve", lambda e: e.tensor_scalar(out=ktail[:, h * 128:(h + 1) * 128], in0=psT[:, h * 128:(h + 1) * 128], scalar1=TAIL[kk][:, h:h + 1], scalar2=None, op0=ALU.mult), [psT, TAIL[kk]], [ktail])
        gdec = tabs["gdec_p"] if prm else tabs["gdec_s"]
        for h in range(4):
            op("pe", lambda e: e.matmul(psX[:, 0:128], lhsT=krT[:, h, :], rhs=qrT[:, h, :], start=True, stop=True), [krT, qrT], [psX])
            op("dve", lambda e: e.tensor_tensor(out=attT[:, :], in0=psX[:, 0:128], in1=DMT[kk][:, h, :], op=ALU.mult), [psX, DMT[kk]], [attT])
            if prm:
                op("pe", lambda e: e.matmul(psX[:, 128:384], lhsT=attT[:, :], rhs=vtok[:, h * 256:(h + 1) * 256], start=True, stop=False), [attT, vtok], [psX])
                op("pe", lambda e: e.matmul(psX[:, 128:384], lhsT=qcT[:, h, :], rhs=Sbf[:, h, :], start=False, stop=True), [qcT, Sbf], [psX])
                op("pe", lambda e: e.matmul(psM[:, 0:256], lhsT=ktail[:, h * 128:(h + 1) * 128], rhs=vtok[:, h * 256:(h + 1) * 256], start=True, stop=True), [ktail, vtok], [psM])
                op("dve", lambda e: e.scalar_tensor_tensor(out=Sst[:, h, :], in0=Sst[:, h, :], scalar=gdec[h], in1=psM[:, 0:256], op0=ALU.mult, op1=ALU.add), [Sst, psM], [Sst])
                op("act", lambda e: e.copy(Sbf[:, h, :], Sst[:, h, :]), [Sst], [Sbf])
            else:
                if h == 0:
                    op("pool", lambda e: e.memset(qcm.h[:], 0.0), [], [qcm])
                for s in range(NS):
                    op("pool", lambda e: e.tensor_copy(qcm[:, s, s * 8:(s + 1) * 8], qcT[:, h, s * 8:(s + 1) * 8]), [qcT], [qcm])
                op("pool", lambda e: e.tensor_tensor(out=ktm.h[:], in0=ktail[:, h * 128:(h + 1) * 128].unsqueeze(1).to_broadcast([128, 16, 128]),
                                                     in1=SM[:, :].unsqueeze(2).to_broadcast([128, 16, 128]), op=ALU.mult), [ktail, SM], [ktm])
                op("pe", lambda e: e.matmul(psX[:, 128:384], lhsT=attT[:, :], rhs=vtok[:, h * 256:(h + 1) * 256], start=True, stop=False), [attT, vtok], [psX])
                for s in range(NS):
                    b = (h * NS + s) % 2
                    row0 = (s * RH + h) * DK
                    dma("sp", s0f[b][:, :], st_ret[l, row0:row0 + DK, :], w=[s0f[b]])
                    op("act", lambda e: e.copy(s0b[b][:, :], s0f[b][:, :]), [s0f[b]], [s0b[b]])
                    op("pe", lambda e: e.matmul(psX[:, 128:384], lhsT=qcm[:, s, :], rhs=s0b[b][:, :], start=False, stop=(s == NS - 1)), [qcm, s0b[b]], [psX])
                    op("pe", lambda e: e.matmul(psM[:, 0:256], lhsT=ktm[:, s, :], rhs=vtok[:, h * 256:(h + 1) * 256], start=True, stop=True), [ktm, vtok], [psM])
                    op("dve", lambda e: e.scalar_tensor_tensor(out=s0f[b][:, :], in0=s0f[b][:, :], scalar=gdec[h], in1=psM[:, 0:256], op0=ALU.mult, op1=ALU.add), [s0f[b], psM], [s0f[b]])
                    dma("pool", ret_s[l, row0:row0 + DK, :], s0f[b][:, :], r=[s0f[b]])
            op("act", lambda e: e.activation(out=junk[:, 0:256], in_=psX[:, 128:384], func=AF.Square, accum_out=st1[:, 4:5]), [psX], [junk, st1])
            op("act", lambda e: e.activation(out=st1[:, 5:6], in_=st1[:, 4:5], func=AF.Sqrt, scale=1.0 / DV, bias=EPS), [st1], [st1])
            op("dve", lambda e: e.reciprocal(st1[:, 6:7], st1[:, 5:6]), [st1], [st1])
            op("dve", lambda e: e.scalar_tensor_tensor(out=ga[:, h * 256:(h + 1) * 256], in0=psX[:, 128:384], scalar=st1[:, 6:7], in1=rgs[:, h * 256:(h + 1) * 256], op0=ALU.mult, op1=ALU.mult), [psX, st1, rgs], [ga])
        for c in range(8):
            transpose_to(gaT[:, c, :], gaT, ga[:, c * 128:(c + 1) * 128], ga)
        if last_p:
            dma("pool", ret_p[l].rearrange("(h d) e -> d h e", d=128), Sst.h[:], r=[Sst])

        if not prm:
            dma("sp", pbst.h[:], st_pool[l].rearrange("(a p) c -> p a c", p=120), w=[pbst])
            for a in range(2):
                for g in range(4):
                    op("pe", lambda e: e.matmul(psX[:, 0:120], lhsT=pbst[:, a, g * 128:(g + 1) * 128], rhs=identf[0:120, 0:120], start=True, stop=True), [pbst, identf], [psX])
                    op("act", lambda e: e.copy(pexs[:, g, :].rearrange("p (s c) -> p s c", c=23)[:, a * 8:(a + 1) * 8, 0:15], psX[:, 0:120].rearrange("p (s b) -> p s b", b=15)), [psX], [pexs])
        nseq, L = (1, 143) if prm else (16, 23)
        ntk = L - 15
        for g in range(4):
            w = 2 << g
            base = pext[:, g, :] if prm else pexs[:, g, :]
            bt = pext if prm else pexs
            cur3 = base.rearrange("p (s c) -> p s c", c=L); curt = bt
            bufs = [pw1, pw2]; bi = 0; m = 1
            while m < w:
                nt_ = bufs[bi % 2]; bi += 1
                n3 = nt_[:, 0:nseq * L].rearrange("p (s c) -> p s c", c=L)
                op("pool", lambda e: e.tensor_tensor(out=n3[:, :, m:L], in0=cur3[:, :, m:L], in1=cur3[:, :, 0:L - m], op=ALU.add), [curt], [nt_])
                cur3, curt = n3, nt_
                m *= 2
            pin = (PINV0 if (prm and T == 0) else PINV1)
            pin3 = pin[:, g, :].rearrange("p (s c) -> p s c", c=ntk)
            u3 = base.rearrange("p (s c) -> p s c", c=L)[:, :, 15:L]
            op("dve", lambda e: e.tensor_tensor(out=m0[:, :].rearrange("p (s c) -> p s c", c=ntk), in0=cur3[:, :, 15:L], in1=pin3, op=ALU.mult), [curt, pin], [m0])
            op("dve", lambda e: e.tensor_tensor(out=dT[:, :].rearrange("p (s c) -> p s c", c=ntk), in0=m0[:, :].rearrange("p (s c) -> p s c", c=ntk), in1=u3, op=ALU.subtract), [m0, bt], [dT])
            op("pe", lambda e: e.matmul(psX[:, 0:128], lhsT=wpool[:, l * 4 + g, :], rhs=dT[:, :], start=True, stop=True), [wpool, dT], [psX])
            op("dve", lambda e: e.scalar_tensor_tensor(out=gbT[:, g, :], in0=psX[:, 0:128], scalar=pscl[:, l * 4 + g:l * 4 + g + 1], in1=pgsT[:, g, :], op0=ALU.mult, op1=ALU.mult), [psX, pscl, pgsT], [gbT])
        if prm:
            op("pool", lambda e: e.tensor_copy(pw1[:, 0:60].rearrange("p (g c) -> p g c", c=15), pext[:, :, 128:143]), [pext], [pw1])
            op("pool", lambda e: e.tensor_copy(pext[:, :, 0:15], pw1[:, 0:60].rearrange("p (g c) -> p g c", c=15)), [pw1], [pext])

        if prm:
            for kv in range(2):
                op("dve", lambda e: e.tensor_scalar(out=cmb[:, kv, :], in0=cms[:, kv, :], scalar1=1.0 / 32, scalar2=pemean[:, l * 2 + kv:l * 2 + kv + 1], op0=ALU.mult, op1=ALU.add), [cms, pemean], [cmb])
                op("pe", lambda e: e.matmul(psX[:, kv * 4:kv * 4 + 4], lhsT=W2[:, l * 2 + kv, :], rhs=cmb[:, kv, :], start=True, stop=True), [W2, cmb], [psX])
            op("act", lambda e: e.copy(kcT[:, 4 * T:4 * T + 4], psX[:, 0:4]), [psX], [kcT])
            op("act", lambda e: e.copy(vcT[:, 4 * T:4 * T + 4], psX[:, 4:8]), [psX], [vcT])
            ncb = 4 * T + 4
        else:
            for s in range(NS):
                for hf in range(2):
                    rw = raw[hf]
                    for b8 in range(8):
                        pg = hf * 8 + b8
                        j = s * NPAGE + pg
                        S.dma("pool", lambda q: q.indirect_dma_start(out=rw[:, b8, :], out_offset=None, in_=c_cmp[l][:, :], in_offset=bass.IndirectOffsetOnAxis(ap=idx[:, j:j + 1], axis=0)), r=[idx.r], w=[rw.r])
                    for b8 in range(8):
                        pg = hf * 8 + b8
                        for kv in range(2):
                            op("pe", lambda e: e.matmul(psX[:, kv * 64 + pg * 4:kv * 64 + pg * 4 + 4], lhsT=rw[:, b8, kv * 128:(kv + 1) * 128], rhs=IND[:, :], start=True, stop=True), [rw, IND], [psX])
                for kv in range(2):
                    op("dve", lambda e: e.tensor_scalar(out=cmbs[:, kv, s * 64:(s + 1) * 64], in0=psX[:, kv * 64:(kv + 1) * 64], scalar1=pemean[:, l * 2 + kv:l * 2 + kv + 1], scalar2=None, op0=ALU.add), [psX, pemean], [cmbs])
            for kv, dst in ((0, kcT), (1, vcT)):
                for hf in range(2):
                    op("pe", lambda e: e.matmul(psA[:, :], lhsT=W2[:, l * 2 + kv, :], rhs=cmbs[:, kv, hf * 512:(hf + 1) * 512], start=True, stop=True), [W2, cmbs], [psA])
                    op("act", lambda e: e.copy(dst[:, hf * 512:(hf + 1) * 512], psA[:, :]), [psA], [dst])
            ncb = 1024
        nch = (ncb + 127) // 128
        for ch in range(nch):
            if prm and (ch + 1) * 128 > ncb:
                op("pool", lambda e: e.memset(vcT[:, ncb:(ch + 1) * 128], 0.0), [], [vcT])
            transpose_to(vctok[:, ch, :], vctok, vcT[:, ch * 128:(ch + 1) * 128], vcT)
        J = ncb // 2
        def select_group(g):
            if prm:
                Jn = J
                op("dve", lambda e: e.tensor_copy(impg[:, g, 0:J], impf[:, 0:J]), [impf], [impg])
                op("dve", lambda e: e.tensor_tensor(out=impg[:, g, J - 2:J], in0=impg[:, g, J - 2:J], in1=FIXK[:, :], op=ALU.mult), [impg, FIXK], [impg])
                op("dve", lambda e: e.tensor_tensor(out=impg[:, g, J - 2:J], in0=impg[:, g, J - 2:J], in1=FIXA[:, :], op=ALU.add), [impg, FIXA], [impg])
                op("dve", lambda e: e.memset(impg[:, g, 0:1], 1e4), [], [impg])
            else:
                Jn = 33
                op("dve", lambda e: e.tensor_reduce(out=impg[:, g, 0:32], in_=impf[:, :].rearrange("p (s j) -> p j s", j=32), axis=AX.X, op=ALU.add), [impf], [impg])
                op("dve", lambda e: e.memset(impg[:, g, 32:33], 2e4), [], [impg])
                op("dve", lambda e: e.memset(impg[:, g, 0:1], 1e4), [], [impg])
            op("pool", lambda e: e.memset(selm[:, :], 0.0), [], [selm])
            if Jn <= 16:
                op("dve", lambda e: e.tensor_scalar(out=selm[:, 0:Jn], in0=impg[:, g, 0:Jn], scalar1=-1e29, scalar2=None, op0=ALU.is_gt), [impg], [selm])
            else:
                op("dve", lambda e: e.max(out=m8[:, 0:8], in_=impg[:, g, 0:Jn]), [impg], [m8])
                op("dve", lambda e: e.match_replace(out=imp2[:, 0:Jn], in_to_replace=m8[:, 0:8], in_values=impg[:, g, 0:Jn], imm_value=-1e30), [impg, m8], [imp2])
                op("dve", lambda e: e.max(out=m8[:, 0:8], in_=imp2[:, 0:Jn]), [imp2], [m8])
                op("dve", lambda e: e.tensor_scalar(out=selm[:, 0:Jn], in0=impg[:, g, 0:Jn], scalar1=m8[:, 7:8], scalar2=None, op0=ALU.is_ge), [impg, m8], [selm])
            transpose_to(selT[g][:, :], selT[g], selm[:, :], selm)

        for h in range(8):
            g, r = h // 4, h % 4
            for c0 in range(0, ncb, 512):
                n = min(512, ncb - c0)
                pst = psS if c0 == 0 else psM
                op("pe", lambda e: e.matmul(pst[:, 0:n], lhsT=nqT[64 * g:64 * g + 64, r, :], rhs=kcT[64 * g:64 * g + 64, c0:c0 + n], start=True, stop=True), [nqT, kcT], [pst])
                op("act", lambda e: e.copy(sc[:, c0:c0 + n], pst[:, 0:n]), [pst], [sc])
            if prm:
                if T == 0:
                    op("dve", lambda e: e.tensor_tensor(out=sc[:, 0:4], in0=sc[:, 0:4], in1=BCT[:, h, 4:8], op=ALU.add), [sc, BCT], [sc])
                else:
                    op("dve", lambda e: e.tensor_tensor(out=sc[:, ncb - 8:ncb], in0=sc[:, ncb - 8:ncb], in1=BCT[:, h, :], op=ALU.add), [sc, BCT], [sc])
            else:
                sc3 = sc[:, :].rearrange("p (s c) -> p s c", c=64)
                op("dve", lambda e: e.tensor_tensor(out=sc3, in0=sc3, in1=XM[:, :].unsqueeze(2).to_broadcast([128, 16, 64]), op=ALU.add), [sc, XM], [sc])
                op("dve", lambda e: e.tensor_tensor(out=sc3[:, :, 60:64], in0=sc3[:, :, 60:64], in1=BCS[:, h, :].unsqueeze(1).to_broadcast([128, 16, 4]), op=ALU.add), [sc, BCS], [sc])
            op("dve", lambda e: e.tensor_reduce(out=m8[:, 8:9], in_=sc[:, 0:ncb], axis=AX.X, op=ALU.max), [sc], [m8])
            op("dve", lambda e: e.tensor_scalar(out=m8[:, 9:10], in0=m8[:, 8:9], scalar1=-1.0, scalar2=None, op0=ALU.mult), [m8], [m8])
            op("act", lambda e: e.activation(out=sc[:, 0:ncb], in_=sc[:, 0:ncb], func=AF.Exp, bias=m8[:, 9:10], accum_out=m8[:, 10:11]), [sc, m8], [sc, m8])
            op("dve", lambda e: e.reciprocal(m8[:, 11:12], m8[:, 10:11]), [m8], [m8])
            if prm and T == 0:
                op("dve", lambda e: e.tensor_tensor(out=m8[:, 11:12], in0=m8[:, 11:12], in1=RV0[:, :], op=ALU.mult), [m8, RV0], [m8])
            op("dve", lambda e: e.tensor_scalar(out=sc[:, 0:ncb], in0=sc[:, 0:ncb], scalar1=m8[:, 11:12], scalar2=None, op0=ALU.mult), [sc, m8], [sc])
            if ncb % 128:
                op("pool", lambda e: e.memset(pb[:, ncb:nch * 128], 0.0), [], [pb])
            op("act", lambda e: e.copy(pb[:, 0:ncb], sc[:, 0:ncb]), [sc], [pb])
            ev = sc[:, 0:ncb].rearrange("p (j u) -> p j u", u=2)
            if r == 0:
                op("pool", lambda e: e.tensor_tensor(out=impf[:, 0:J], in0=ev[:, :, 0], in1=ev[:, :, 1], op=ALU.add), [sc], [impf])
            else:
                op("pool", lambda e: e.tensor_tensor(out=imt[:, 0:J], in0=ev[:, :, 0], in1=ev[:, :, 1], op=ALU.add), [sc], [imt])
                op("pool", lambda e: e.tensor_tensor(out=impf[:, 0:J], in0=impf[:, 0:J], in1=imt[:, 0:J], op=ALU.add), [impf, imt], [impf])
            for ch in range(nch):
                op("pe", lambda e: e.transpose(psT[:, ch * 128:(ch + 1) * 128], pb[:, ch * 128:(ch + 1) * 128], identb[:, :]), [pb, identb], [psT])
            op("act", lambda e: e.copy(pT.h[:].rearrange("p c q -> p (c q)")[:, 0:nch * 128], psT[:, 0:nch * 128]), [psT], [pT])
            for ch in range(nch):
                op("pe", lambda e: e.matmul(psX[:, h * 64:(h + 1) * 64], lhsT=pT[:, ch, :], rhs=vctok[:, ch, g * 64:(g + 1) * 64], start=(ch == 0), stop=(ch == nch - 1)), [pT, vctok], [psX])
            if r == 3:
                select_group(g)
        op("dve", lambda e: e.tensor_tensor(out=oc[:, :].rearrange("p (h c) -> p h c", c=64), in0=psX[:, :].rearrange("p (h c) -> p h c", c=64),
                                            in1=nbgs[:, 0:8].unsqueeze(2).to_broadcast([128, 8, 64]), op=ALU.mult), [psX, nbgs], [oc])
        if prm:
            dls = [d for d in range(4, -1, -1) if T - d >= 0]
            for g in range(2):
                qap = nqT[64 * g:64 * g + 64, :, :]
                for i, dl in enumerate(dls):
                    kb = (T - dl) % 8
                    masks = []
                    if dl <= 1:
                        masks = [(E01[:, g * 2 + dl, :], E01)]
                    elif dl == 4:
                        masks = [(MW4[:, :], MW4)]
                    attn_slot(g, winKT[64 * g:64 * g + 64, kb, :], winKT, qap, 512, winV[:, kb, g * 65:(g + 1) * 65], winV, masks, 0, i == 0, i == len(dls) - 1)
            finalize_branch(16, False)
            for g in range(2):
                qap = nqT[64 * g:64 * g + 64, :, :]
                for KB in range(T + 1):
                    dl = T - KB
                    masks = [(E01[:, g * 2 + dl, :], E01)] if dl <= 1 else []
                    attn_slot(g, selKT[64 * g:64 * g + 64, KB * 128:(KB + 1) * 128], selKT, qap, 512, selV[:, KB, g * 65:(g + 1) * 65], selV, masks, 0, KB == 0, KB == T,
                              selexp=(FEXP[:, KB * 128:(KB + 1) * 128], selT[g][:, :], selT[g], 128))
            finalize_branch(8, False)
        else:
            for branch in ("win", "sel"):
                for s in range(NS):
                    nb = 4 if branch == "win" else NPAGE
                    for b0 in range(0, nb, 8):
                        n8 = min(8, nb - b0)
                        rw = raw[(b0 // 8) % 2]
                        if branch == "win":
                            dma("sp", rw[:, 0:4, :], c_win[l, s * 512:(s + 1) * 512, :].rearrange("(b p) c -> p b c", p=128), w=[rw])
                        else:
                            for b in range(n8):
                                j = s * NPAGE + b0 + b
                                S.dma("pool", lambda q: q.indirect_dma_start(out=rw[:, b, :], out_offset=None, in_=c_sel[l][:, :], in_offset=bass.IndirectOffsetOnAxis(ap=idx[:, j:j + 1], axis=0)), r=[idx.r], w=[rw.r])
                        op("dve", lambda e: e.tensor_copy(kpg[:, 0:n8, :], rw[:, 0:n8, 0:128]), [rw], [kpg])
                        op("pool", lambda e: e.tensor_copy(Vs[:, b0:b0 + n8, :].rearrange("p b (g c) -> p b g c", c=65)[:, :, :, 0:64], rw[:, 0:n8, 128:256].rearrange("p b (g c) -> p b g c", c=64)), [rw], [Vs])
                        for b in range(n8):
                            op("pe", lambda e: e.transpose(psT[:, b * 128:(b + 1) * 128], kpg[:, b, :], identb[:, :]), [kpg, identb], [psT])
                        op("act", lambda e: e.copy(KTs.h[:].rearrange("p b k -> p (b k)")[:, b0 * 128:(b0 + n8) * 128], psT[:, 0:n8 * 128]), [psT], [KTs])
                    for g in range(2):
                        qap = nqT[64 * g:64 * g + 64, :, s * 8:(s + 1) * 8]
                        col0 = s * 32
                        for b in range(nb):
                            masks = []
                            sx = None
                            if branch == "win":
                                if b == 0:
                                    masks = [(MW0S[:, :], MW0S)]
                                elif b == 3:
                                    masks = [(E15[:, g, :], E15)]
                            else:
                                if b == 15:
                                    masks = [(E15[:, g, :], E15)]
                                sx = (FEXP[:, b * 128:(b + 1) * 128], selT[g][:, s * 8:(s + 1) * 8], selT[g], 8)
                            attn_slot(g, KTs[64 * g:64 * g + 64, b, :], KTs, qap, 32, Vs[:, b, g * 65:(g + 1) * 65], Vs, masks, col0, b == 0, False, selexp=sx)
                        ok, ov = (ownKw, ownVw) if branch == "win" else (ownKs, ownVs)
                        attn_slot(g, ok[64 * g:64 * g + 64, :], ok, qap, 32, ov[:, g * 65:(g + 1) * 65], ov, [(ESD[:, g * 16 + s, :], ESD)], col0, False, True)
                for g in range(2):
                    op("act", lambda e: e.copy(oTs[0:65, :].rearrange("p (r s t) -> p r s t", r=4, t=8), psO[g][0:65, :].rearrange("p (s r t) -> p r s t", r=4, t=8)), [psO[g]], [oTs])
                    for r in range(4):
                        op("pe", lambda e: e.matmul(psX[:, r * 65:(r + 1) * 65], lhsT=oTs[0:65, r * 128:(r + 1) * 128], rhs=identf[0:65, 0:65], start=True, stop=True), [oTs, identf], [psX])
                    gate0 = 16 if branch == "win" else 8
                    den = psX[:, 0:260].rearrange("p (r c) -> p r c", c=65)[:, :, 64]
                    op("dve", lambda e: e.tensor_scalar(out=wgt[:, 0:4], in0=den, scalar1=1e-30, scalar2=None, op0=ALU.max), [psX], [wgt])
                    op("dve", lambda e: e.reciprocal(wgt[:, 4:8], wgt[:, 0:4]), [wgt], [wgt])
                    op("dve", lambda e: e.tensor_tensor(out=wgt[:, 0:4], in0=wgt[:, 4:8], in1=nbgs[:, gate0 + 4 * g:gate0 + 4 * g + 4], op=ALU.mult), [wgt, nbgs], [wgt])
                    for r in range(4):
                        h = 4 * g + r
                        op("dve", lambda e: e.scalar_tensor_tensor(out=oc[:, h * 64:(h + 1) * 64], in0=psX[:, r * 65:r * 65 + 64], scalar=wgt[:, r:r + 1], in1=oc[:, h * 64:(h + 1) * 64], op0=ALU.mult, op1=ALU.add), [psX, wgt, oc], [oc])
        op("dve", lambda e: e.tensor_tensor(out=gc[:, :], in0=oc[:, :], in1=ngs[:, :], op=ALU.mult), [oc, ngs], [gc])
        for c in range(4):
            transpose_to(gcT[:, c, :], gcT, gc[:, c * 128:(c + 1) * 128], gc)

        for mg_ in range(2):
            wa_t = wload(wb[0:1024, :], mg_ * 512, 512)
            wbc_t = wload(wb[1024:2048, :], mg_ * 512, 512)
            for mm in range(4):
                m = mg_ * 4 + mm
                pst = pp[pi[0] % 2]; pi[0] += 1
                for kc in range(8):
                    op("pe", lambda e: e.matmul(pst[:, 0:128], lhsT=wa_t[:, kc, mm * 128:(mm + 1) * 128], rhs=gaT[:, kc, :], start=(kc == 0), stop=(kc == 7)), [wa_t, gaT], [pst])
                for kc in range(4):
                    op("pe", lambda e: e.matmul(pst[:, 128:256], lhsT=wbc_t[:, kc, mm * 128:(mm + 1) * 128], rhs=gbT[:, kc, :], start=(kc == 0), stop=(kc == 3)), [wbc_t, gbT], [pst])
                for kc in range(4):
                    op("pe", lambda e: e.matmul(pst[:, 256:384], lhsT=wbc_t[:, 4 + kc, mm * 128:(mm + 1) * 128], rhs=gcT[:, kc, :], start=(kc == 0), stop=(kc == 3)), [wbc_t, gcT], [pst])
                op("dve", lambda e: e.tensor_tensor(out=m0[:, :], in0=pst[:, 0:128], in1=mgs[:, m, :], op=ALU.mult), [pst, mgs], [m0])
                op("dve", lambda e: e.tensor_tensor(out=m1[:, :], in0=pst[:, 128:256], in1=mgs[:, 8 + m, :], op=ALU.mult), [pst, mgs], [m1])
                op("pool", lambda e: e.tensor_tensor(out=m0[:, :], in0=m0[:, :], in1=m1[:, :], op=ALU.add), [m0, m1], [m0])
                op("dve", lambda e: e.tensor_tensor(out=m1[:, :], in0=pst[:, 256:384], in1=mgs[:, 16 + m, :], op=ALU.mult), [pst, mgs], [m1])
                op("pool", lambda e: e.tensor_tensor(out=mT[:, m, :], in0=m0[:, :], in1=m1[:, :], op=ALU.add), [m0, m1], [mT])
        for hf in range(2):
            wo_t = wload(wb[2048:3072, :], hf * 512, 512)
            pst = pp[pi[0] % 2]; pi[0] += 1
            for kc in range(8):
                op("pe", lambda e: e.matmul(pst[:, :], lhsT=mT[:, kc, :], rhs=wo_t[:, kc, :], start=(kc == 0), stop=(kc == 7)), [wo_t, mT], [pst])
            op("dve", lambda e: e.tensor_tensor(out=xt[:, hf * 512:(hf + 1) * 512], in0=xt[:, hf * 512:(hf + 1) * 512], in1=pst[:, :], op=ALU.add), [xt, pst], [xt])
        if l == 0:
            dma("pool", y1p[pos0:pos0 + 128, :] if prm else y1s[:, :], xt[:, :], r=[xt], w=[Y1])
        else:
            dma("sp", sc[:, :], fgain[0:1, :].to_broadcast([128, D]), w=[sc])
            rstd = rmsnorm_rstd(xt, 0)
            op("dve", lambda e: e.scalar_tensor_tensor(out=rgs[:, :], in0=xt[:, :], scalar=rstd, in1=sc[:, :], op0=ALU.mult, op1=ALU.mult), [xt, st1, sc], [rgs])
            dma("pool", y_p[pos0:pos0 + 128, :] if prm else y_s[:, :], rgs[:, :], r=[rgs])

    Y1 = Tl(None)
    for l in range(DEPTH):
        op("dve", lambda e: e.memset(Sst.h[:], 0.0), [], [Sst])
        op("pool", lambda e: e.memset(Sbf.h[:], 0.0), [], [Sbf])
        op("pool", lambda e: e.memset(pext.h[:], 0.0), [], [pext])
        if l == 1:
            S.finish()
        for T in range(NTP):
            unit(l, "p", T)
        unit(l, "s", 0)
    S.finish()
    es.close()
    S.close()
    return nc


_CACHE = {}


def _prep_shared(inp, seq):
    f32 = np.float32
    ci = _col_index()
    tabs = _host_tables(seq)
    sh = {}
    sh["w_all"] = np.ascontiguousarray(np.asarray(inp["w_in"], f32)[:, :, ci])
    sh["w_b"] = np.ascontiguousarray(np.concatenate([np.asarray(inp[k], f32) for k in ("w_br_a", "w_br_b", "w_br_c", "w_out")], axis=1))
    g = np.asarray(inp["norm_gain"], f32)
    sh["gains_fm"] = np.ascontiguousarray(g.reshape(DEPTH, 8, 128).transpose(2, 0, 1).reshape(128, 16))
    sh["fgain"] = np.asarray(inp["final_gain"], f32).reshape(1, D)
    sh["w_pool"] = np.asarray(inp["w_pool"], f32).reshape(DEPTH, 512, 128)
    sh["pscale_fm"] = np.ascontiguousarray(np.asarray(inp["pool_scale"], f32).reshape(DEPTH, 4, 128).transpose(2, 0, 1).reshape(128, 8))
    sh["pe_cmp"] = np.asarray(inp["pe_cmp"], f32).reshape(DEPTH, 32, 256)
    sh["w_ck"] = np.asarray(inp["w_ck"], f32); sh["w_cv"] = np.asarray(inp["w_cv"], f32)
    sh["rel_bias"] = np.asarray(inp["rel_bias"], f32)
    for l in range(DEPTH):
        sh["c_cmp%d" % l] = np.asarray(inp["cache_cmp"], f32)[l].reshape(-1, 256)
        sh["c_sel%d" % l] = np.asarray(inp["cache_sel"], f32)[l].reshape(-1, 256)
    for k, shp in TABLE_SHAPES(seq).items():
        sh["t_" + k] = np.ascontiguousarray(np.asarray(tabs[k], f32).reshape(shp))
    return sh


def kernel(**inp):
    f32 = np.float32
    xpr = np.asarray(inp["x_prompt"], f32)
    B, seq, _ = xpr.shape
    n_cores = N_CORES
    nb = np.asarray(inp["x_sample"]).shape[0]
    assert nb == n_cores * NS
    if seq not in _CACHE:
        _CACHE[seq] = build(seq)
    nc = _CACHE[seq]
    sh = _prep_shared(inp, seq)
    xs = np.asarray(inp["x_sample"], f32)
    st_ret = np.asarray(inp["state_ret"], f32); st_pool = np.asarray(inp["state_pool"], f32)
    c_win = np.asarray(inp["cache_win"], f32); pt = np.asarray(inp["page_table"], np.int32)
    in_maps = []
    for c in range(n_cores):
        b = c % B
        sl = slice(c * NS, (c + 1) * NS)
        m = dict(sh)
        m["xp"] = np.ascontiguousarray(xpr[b])
        m["xs"] = np.ascontiguousarray(xs[sl].reshape(NS * TS, D))
        m["st_ret"] = np.ascontiguousarray(st_ret[:, sl].reshape(DEPTH, NS * RH * DK, DV))
        m["st_pool"] = np.ascontiguousarray(st_pool[:, sl].reshape(DEPTH, NS * 15, 512))
        m["c_win"] = np.ascontiguousarray(c_win[:, sl].reshape(DEPTH, NS * 512, 256))
        m["ptab"] = np.ascontiguousarray(pt[sl].reshape(1, NS * NPAGE))
        in_maps.append(m)
    res = run_bass_kernel_spmd(nc, in_maps, core_ids=list(range(n_cores))).results
    wk = min(512, seq)

    def cat_p(name, shape):
        return np.stack([res[b][name].reshape(shape) for b in range(B)], axis=1) if shape[0] == DEPTH else None

    y_prompt = np.stack([res[b]["y_p"] for b in range(B)]).astype(f32)
    y_sample = np.concatenate([res[c]["y_s"].reshape(NS, TS, D) for c in range(n_cores)]).astype(f32)
    ret_prompt = np.stack([res[b]["ret_p"].reshape(DEPTH, RH, DK, DV) for b in range(B)], axis=1)
    ret_sample = np.concatenate([res[c]["ret_s"].reshape(DEPTH, NS, RH, DK, DV) for c in range(n_cores)], axis=1)
    pool_prompt = np.stack([res[b]["pool_p"].reshape(DEPTH, 15, 512) for b in range(B)], axis=1)
    pool_sample = np.concatenate([res[c]["pool_s"].reshape(DEPTH, NS, 15, 512) for c in range(n_cores)], axis=1)
    win_prompt = np.stack([res[b]["win_p"].reshape(DEPTH, wk, 2, 2, 64) for b in range(B)], axis=1)
    win_sample = np.concatenate([res[c]["win_s"].reshape(DEPTH, NS, 512, 2, 2, 64) for c in range(n_cores)], axis=1)
    cmp_prompt = np.stack([res[b]["cmp_p"].reshape(DEPTH, seq, 2, 2, 64) for b in range(B)], axis=1)
    cmp_sample = np.concatenate([res[c]["cmp_s"].reshape(DEPTH, NS, TS, 2, 2, 64) for c in range(n_cores)], axis=1)
    sel_prompt = np.stack([res[b]["sel_p"].reshape(DEPTH, seq, 2, 2, 64) for b in range(B)], axis=1)
    sel_sample = np.concatenate([res[c]["sel_s"].reshape(DEPTH, NS, TS, 2, 2, 64) for c in range(n_cores)], axis=1)
    outs = (y_prompt, y_sample, ret_prompt, ret_sample, pool_prompt, pool_sample, win_prompt, win_sample,
            cmp_prompt, cmp_sample, sel_prompt, sel_sample)
    return tuple(np.ascontiguousarray(o, dtype=f32) for o in outs)
```

```python
import math
from contextlib import ExitStack
import numpy as np
import ml_dtypes
import concourse.bass as bass
import concourse.mybir as mybir
from concourse.bass_utils import run_bass_kernel_spmd

F32 = mybir.dt.float32
BF16 = mybir.dt.bfloat16
I32 = mybir.dt.int32
AF = mybir.ActivationFunctionType
ALU = mybir.AluOpType
AX = mybir.AxisListType

D = 1024
SEQ = 8192
N_CORES = 8
NS = 16
TS = 8
PAST = 2048
NPAGE = 16
PAGE = 128
NPHYS = 2560
DEPTH = 2
EPS = 1e-6
RH, DK, DV = 4, 128, 256
NEG = -30000.0

O_RQ, O_RK, O_RV, O_RG, O_PU, O_PG, O_NQ, O_KV, O_NBG, O_NG, O_MG = (
    0, 512, 1024, 2048, 3072, 3584, 4096, 4608, 5376, 5400, 5912)
NFM = 56
TM_RV, TM_RG, TM_NG, TM_KVR, TM_PUT = 7168, 8192, 9216, 9728, 10752
NCA = 11264


def _swap(idx):
    return idx.reshape(-1, 2)[:, ::-1].reshape(-1)


def _col_index():
    ar = np.arange
    fm = []
    for h in range(4):
        fm += [O_RQ + h * 128 + ar(128), _swap(O_RQ + h * 128 + ar(128))]
    for h in range(4):
        fm += [O_RK + h * 128 + ar(128), _swap(O_RK + h * 128 + ar(128))]
    fm += [O_PU + g * 128 + ar(128) for g in range(4)]
    fm += [O_PG + g * 128 + ar(128) for g in range(4)]
    for c in range(4):
        fm += [np.concatenate([O_NQ + c * 64 + ar(64), O_NQ + (4 + c) * 64 + ar(64)])]
    fm += [O_KV + 0 + ar(128), O_KV + 128 + ar(128), O_KV + 256 + ar(128), O_KV + 512 + ar(128)]
    fm += [O_MG + j * 128 + ar(128) for j in range(24)]
    tm = [O_RV + ar(1024), O_RG + ar(1024), O_NG + ar(512), O_KV + ar(768), O_NBG + ar(24), np.full(232, -1), O_PU + ar(512)]
    idx = np.concatenate(fm + tm)
    assert idx.shape[0] == NCA
    return idx


class Reg:
    __slots__ = ("w", "rs")

    def __init__(self):
        self.w = None
        self.rs = {}


class Sched:
    EPOCH = 30000

    def __init__(self, nc, n_dma_sems=40):
        self.nc = nc
        self.engs = {"pe": nc.tensor, "act": nc.scalar, "dve": nc.vector,
                     "pool": nc.gpsimd, "sp": nc.sync}
        self.sems = []
        self._ctx = []
        self.cur = {}
        self.cnt = {}
        self.last = {}
        for k in self.engs:
            self.cur[k] = self._new_sem("s_" + k)
            self.cnt[k] = 0
            self.last[k] = []
        self.dkeys, self.dval = [], []
        for i in range(n_dma_sems):
            self.dkeys.append(self._new_sem("d%d" % i))
            self.dval.append(0)
        self.dnext = 0
        self.known = {k: {} for k in self.engs}
        self.n_ins = 0
        self.n_wait = 0

    def _new_sem(self, name):
        cm = self.nc.semaphore(name + "_%d" % len(self.sems))
        self.sems.append(cm.__enter__())
        self._ctx.append(cm)
        return len(self.sems) - 1

    def close(self):
        for cm in reversed(self._ctx):
            cm.__exit__(None, None, None)

    def _wait(self, e, dep):
        if dep is None:
            return
        key, val = dep
        if key == self.cur[e] and e in ("pe", "sp"):
            return
        if self.known[e].get(key, 0) >= val:
            return
        self.engs[e].wait_ge(self.sems[key], val)
        self.known[e][key] = val
        self.n_wait += 1

    def _deps(self, e, r, w):
        for x in r:
            self._wait(e, x.w)
        for x in w:
            self._wait(e, x.w)
            for k, v in list(x.rs.items()):
                self._wait(e, (k, v))

    def _mark(self, tok, r, w):
        k, v = tok
        for x in r:
            if x.rs.get(k, 0) < v:
                x.rs[k] = v
        for x in w:
            x.w = tok
            x.rs = {}

    def op(self, e, fn, r=(), w=()):
        if self.cnt[e] >= self.EPOCH:
            self.last[e].append((self.cur[e], self.cnt[e]))
            self.cur[e] = self._new_sem("s_" + e)
            self.cnt[e] = 0
        self._deps(e, r, w)
        ins = fn(self.engs[e])
        self.cnt[e] += 1
        ins.then_inc(self.sems[self.cur[e]], 1)
        self._mark((self.cur[e], self.cnt[e]), r, w)
        self.n_ins += 1

    def dma(self, e, fn, r=(), w=()):
        i = self.dnext
        self.dnext = (self.dnext + 1) % len(self.dkeys)
        k = self.dkeys[i]
        if self.dval[i] > 0:
            self._wait(e, (k, self.dval[i]))
        self._deps(e, r, w)
        ins = fn(self.engs[e])
        self.dval[i] += 16
        ins.then_inc(self.sems[k], 16)
        self._mark((k, self.dval[i]), r, w)
        self.n_ins += 1

    def _all_tokens(self):
        toks = []
        for i, k in enumerate(self.dkeys):
            if self.dval[i] > 0:
                toks.append((k, self.dval[i]))
        for e in ("pe", "act", "dve", "pool"):
            toks += self.last[e]
            if self.cnt[e] > 0:
                toks.append((self.cur[e], self.cnt[e]))
        return toks

    def finish(self):
        for t in self._all_tokens():
            self._wait("sp", t)

    def barrier_all(self):
        toks = self._all_tokens()
        for e in ("sp", "pe", "act", "dve", "pool"):
            for t in toks:
                if t[0] != self.cur[e]:
                    self._wait(e, t)


class Tl:
    def __init__(self, h):
        self.h = h
        self.r = Reg()

    def __getitem__(self, k):
        return self.h[k]


class Sub:
    def __init__(self, ap):
        self.h = ap
        self.r = Reg()

    def __getitem__(self, k):
        return self.h[k]


class Vw:
    def __init__(self, base, ap):
        self.h = ap
        self.r = base.r

    def __getitem__(self, k):
        return self.h[k]


def _t5_bucket(dist):
    n = np.maximum(np.asarray(dist, np.int64), 0)
    nf = np.maximum(n, 1).astype(np.float32)
    lg = np.log(nf / np.float32(16)).astype(np.float32) / np.float32(math.log(128 / 16))
    large = 16 + (lg.astype(np.float32) * np.float32(16)).astype(np.int32)
    large = np.minimum(large, 31)
    return np.where(n < 16, n, large).astype(np.int32)


def _onehot_cols(dist, valid):
    b = _t5_bucket(np.asarray(dist, np.int32))
    oh = np.zeros((32,) + dist.shape, np.float32)
    for k in range(32):
        oh[k] = ((b == k) & valid).astype(np.float32)
    return oh


def _host_tables(seq):
    f32 = np.float32
    tb = {}
    inv = (10000.0 ** (-np.arange(0, DK, 2, dtype=f32) / DK)).astype(f32)

    def rot(pos):
        ang = pos.astype(f32)[None, :] * np.repeat(inv, 2)[:, None]
        c = np.cos(ang).astype(f32)
        s = np.sin(ang).astype(f32)
        sign = np.where(np.arange(DK) % 2 == 0, -1.0, 1.0).astype(f32)[:, None]
        ss = s * sign
        sc = f32(DK ** -0.5)
        return np.stack([c, ss, c * sc, ss * sc]).astype(f32)

    tb["rotp"] = rot(np.arange(seq))
    tb["rots"] = rot(PAST + (np.arange(128) % TS))
    log_g = np.log1p(-np.exp2(-5.0 - np.arange(RH, dtype=f32))).astype(f32)

    def ret_tabs(c, same):
        idx = np.arange(128)
        loc = idx % c
        rel = (loc[None, :] - loc[:, None]).astype(f32)
        dm = np.zeros((128, RH, 128), f32)
        cr = np.zeros((1, RH, 128), f32)
        tl = np.zeros((128, RH), f32)
        for h in range(RH):
            m = np.where((rel >= 0) & same, np.exp(np.maximum(rel, 0) * log_g[h]), 0.0)
            dm[:, h, :] = m
            cr[0, h, :] = np.exp((loc + 1.0) * log_g[h])
            tl[:, h] = np.exp((c - 1.0 - loc) * log_g[h])
        return dm, cr, tl

    allsame = np.ones((128, 128), bool)
    seqid = np.arange(128) // TS
    sames = seqid[:, None] == seqid[None, :]
    tb["dmt_p"], tb["cr_p"], tb["tail_p"] = ret_tabs(128, allsame)
    tb["dmt_s"], tb["cr_s"], tb["tail_s"] = ret_tabs(TS, sames)
    tb["gdec_p"] = [float(np.exp(128 * log_g[h])) for h in range(RH)]
    tb["gdec_s"] = [float(np.exp(TS * log_g[h])) for h in range(RH)]
    pin0 = np.zeros((1, 4, 128), f32)
    pin1 = np.zeros((1, 4, 128), f32)
    for g, w in enumerate((2, 4, 8, 16)):
        pin0[0, g] = 1.0 / np.minimum(np.arange(128) + 1, w)
        pin1[0, g] = 1.0 / w
    tb["pinv0"], tb["pinv1"] = pin0, pin1
    k = np.arange(128)
    cols, valid = [], []
    for dl in (0, 1):
        for q in range(128):
            d = 128 * dl + q - k
            cols.append(d); valid.append(d >= 0)
    for t in range(TS):
        d = 128 + t - k
        cols.append(d); valid.append(d >= 0)
    for c in range(128):
        d = (c % TS) - (k % TS)
        cols.append(d); valid.append((d >= 0) & ((c // TS) == (k // TS)))
    for x in range(8):
        d = k - 32 * (x - 4) - 31
        cols.append(d); valid.append(d >= 0)
    for x in range(4):
        d = PAST + (k % TS) - 32 * (60 + x) - 31
        cols.append(d); valid.append(d >= 0)
    cols = np.stack(cols); valid = np.stack(valid)
    tb["oh"] = _onehot_cols(cols, valid)
    tb["ohvalid"] = valid.T.astype(f32).copy()
    tb["mw4"] = np.tile((k[None, :] < k[:, None]).astype(f32), (1, 4))
    tb["mw0s"] = np.tile((k[:, None] > np.arange(TS)[None, :]).astype(f32), (1, 4))
    xm = np.where(seqid[:, None] == np.arange(NS)[None, :], 0.0, NEG).astype(f32)
    tb["xm"] = xm
    tb["sm"] = (seqid[:, None] == np.arange(NS)[None, :]).astype(f32)
    smrow = np.zeros((1, NS, 128), f32)
    for s in range(NS):
        smrow[0, s, s * TS:(s + 1) * TS] = 1.0
    tb["smrow"] = smrow
    rv0 = (k >= 31).astype(f32)[:, None]
    tb["rv0"] = rv0
    fixk = np.zeros((128, 2), f32); fixa = np.zeros((128, 2), f32)
    fixk[:, 0] = (k >= 64); fixa[:, 0] = np.where(k < 64, 2e4, 0.0)
    fixk[:, 1] = 0.0; fixa[:, 1] = np.where(k < 64, -1e30, 2e4)
    tb["fixk"], tb["fixa"] = fixk, fixa
    nkb = max(seq // 128, NPAGE)
    fexp = np.zeros((128, nkb * 128), f32)
    for m in range(nkb * 128):
        if m // 64 < 128:
            fexp[m // 64, m] = 1.0
    tb["fexp"] = fexp
    ind = np.zeros((128, 4), f32)
    for r in range(128):
        ind[r, r // 32] = 1.0 / 32
    tb["ind"] = ind
    tb["iota"] = np.arange(128, dtype=f32)[:, None].copy()
    tb["c32"] = np.full((32, 1), 1.0 / 32, f32)
    tb["ident"] = np.eye(128, dtype=f32)
    return tb


TABLE_SHAPES = lambda seq: {
    "rotp": [4, 128, seq], "rots": [4, 128, 128],
    "dmt_p": [128, 512], "dmt_s": [128, 512], "cr_p": [1, 512], "cr_s": [1, 512],
    "tail_p": [128, 4], "tail_s": [128, 4], "pinv0": [1, 512], "pinv1": [1, 512],
    "oh": [32, 404 * 128], "ohvalid": [128, 404], "mw4": [128, 512], "mw0s": [128, 32],
    "xm": [128, 16], "sm": [128, 16], "smrow": [1, 2048], "rv0": [128, 1],
    "fixk": [128, 2], "fixa": [128, 2], "fexp": [128, max(seq // 128, NPAGE) * 128],
    "ind": [128, 4], "iota": [128, 1], "c32": [32, 1], "ident": [128, 128],
}


def build(seq):
    NTP = seq // 128
    NKB = max(NTP, NPAGE)
    WKEEP = min(512, seq)
    tabs = _host_tables(seq)
    nc = bass.Bass("TRN2", target_bir_lowering=False)
    S = Sched(nc)
    es = ExitStack()

    def din(name, shape, dt=F32):
        return nc.dram_tensor(name, list(shape), dt, kind="ExternalInput").ap()

    def dout(name, shape):
        return nc.dram_tensor(name, list(shape), F32, kind="ExternalOutput").ap()

    def dscr(name, shape, dt):
        return nc.dram_tensor(name, list(shape), dt, kind="Internal").ap()

    def sb(name, shape, dt=F32):
        return Tl(es.enter_context(nc.sbuf_tensor(name, list(shape), dt)))

    def ps(name, shape, dt=F32):
        return Tl(es.enter_context(nc.psum_tensor(name, list(shape), dt)))

    xp = din("xp", [seq, D]); xs = din("xs", [128, D])
    st_ret = din("st_ret", [DEPTH, NS * RH * DK, DV])
    st_pool = din("st_pool", [DEPTH, NS * 15, 512])
    c_win = din("c_win", [DEPTH, NS * 512, 256])
    c_cmp = [din("c_cmp%d" % l, [NPHYS * PAGE, 256]) for l in range(DEPTH)]
    c_sel = [din("c_sel%d" % l, [NPHYS * PAGE, 256]) for l in range(DEPTH)]
    ptab = din("ptab", [1, NS * NPAGE], I32)
    w_all = din("w_all", [DEPTH, D, NCA]); w_b = din("w_b", [DEPTH, 3072, D])
    gains_fm = din("gains_fm", [128, 16]); fgain = din("fgain", [1, D])
    w_pool = din("w_pool", [DEPTH, 512, 128]); pscale_fm = din("pscale_fm", [128, 8])
    pe_cmp = din("pe_cmp", [DEPTH, 32, 256])
    w_ck = din("w_ck", [DEPTH, 64, 64]); w_cv = din("w_cv", [DEPTH, 64, 64])
    rel_bias = din("rel_bias", [32, 8])
    tin = {k: din("t_" + k, shp) for k, shp in TABLE_SHAPES(seq).items()}

    y_p = dout("y_p", [seq, D]); y_s = dout("y_s", [128, D])
    ret_p = dout("ret_p", [DEPTH, RH * DK, DV]); ret_s = dout("ret_s", [DEPTH, NS * RH * DK, DV])
    pool_p = dout("pool_p", [DEPTH, 15, 512]); pool_s = dout("pool_s", [DEPTH, NS * 15, 512])
    win_p = dout("win_p", [DEPTH, WKEEP, 256]); win_s = dout("win_s", [DEPTH, NS * 512, 256])
    cmp_p = dout("cmp_p", [DEPTH, seq, 256]); cmp_s = dout("cmp_s", [DEPTH, 128, 256])
    sel_p = dout("sel_p", [DEPTH, seq, 256]); sel_s = dout("sel_s", [DEPTH, 128, 256])

    y1p = dscr("y1p", [seq, D], F32); y1s = dscr("y1s", [128, D], F32)
    wa_bf = dscr("wa_bf", [DEPTH, NCA // 256, 128, 8, 256], BF16); wb_bf = dscr("wb_bf", [DEPTH, 3, 4, 128, 8, 256], BF16)

    psA = ps("psA", [128, 512]); psB = ps("psB", [128, 512]); psT = ps("psT", [128, 1024], BF16)
    psS = ps("psS", [128, 512]); psM = ps("psM", [128, 512])
    psO = [ps("psO0", [128, 512]), ps("psO1", [128, 512])]; psX = ps("psX", [128, 512])

    def op(e, fn, r=(), w=()):
        S.op(e, fn, r=[t.r for t in r], w=[t.r for t in w])

    def dma(e, out, in_, r=(), w=()):
        S.dma(e, lambda q: q.dma_start(out=out, in_=in_), r=[t.r for t in r], w=[t.r for t in w])

    def cast_dma(dst, src, ncols, r=(), w=()):
        for c0 in range(0, ncols, 2048):
            n = min(2048, ncols - c0)
            dma("pool", dst[:, c0:c0 + n], src[:, c0:c0 + n], r=r, w=w)

    identf = sb("identf", [128, 128]); identb = sb("identb", [128, 128], BF16)
    gains = sb("gains", [128, 16]); pscl = sb("pscl", [128, 8])
    wpool = sb("wpool", [128, DEPTH * 4, 128], BF16)
    pemean = sb("pemean", [128, 4]); W2 = sb("W2", [128, 4, 128], BF16)
    E01 = sb("E01", [128, 4, 512], BF16)
    E15 = sb("E15", [128, 2, 32], BF16)
    ESD = sb("ESD", [128, 2 * 16, 32], BF16)
    BCT = sb("BCT", [128, 8, 8]); BCS = sb("BCS", [128, 8, 4])
    idx = sb("idx", [128, NS * NPAGE], I32)

    def ctab(name, shape, key, bcast=False, dt=F32):
        t = sb(name, shape, dt)
        src = tin[key]
        n = 1
        for x in shape[1:]:
            n *= x
        src2 = src[0:1, :].to_broadcast([128, n]) if bcast else src[:, :]
        dst = t[:, :] if len(shape) == 2 else t.h[:].rearrange("p a b -> p (a b)")
        if dt == F32:
            dma("sp", dst, src2, w=[t])
        else:
            cast_dma(dst, src2, n, w=[t])
        return t

    DMT = {"p": ctab("dmt_p", [128, 4, 128], "dmt_p"), "s": ctab("dmt_s", [128, 4, 128], "dmt_s")}
    CR = {"p": ctab("cr_p", [128, 4, 128], "cr_p", True), "s": ctab("cr_s", [128, 4, 128], "cr_s", True)}
    TAIL = {"p": ctab("tail_p", [128, 4], "tail_p"), "s": ctab("tail_s", [128, 4], "tail_s")}
    PINV0 = ctab("pinv0", [128, 4, 128], "pinv0", True); PINV1 = ctab("pinv1", [128, 4, 128], "pinv1", True)
    MW4 = ctab("mw4", [128, 512], "mw4", dt=BF16); MW0S = ctab("mw0s", [128, 32], "mw0s", dt=BF16)
    XM = ctab("xm", [128, 16], "xm"); SM = ctab("sm", [128, 16], "sm")
    RV0 = ctab("rv0", [128, 1], "rv0"); FIXK = ctab("fixk", [128, 2], "fixk"); FIXA = ctab("fixa", [128, 2], "fixa")
    FEXP = ctab("fexp", [128, NKB * 128], "fexp", dt=BF16)
    IND = ctab("ind", [128, 4], "ind")

    with ExitStack() as es2:
        stg = [Tl(es2.enter_context(nc.sbuf_tensor("stg%d" % i, [128, NCA], BF16))) for i in range(2)]
        i = 0
        for l in range(DEPTH):
            for rb in range(D // 128):
                t = stg[i % 2]; i += 1
                cast_dma(t, w_all[l, rb * 128:(rb + 1) * 128, :], NCA, w=[t])
                dma("sp", wa_bf[l, :, :, rb, :].rearrange("b p c -> p b c"), t[:, :].rearrange("p (b c) -> p b c", c=256), r=[t])
            for rb in range(3):
                t = stg[i % 2]; i += 1
                for a8 in range(8):
                    cast_dma(t[:, a8 * 1024:(a8 + 1) * 1024], w_b[l, rb * 1024 + a8 * 128:rb * 1024 + (a8 + 1) * 128, :], 1024, w=[t])
                for cb in range(4):
                    dma("sp", wb_bf[l, rb, cb], t[:, 0:8192].rearrange("p (a cb c) -> p cb a c", a=8, cb=4)[:, cb, :, :], r=[t])
        S.barrier_all()
    WSCR = Tl(None)

    with ExitStack() as es3:
        def sbt(name, shape, dt=F32):
            return Tl(es3.enter_context(nc.sbuf_tensor(name, list(shape), dt)))
        dma("sp", identf[:, :], tin["ident"][:, :], w=[identf])
        op("dve", lambda e: e.tensor_copy(identb[:, :], identf[:, :]), [identf], [identb])
        dma("sp", gains[:, :], gains_fm[:, :], w=[gains])
        dma("sp", pscl[:, :], pscale_fm[:, :], w=[pscl])
        for l in range(DEPTH):
            dma("pool", wpool[:, l * 4:(l + 1) * 4, :], w_pool[l].rearrange("(g c) e -> c g e", c=128), w=[wpool])
        IOTA = sbt("iota", [128, 1]); dma("sp", IOTA[:, :], tin["iota"][:, :], w=[IOTA])
        OHV = sbt("ohv", [128, 404]); dma("sp", OHV[:, :], tin["ohvalid"][:, :], w=[OHV])
        c32 = sbt("c32", [32, 1]); dma("sp", c32[:, :], tin["c32"][:, :], w=[c32])
        pes = sbt("pes", [32, 256])
        for l in range(DEPTH):
            dma("sp", pes[:, :], pe_cmp[l], w=[pes])
            for kv in range(2):
                op("pe", lambda e: e.matmul(psX[:, 0:1], lhsT=pes[:, kv * 128:(kv + 1) * 128], rhs=c32[:, :], start=True, stop=True), [pes, c32], [psX])
                op("dve", lambda e: e.tensor_copy(pemean[:, l * 2 + kv:l * 2 + kv + 1], psX[:, 0:1]), [psX], [pemean])
        w2f = sbt("w2f", [128, 4, 128])
        op("dve", lambda e: e.memset(w2f.h[:], 0.0), [], [w2f])
        for l in range(DEPTH):
            for kv, wsrc in enumerate((w_ck, w_cv)):
                for g in range(2):
                    dma("sp", w2f[g * 64:(g + 1) * 64, l * 2 + kv, g * 64:(g + 1) * 64], wsrc[l], w=[w2f])
        op("dve", lambda e: e.tensor_copy(W2.h[:], w2f.h[:]), [w2f], [W2])
        rbs = sbt("rbs", [32, 8]); dma("sp", rbs[:, :], rel_bias[:, :], w=[rbs])
        rb31 = sbt("rb31", [128, 8]); dma("sp", rb31[:, :], rel_bias[31:32, :].to_broadcast([128, 8]), w=[rb31])
        TB = sbt("TB", [128, 404, 8])
        ohs = sbt("ohs", [32, 32, 128])
        NOH = 404
        for c0 in range(0, NOH, 32):
            n = min(32, NOH - c0)
            dma("sp", ohs.h[:].rearrange("b c k -> b (c k)")[:, 0:n * 128], tin["oh"][:, c0 * 128:(c0 + n) * 128], w=[ohs])
            for j in range(n):
                op("pe", lambda e: e.matmul(psA[:, j * 8:(j + 1) * 8], lhsT=ohs[:, j, :], rhs=rbs[:, :], start=True, stop=True), [ohs, rbs], [psA])
            op("dve", lambda e: e.tensor_copy(TB.h[:].rearrange("p c h -> p (c h)")[:, c0 * 8:(c0 + n) * 8], psA[:, 0:n * 8]), [psA], [TB])
        op("dve", lambda e: e.tensor_tensor(out=TB.h[:], in0=TB.h[:], in1=rb31[:, :].unsqueeze(1).to_broadcast([128, 404, 8]), op=ALU.subtract), [TB, rb31], [TB])
        op("dve", lambda e: e.tensor_tensor(out=TB.h[:], in0=TB.h[:], in1=OHV[:, :].unsqueeze(2).to_broadcast([128, 404, 8]), op=ALU.mult), [TB, OHV], [TB])
        TE = sbt("TE", [128, 392, 8])
        op("act", lambda e: e.activation(out=TE.h[:], in_=TB[:, 0:392, :], func=AF.Exp), [TB], [TE])
        op("dve", lambda e: e.tensor_tensor(out=TE.h[:], in0=TE.h[:], in1=OHV[:, 0:392].unsqueeze(2).to_broadcast([128, 392, 8]), op=ALU.mult), [TE, OHV], [TE])
        for g in range(2):
            for r in range(4):
                h = 4 * g + r
                for dl in range(2):
                    op("dve", lambda e: e.tensor_copy(E01[:, g * 2 + dl, r * 128:(r + 1) * 128], TE[:, dl * 128:(dl + 1) * 128, h]), [TE], [E01])
                op("dve", lambda e: e.tensor_copy(E15[:, g, r * 8:(r + 1) * 8], TE[:, 256:264, h]), [TE], [E15])
                op("dve", lambda e: e.tensor_copy(ESD[:, g * 16:(g + 1) * 16, r * 8:(r + 1) * 8],
                                                  TE[:, 264:392, h].rearrange("p (s t) -> p s t", t=8)), [TE], [ESD])
        negv = sbt("negv", [128, 12])
        op("dve", lambda e: e.tensor_scalar(out=negv[:, :], in0=OHV[:, 392:404], scalar1=-1.0, scalar2=-NEG, op0=ALU.add, op1=ALU.mult), [OHV], [negv])
        op("dve", lambda e: e.tensor_tensor(out=BCT.h[:], in0=TB[:, 392:400, :].rearrange("p c h -> p h c"),
                                            in1=negv[:, 0:8].unsqueeze(1).to_broadcast([128, 8, 8]), op=ALU.add), [TB, negv], [BCT])
        op("dve", lambda e: e.tensor_tensor(out=BCS.h[:], in0=TB[:, 400:404, :].rearrange("p c h -> p h c"),
                                            in1=negv[:, 8:12].unsqueeze(1).to_broadcast([128, 8, 4]), op=ALU.add), [TB, negv], [BCS])
        pti = sbt("pti", [128, NS * NPAGE], I32); ptf = sbt("ptf", [128, NS * NPAGE])
        dma("sp", pti[:, :], ptab[0:1, :].to_broadcast([128, NS * NPAGE]), w=[pti])
        op("dve", lambda e: e.tensor_copy(ptf[:, :], pti[:, :]), [pti], [ptf])
        op("dve", lambda e: e.tensor_scalar(out=ptf[:, :], in0=ptf[:, :], scalar1=128.0, scalar2=IOTA[:, 0:1], op0=ALU.mult, op1=ALU.add), [ptf, IOTA], [ptf])
        op("dve", lambda e: e.tensor_copy(idx[:, :], ptf[:, :]), [ptf], [idx])
        S.barrier_all()

    WB = [sb("wbuf%d" % i, [128, 8, 256], BF16) for i in range(4)]
    wbi = [0]

    class WBlk:
        def __init__(self, halves):
            self.hv = halves

        def col(self, kc, a, b):
            hh = a // 256
            assert (b - 1) // 256 == hh
            t = self.hv[hh]
            return t[:, kc, a - hh * 256:b - hh * 256], t

    def wload(src, c0, w):
        hv = []
        for hh in range(2):
            t = WB[wbi[0] % 4]; wbi[0] += 1
            hv.append(t)
            a0, a1 = hh * 256, min(w, (hh + 1) * 256)
            if a1 > a0:
                dma("sp", t[:, :, 0:a1 - a0], src[c0 // 256 + hh, :, :, 0:a1 - a0], r=[WSCR], w=[t])
        return WBlk(hv)

    xt = sb("xt", [128, D])
    st1 = sb("st1", [128, 8]); xnT = sb("xnT", [128, 8, 128], BF16)
    ROT = sb("ROT", [128, 4, 128])
    t1 = sb("t1", [128, 128]); t2 = sb("t2", [128, 128])
    qrT = sb("qrT", [128, 4, 128], BF16); qcT = sb("qcT", [128, 4, 128], BF16); krT = sb("krT", [128, 4, 128], BF16)
    ktail = sb("ktail", [128, 512], BF16); vtok = sb("vtok", [128, 1024], BF16); rgs = sb("rgs", [128, 1024])
    pext = sb("pext", [128, 4, 143])
    pw1 = sb("pw1", [128, 16 * 23]); pw2 = sb("pw2", [128, 16 * 23])
    pgsT = sb("pgsT", [128, 4, 128], BF16); nqT = sb("nqT", [128, 4, 128], BF16)
    cms = sb("cms", [128, 2, 4]); cmb = sb("cmb", [128, 2, 4], BF16)
    mgs = sb("mgs", [128, 24, 128], BF16); ngs = sb("ngs", [128, 512]); kvrow = sb("kvrow", [128, 768])
    nbgs = sb("nbgs", [128, 24])
    attT = sb("attT", [128, 128], BF16); ga = sb("ga", [128, 1024], BF16); xn = Vw(ga, ga.h[:]); gaT = sb("gaT", [128, 8, 128], BF16); junk = Vw(gaT, gaT.h[:].rearrange("p a b -> p (a b)"))
    gbT = sb("gbT", [128, 4, 128], BF16); gc = sb("gc", [128, 512], BF16); gcT = sb("gcT", [128, 4, 128], BF16)
    dT = sb("dT", [128, 128], BF16); oc = sb("oc", [128, 512]); mT = sb("mT", [128, 8, 128], BF16)
    putok = Vw(oc, oc.h[:])
    m0 = Vw(t1, t1.h[:]); m1 = Vw(t2, t2.h[:])
    Sst = sb("Sst", [128, 4, 256]); Sbf = sb("Sbf", [128, 4, 256], BF16)
    selKT = sb("selKT", [128, NTP * 128], BF16); selV = sb("selV", [128, NTP, 130], BF16)
    winKT = sb("winKT", [128, 8, 128], BF16); winV = sb("winV", [128, 8, 130], BF16)
    kcT = sb("kcT", [128, 1024], BF16); vcT = sb("vcT", [128, 1024], BF16); vctok = sb("vctok", [128, 8, 128], BF16)
    sc = sb("sc", [128, 1024]); pb = sb("pb", [128, 1024], BF16); pT = sb("pT", [128, 8, 128], BF16)
    scrB = sb("scrB", [128, 1024]); impf = Vw(scrB, scrB.h[:, 0:512]); imt = Vw(scrB, scrB.h[:, 512:1024]); impg = sb("impg", [128, 2, 128]); imp2 = sb("imp2", [128, 128])
    scH = [Sub(sc.h[:, i * 512:(i + 1) * 512]) for i in range(2)]; pbH = [Sub(pb.h[:, i * 512:(i + 1) * 512]) for i in range(2)]
    pTH = [Sub(pT.h[:, i * 4:(i + 1) * 4, :]) for i in range(2)]; m8H = [sb("m8h%d" % i, [128, 4]) for i in range(2)]
    m8 = sb("m8", [128, 16]); selm = sb("selm", [128, 128], BF16); selT = [sb("selT0", [128, 128], BF16), sb("selT1", [128, 128], BF16)]
    pexs_ = [sb("pex%d" % i, [128, 512], BF16) for i in range(2)]; pmks_ = [sb("pmk%d" % i, [128, 512], BF16) for i in range(2)]; slot_i = [0]; oTs = sb("oTs", [65, 512]); wgt = sb("wgt", [128, 8])
    raw = [sb("raw0", [128, 8, 256]), sb("raw1", [128, 8, 256])]
    pexs = Vw(raw[1], raw[1].h[:].rearrange("p a b -> p (a b)")[:, 0:4 * 368].rearrange("p (g c) -> p g c", g=4))
    kpg = sb("kpg", [128, 8, 128], BF16); KTs = sb("KTs", [128, 16, 128], BF16); Vs = sb("Vs", [128, 16, 130], BF16)
    ownKs = sb("ownKs", [128, 128], BF16); ownKw = sb("ownKw", [128, 128], BF16)
    ownVs = sb("ownVs", [128, 130], BF16); ownVw = sb("ownVw", [128, 130], BF16)
    cmbs = Vw(KTs, KTs.h[:].rearrange("p a b -> p (a b)").rearrange("p (k n) -> p k n", k=2))
    qcm = Vw(sc, sc.h[:].bitcast(BF16).rearrange("p (s n) -> p s n", s=16)); ktm = Vw(scrB, scrB.h[:].bitcast(BF16).rearrange("p (s n) -> p s n", s=16))
    s0f = [sb("s0f%d" % i, [128, 256]) for i in range(2)]; s0b = [sb("s0b%d" % i, [128, 256], BF16) for i in range(2)]
    pbst = Vw(sc, sc.h[0:120, :].rearrange("p (a c) -> p a c", a=2))
    for t in (selV, winV, Vs):
        op("pool", lambda e: e.memset(t.h[:], 1.0), [], [t])
    for t in (ownVs, ownVw):
        op("pool", lambda e: e.memset(t[:, :], 1.0), [], [t])
    op("pool", lambda e: e.memset(pb[:, :], 0.0), [], [pb])

    def rmsnorm_rstd(src, col):
        op("act", lambda e: e.activation(out=junk[:, :], in_=src[:, :], func=AF.Square, accum_out=st1[:, col:col + 1]), [src], [junk, st1])
        op("act", lambda e: e.activation(out=st1[:, col + 1:col + 2], in_=st1[:, col:col + 1], func=AF.Sqrt, scale=1.0 / D, bias=EPS), [st1], [st1])
        op("dve", lambda e: e.reciprocal(st1[:, col + 2:col + 3], st1[:, col + 1:col + 2]), [st1], [st1])
        return st1[:, col + 2:col + 3]

    def transpose_to(dst_ap, dst_t, src_ap, src_t, eng="act"):
        op("pe", lambda e: e.transpose(psT[:, 0:128], src_ap, identb[:, :]), [src_t, identb], [psT])
        if eng == "act":
            op("act", lambda e: e.copy(dst_ap, psT[:, 0:128]), [psT], [dst_t])
        else:
            op("dve", lambda e: e.tensor_copy(dst_ap, psT[:, 0:128]), [psT], [dst_t])

    def finalize_branch(gate_col0, first):
        flush_slots()
        for g in range(2):
            op("act", lambda e: e.copy(oTs[0:65, :], psO[g][0:65, :]), [psO[g]], [oTs])
            for r in range(4):
                op("pe", lambda e: e.matmul(psX[:, r * 65:(r + 1) * 65], lhsT=oTs[0:65, r * 128:(r + 1) * 128], rhs=identf[0:65, 0:65], start=True, stop=True), [oTs, identf], [psX])
            den = psX[:, 0:260].rearrange("p (r c) -> p r c", c=65)[:, :, 64]
            op("dve", lambda e: e.tensor_scalar(out=wgt[:, 0:4], in0=den, scalar1=1e-30, scalar2=None, op0=ALU.max), [psX], [wgt])
            op("dve", lambda e: e.reciprocal(wgt[:, 4:8], wgt[:, 0:4]), [wgt], [wgt])
            op("dve", lambda e: e.tensor_tensor(out=wgt[:, 0:4], in0=wgt[:, 4:8], in1=nbgs[:, gate_col0 + 4 * g:gate_col0 + 4 * g + 4], op=ALU.mult), [wgt, nbgs], [wgt])
            for r in range(4):
                h = 4 * g + r
                if first:
                    op("dve", lambda e: e.tensor_scalar(out=oc[:, h * 64:(h + 1) * 64], in0=psX[:, r * 65:r * 65 + 64], scalar1=wgt[:, r:r + 1], scalar2=None, op0=ALU.mult), [psX, wgt], [oc])
                else:
                    op("dve", lambda e: e.scalar_tensor_tensor(out=oc[:, h * 64:(h + 1) * 64], in0=psX[:, r * 65:r * 65 + 64], scalar=wgt[:, r:r + 1], in1=oc[:, h * 64:(h + 1) * 64], op0=ALU.mult, op1=ALU.add), [psX, wgt, oc], [oc])

    def attn_slot(g, kt_ap, kt_t, q_ap, ncol, v_ap, v_t, masks, col0, start, stop, selexp=None):
        i = slot_i[0]; slot_i[0] += 1
        pss, psm = (psS, psM) if i % 2 == 0 else (psA, psB)
        pex, pmk = pexs_[i % 2], pmks_[i % 2]
        op("pe", lambda e: e.matmul(pss[:, 0:ncol].rearrange("p (r q) -> p r q", r=4), lhsT=kt_ap, rhs=q_ap, start=True, stop=True), [kt_t, nqT], [pss])
        if selexp is not None:
            f_ap, st_ap, st_t, nq = selexp
            op("pe", lambda e: e.matmul(psm[:, 0:nq], lhsT=f_ap, rhs=st_ap, start=True, stop=True), [FEXP, st_t], [psm])
        op("act", lambda e: e.activation(out=pex[:, 0:ncol], in_=pss[:, 0:ncol], func=AF.Exp), [pss], [pex])
        cur = pex
        if selexp is not None:
            op("dve", lambda e: e.tensor_tensor(out=pmk[:, 0:ncol].rearrange("p (r q) -> p r q", r=4), in0=pex[:, 0:ncol].rearrange("p (r q) -> p r q", r=4),
                                                in1=psm[:, 0:nq].unsqueeze(1).to_broadcast([128, 4, nq]), op=ALU.mult), [pex, psm], [pmk])
            cur = pmk
        for m_ap, m_t in masks:
            op("dve", lambda e: e.tensor_tensor(out=pmk[:, 0:ncol], in0=cur[:, 0:ncol], in1=m_ap, op=ALU.mult), [cur, m_t], [pmk])
            cur = pmk
        flush_slots()
        pend.append((g, col0, ncol, v_ap, v_t, cur, start, stop))

    pend = []

    def flush_slots():
        while pend:
            g, col0, ncol, v_ap, v_t, cur, start, stop = pend.pop(0)
            op("pe", lambda e: e.matmul(psO[g][0:65, col0:col0 + ncol], lhsT=v_ap, rhs=cur[:, 0:ncol], start=start, stop=stop), [v_t, cur], [psO[g]])

    def unit(l, kind, T):
        prm = (kind == "p")
        last_p = prm and T == NTP - 1
        pos0 = T * 128
        wa = wa_bf[l]; wb = wb_bf[l]
        if l == 0:
            src = xp[pos0:pos0 + 128, :] if prm else xs[:, :]
        else:
            src = y1p[pos0:pos0 + 128, :] if prm else y1s[:, :]
        dma("sp", xt[:, :], src, w=[xt])
        dma("sp", ROT.h[:], (tin["rotp"][:, :, pos0:pos0 + 128] if prm else tin["rots"][:, :, :]).rearrange("t d n -> d t n"), w=[ROT])
        rstd = rmsnorm_rstd(xt, 0)
        op("dve", lambda e: e.tensor_scalar(out=xn[:, :], in0=xt[:, :], scalar1=rstd, scalar2=None, op0=ALU.mult), [xt, st1], [xn])
        for c in range(8):
            op("pe", lambda e: e.transpose(psT[:, c * 128:(c + 1) * 128], xn[:, c * 128:(c + 1) * 128], identb[:, :]), [xn, identb], [psT])
        op("dve", lambda e: e.tensor_tensor(out=xnT.h[:], in0=psT[:, :].rearrange("p (c n) -> p c n", c=8),
                                            in1=gains[:, l * 8:(l + 1) * 8].unsqueeze(2).to_broadcast([128, 8, 128]), op=ALU.mult), [psT, gains], [xnT])
        kk = "p" if prm else "s"
        pp = [psA, psB]; pi = [0]
        for blk in range(NFM // 4):
            wt = wload(wa, blk * 512, 512)
            for j in range(4):
                m = blk * 4 + j
                if (m in (28, 29)) and not prm:
                    continue
                pair = m < 16
                if not (pair and m % 2 == 1):
                    pst = pp[pi[0] % 2]; pi[0] += 1
                off = 128 if (pair and m % 2 == 1) else 0
                for kc in range(8):
                    wap, wtl = wt.col(kc, j * 128, (j + 1) * 128)
                    op("pe", lambda e: e.matmul(pst[:, off:off + 128], lhsT=wap, rhs=xnT[:, kc, :], start=(kc == 0), stop=(kc == 7)), [wtl, xnT], [pst])
                if pair:
                    if m % 2 == 0:
                        continue
                    h = (m % 8) // 2
                    isq = m < 8
                    ta, tbb = (0, 1) if isq else (2, 3)
                    op("dve", lambda e: e.tensor_tensor(out=t1[:, :], in0=pst[:, 0:128], in1=ROT[:, ta, :], op=ALU.mult), [pst, ROT], [t1])
                    op("dve", lambda e: e.tensor_tensor(out=t2[:, :], in0=pst[:, 128:256], in1=ROT[:, tbb, :], op=ALU.mult), [pst, ROT], [t2])
                    if isq:
                        op("pool", lambda e: e.tensor_tensor(out=t1[:, :], in0=t1[:, :], in1=t2[:, :], op=ALU.add), [t1, t2], [t1])
                        op("act", lambda e: e.copy(qrT[:, h, :], t1[:, :]), [t1], [qrT])
                        op("dve", lambda e: e.tensor_tensor(out=qcT[:, h, :], in0=t1[:, :], in1=CR[kk][:, h, :], op=ALU.mult), [t1, CR[kk]], [qcT])
                    else:
                        op("pool", lambda e: e.tensor_tensor(out=krT[:, h, :], in0=t1[:, :], in1=t2[:, :], op=ALU.add), [t1, t2], [krT])
                elif m < 20:
                    g = m - 16
                    if prm:
                        op("act", lambda e: e.copy(pext[:, g, 15:143], pst[:, 0:128]), [pst], [pext])
                    else:
                        op("act", lambda e: e.copy(pexs[:, g, :].rearrange("p (s c) -> p s c", c=23)[:, :, 15:23], pst[:, 0:128].rearrange("p (s t) -> p s t", t=8)), [pst], [pexs])
                elif m < 24:
                    op("act", lambda e: e.activation(out=pgsT[:, m - 20, :], in_=pst[:, 0:128], func=AF.Silu), [pst], [pgsT])
                elif m < 28:
                    op("act", lambda e: e.activation(out=nqT[:, m - 24, :], in_=pst[:, 0:128], func=AF.Copy, scale=0.125), [pst], [nqT])
                elif m < 30:
                    op("dve", lambda e: e.tensor_reduce(out=cms[:, m - 28, :], in_=pst[:, 0:128].rearrange("p (b t) -> p b t", t=32), axis=AX.X, op=ALU.add), [pst], [cms])
                elif m == 30:
                    if prm:
                        op("act", lambda e: e.copy(selKT[:, pos0:pos0 + 128], pst[:, 0:128]), [pst], [selKT])
                    else:
                        op("act", lambda e: e.copy(ownKs[:, :], pst[:, 0:128]), [pst], [ownKs])
                elif m == 31:
                    if prm:
                        op("act", lambda e: e.copy(winKT[:, T % 8, :], pst[:, 0:128]), [pst], [winKT])
                    else:
                        op("act", lambda e: e.copy(ownKw[:, :], pst[:, 0:128]), [pst], [ownKw])
                else:
                    op("act", lambda e: e.activation(out=mgs[:, m - 32, :], in_=pst[:, 0:128], func=AF.Sigmoid), [pst], [mgs])
        need_put = (not prm) or last_p
        tmb = [(TM_RV, 512, "rv0"), (TM_RV + 512, 512, "rv1"), (TM_RG, 512, "rg0"), (TM_RG + 512, 512, "rg1"),
               (TM_NG, 512, "ng"), (TM_KVR, 512, "kv0"), (TM_KVR + 512, 280, "kv1")]
        if need_put:
            tmb.append((TM_PUT, 512, "put"))
        for c0, w, nm in tmb:
            wt = wload(wa, c0, w)
            pst = pp[pi[0] % 2]; pi[0] += 1
            for a0 in range(0, w, 256):
                a1 = min(w, a0 + 256)
                for kc in range(8):
                    wap, wtl = wt.col(kc, a0, a1)
                    op("pe", lambda e: e.matmul(pst[:, a0:a1], lhsT=xnT[:, kc, :], rhs=wap, start=(kc == 0), stop=(kc == 7)), [wtl, xnT], [pst])
            if nm[:2] == "rv":
                o = 512 * int(nm[2])
                op("act", lambda e: e.copy(vtok[:, o:o + 512], pst[:, :]), [pst], [vtok])
            elif nm[:2] == "rg":
                o = 512 * int(nm[2])
                op("act", lambda e: e.activation(out=rgs[:, o:o + 512], in_=pst[:, :], func=AF.Silu), [pst], [rgs])
            elif nm == "ng":
                op("act", lambda e: e.activation(out=ngs[:, :], in_=pst[:, :], func=AF.Silu), [pst], [ngs])
            elif nm == "kv0":
                op("dve", lambda e: e.tensor_copy(kvrow[:, 0:512], pst[:, :]), [pst], [kvrow])
            elif nm == "kv1":
                op("dve", lambda e: e.tensor_copy(kvrow[:, 512:768], pst[:, 0:256]), [pst], [kvrow])
                op("act", lambda e: e.activation(out=nbgs[:, :], in_=pst[:, 256:280], func=AF.Sigmoid), [pst], [nbgs])
            else:
                op("act", lambda e: e.copy(putok[:, :], pst[:, :]), [pst], [putok])
        if prm:
            dma("pool", cmp_p[l, pos0:pos0 + 128, :], kvrow[:, 0:256], r=[kvrow])
            dma("pool", sel_p[l, pos0:pos0 + 128, :], kvrow[:, 256:512], r=[kvrow])
            if pos0 + 128 > seq - WKEEP:
                dma("pool", win_p[l, pos0 - (seq - WKEEP):pos0 - (seq - WKEEP) + 128, :], kvrow[:, 512:768], r=[kvrow])
            op("pool", lambda e: e.tensor_copy(selV[:, T, :].rearrange("p (g c) -> p g c", c=65)[:, :, 0:64], kvrow[:, 384:512].rearrange("p (g c) -> p g c", c=64)), [kvrow], [selV])
            op("pool", lambda e: e.tensor_copy(winV[:, T % 8, :].rearrange("p (g c) -> p g c", c=65)[:, :, 0:64], kvrow[:, 640:768].rearrange("p (g c) -> p g c", c=64)), [kvrow], [winV])
        else:
            dma("pool", cmp_s[l, :, :], kvrow[:, 0:256], r=[kvrow])
            dma("pool", sel_s[l, :, :], kvrow[:, 256:512], r=[kvrow])
            for s_ in range(NS):
                dma("pool", win_s[l, s_ * 512 + 504:s_ * 512 + 512, :], kvrow[s_ * 8:(s_ + 1) * 8, 512:768], r=[kvrow])
            dma("pool", win_s[l].rearrange("(s r) c -> s r c", r=512)[:, 0:504, :], c_win[l].rearrange("(s r) c -> s r c", r=512)[:, 8:512, :])
            op("pool", lambda e: e.tensor_copy(ownVs[:, :].rearrange("p (g c) -> p g c", c=65)[:, :, 0:64], kvrow[:, 384:512].rearrange("p (g c) -> p g c", c=64)), [kvrow], [ownVs])
            op("pool", lambda e: e.tensor_copy(ownVw[:, :].rearrange("p (g c) -> p g c", c=65)[:, :, 0:64], kvrow[:, 640:768].rearrange("p (g c) -> p g c", c=64)), [kvrow], [ownVw])
            for s_ in range(NS):
                dma("pool", pool_s[l, s_ * 15 + 7:s_ * 15 + 15, :], putok[s_ * 8:(s_ + 1) * 8, :], r=[putok])
            dma("pool", pool_s[l].rearrange("(s r) c -> s r c", r=15)[:, 0:7, :], st_pool[l].rearrange("(s r) c -> s r c", r=15)[:, 8:15, :])
        if last_p:
            dma("pool", pool_p[l, :, :], putok[113:128, :], r=[putok])

        for h in range(4):
            op("pe", lambda e: e.transpose(psT[:, h * 128:(h + 1) * 128], krT[:, h, :], identb[:, :]), [krT, identb], [psT])
        for h in range(4):
            op("dve", lambda e: e.tensor_scalar(out=ktail[:, h * 128:(h + 1) * 128], in0=psT[:, h * 128:(h + 1) * 128], scalar1=TAIL[kk][:, h:h + 1], scalar2=None, op0=ALU.mult), [psT, TAIL[kk]], [ktail])
        gdec = tabs["gdec_p"] if prm else tabs["gdec_s"]
        for h in range(4):
            op("pe", lambda e: e.matmul(psX[:, 0:128], lhsT=krT[:, h, :], rhs=qrT[:, h, :], start=True, stop=True), [krT, qrT], [psX])
            op("dve", lambda e: e.tensor_tensor(out=attT[:, :], in0=psX[:, 0:128], in1=DMT[kk][:, h, :], op=ALU.mult), [psX, DMT[kk]], [attT])
            if prm:
                op("pe", lambda e: e.matmul(psX[:, 128:384], lhsT=attT[:, :], rhs=vtok[:, h * 256:(h + 1) * 256], start=True, stop=False), [attT, vtok], [psX])
                op("pe", lambda e: e.matmul(psX[:, 128:384], lhsT=qcT[:, h, :], rhs=Sbf[:, h, :], start=False, stop=True), [qcT, Sbf], [psX])
                op("pe", lambda e: e.matmul(psM[:, 0:256], lhsT=ktail[:, h * 128:(h + 1) * 128], rhs=vtok[:, h * 256:(h + 1) * 256], start=True, stop=True), [ktail, vtok], [psM])
                op("dve", lambda e: e.scalar_tensor_tensor(out=Sst[:, h, :], in0=Sst[:, h, :], scalar=gdec[h], in1=psM[:, 0:256], op0=ALU.mult, op1=ALU.add), [Sst, psM], [Sst])
                op("act", lambda e: e.copy(Sbf[:, h, :], Sst[:, h, :]), [Sst], [Sbf])
            else:
                if h == 0:
                    op("pool", lambda e: e.memset(qcm.h[:], 0.0), [], [qcm])
                for s in range(NS):
                    op("pool", lambda e: e.tensor_copy(qcm[:, s, s * 8:(s + 1) * 8], qcT[:, h, s * 8:(s + 1) * 8]), [qcT], [qcm])
                op("pool", lambda e: e.tensor_tensor(out=ktm.h[:], in0=ktail[:, h * 128:(h + 1) * 128].unsqueeze(1).to_broadcast([128, 16, 128]),
                                                     in1=SM[:, :].unsqueeze(2).to_broadcast([128, 16, 128]), op=ALU.mult), [ktail, SM], [ktm])
                op("pe", lambda e: e.matmul(psX[:, 128:384], lhsT=attT[:, :], rhs=vtok[:, h * 256:(h + 1) * 256], start=True, stop=False), [attT, vtok], [psX])
                for s in range(NS):
                    b = (h * NS + s) % 2
                    row0 = (s * RH + h) * DK
                    dma("sp", s0f[b][:, :], st_ret[l, row0:row0 + DK, :], w=[s0f[b]])
                    op("act", lambda e: e.copy(s0b[b][:, :], s0f[b][:, :]), [s0f[b]], [s0b[b]])
                    op("pe", lambda e: e.matmul(psX[:, 128:384], lhsT=qcm[:, s, :], rhs=s0b[b][:, :], start=False, stop=(s == NS - 1)), [qcm, s0b[b]], [psX])
                    op("pe", lambda e: e.matmul(psM[:, 0:256], lhsT=ktm[:, s, :], rhs=vtok[:, h * 256:(h + 1) * 256], start=True, stop=True), [ktm, vtok], [psM])
                    op("dve", lambda e: e.scalar_tensor_tensor(out=s0f[b][:, :], in0=s0f[b][:, :], scalar=gdec[h], in1=psM[:, 0:256], op0=ALU.mult, op1=ALU.add), [s0f[b], psM], [s0f[b]])
                    dma("pool", ret_s[l, row0:row0 + DK, :], s0f[b][:, :], r=[s0f[b]])
            op("act", lambda e: e.activation(out=junk[:, 0:256], in_=psX[:, 128:384], func=AF.Square, accum_out=st1[:, 4:5]), [psX], [junk, st1])
            op("act", lambda e: e.activation(out=st1[:, 5:6], in_=st1[:, 4:5], func=AF.Sqrt, scale=1.0 / DV, bias=EPS), [st1], [st1])
            op("dve", lambda e: e.reciprocal(st1[:, 6:7], st1[:, 5:6]), [st1], [st1])
            op("dve", lambda e: e.scalar_tensor_tensor(out=ga[:, h * 256:(h + 1) * 256], in0=psX[:, 128:384], scalar=st1[:, 6:7], in1=rgs[:, h * 256:(h + 1) * 256], op0=ALU.mult, op1=ALU.mult), [psX, st1, rgs], [ga])
        for c in range(8):
            transpose_to(gaT[:, c, :], gaT, ga[:, c * 128:(c + 1) * 128], ga)
        if last_p:
            dma("pool", ret_p[l].rearrange("(h d) e -> d h e", d=128), Sst.h[:], r=[Sst])

        if not prm:
            dma("sp", pbst.h[:], st_pool[l].rearrange("(a p) c -> p a c", p=120), w=[pbst])
            for a in range(2):
                for g in range(4):
                    op("pe", lambda e: e.matmul(psX[:, 0:120], lhsT=pbst[:, a, g * 128:(g + 1) * 128], rhs=identf[0:120, 0:120], start=True, stop=True), [pbst, identf], [psX])
                    op("act", lambda e: e.copy(pexs[:, g, :].rearrange("p (s c) -> p s c", c=23)[:, a * 8:(a + 1) * 8, 0:15], psX[:, 0:120].rearrange("p (s b) -> p s b", b=15)), [psX], [pexs])
        nseq, L = (1, 143) if prm else (16, 23)
        ntk = L - 15
        for g in range(4):
            w = 2 << g
            base = pext[:, g, :] if prm else pexs[:, g, :]
            bt = pext if prm else pexs
            cur3 = base.rearrange("p (s c) -> p s c", c=L); curt = bt
            bufs = [pw1, pw2]; bi = 0; m = 1
            while m < w:
                nt_ = bufs[bi % 2]; bi += 1
                n3 = nt_[:, 0:nseq * L].rearrange("p (s c) -> p s c", c=L)
                op("pool", lambda e: e.tensor_tensor(out=n3[:, :, m:L], in0=cur3[:, :, m:L], in1=cur3[:, :, 0:L - m], op=ALU.add), [curt], [nt_])
                cur3, curt = n3, nt_
                m *= 2
            pin = (PINV0 if (prm and T == 0) else PINV1)
            pin3 = pin[:, g, :].rearrange("p (s c) -> p s c", c=ntk)
            u3 = base.rearrange("p (s c) -> p s c", c=L)[:, :, 15:L]
            op("dve", lambda e: e.tensor_tensor(out=m0[:, :].rearrange("p (s c) -> p s c", c=ntk), in0=cur3[:, :, 15:L], in1=pin3, op=ALU.mult), [curt, pin], [m0])
            op("dve", lambda e: e.tensor_tensor(out=dT[:, :].rearrange("p (s c) -> p s c", c=ntk), in0=m0[:, :].rearrange("p (s c) -> p s c", c=ntk), in1=u3, op=ALU.subtract), [m0, bt], [dT])
            op("pe", lambda e: e.matmul(psX[:, 0:128], lhsT=wpool[:, l * 4 + g, :], rhs=dT[:, :], start=True, stop=True), [wpool, dT], [psX])
            op("dve", lambda e: e.scalar_tensor_tensor(out=gbT[:, g, :], in0=psX[:, 0:128], scalar=pscl[:, l * 4 + g:l * 4 + g + 1], in1=pgsT[:, g, :], op0=ALU.mult, op1=ALU.mult), [psX, pscl, pgsT], [gbT])
        if prm:
            op("pool", lambda e: e.tensor_copy(pw1[:, 0:60].rearrange("p (g c) -> p g c", c=15), pext[:, :, 128:143]), [pext], [pw1])
            op("pool", lambda e: e.tensor_copy(pext[:, :, 0:15], pw1[:, 0:60].rearrange("p (g c) -> p g c", c=15)), [pw1], [pext])

        if prm:
            for kv in range(2):
                op("dve", lambda e: e.tensor_scalar(out=cmb[:, kv, :], in0=cms[:, kv, :], scalar1=1.0 / 32, scalar2=pemean[:, l * 2 + kv:l * 2 + kv + 1], op0=ALU.mult, op1=ALU.add), [cms, pemean], [cmb])
                op("pe", lambda e: e.matmul(psX[:, kv * 4:kv * 4 + 4], lhsT=W2[:, l * 2 + kv, :], rhs=cmb[:, kv, :], start=True, stop=True), [W2, cmb], [psX])
            op("act", lambda e: e.copy(kcT[:, 4 * T:4 * T + 4], psX[:, 0:4]), [psX], [kcT])
            op("act", lambda e: e.copy(vcT[:, 4 * T:4 * T + 4], psX[:, 4:8]), [psX], [vcT])
            ncb = 4 * T + 4
        else:
            for s in range(NS):
                for hf in range(2):
                    rw = raw[hf]
                    for b8 in range(8):
                        pg = hf * 8 + b8
                        j = s * NPAGE + pg
                        S.dma("pool", lambda q: q.indirect_dma_start(out=rw[:, b8, :], out_offset=None, in_=c_cmp[l][:, :], in_offset=bass.IndirectOffsetOnAxis(ap=idx[:, j:j + 1], axis=0)), r=[idx.r], w=[rw.r])
                    for b8 in range(8):
                        pg = hf * 8 + b8
                        for kv in range(2):
                            op("pe", lambda e: e.matmul(psX[:, kv * 64 + pg * 4:kv * 64 + pg * 4 + 4], lhsT=rw[:, b8, kv * 128:(kv + 1) * 128], rhs=IND[:, :], start=True, stop=True), [rw, IND], [psX])
                for kv in range(2):
                    op("dve", lambda e: e.tensor_scalar(out=cmbs[:, kv, s * 64:(s + 1) * 64], in0=psX[:, kv * 64:(kv + 1) * 64], scalar1=pemean[:, l * 2 + kv:l * 2 + kv + 1], scalar2=None, op0=ALU.add), [psX, pemean], [cmbs])
            for kv, dst in ((0, kcT), (1, vcT)):
                for hf in range(2):
                    op("pe", lambda e: e.matmul(psA[:, :], lhsT=W2[:, l * 2 + kv, :], rhs=cmbs[:, kv, hf * 512:(hf + 1) * 512], start=True, stop=True), [W2, cmbs], [psA])
                    op("act", lambda e: e.copy(dst[:, hf * 512:(hf + 1) * 512], psA[:, :]), [psA], [dst])
            ncb = 1024
        nch = (ncb + 127) // 128
        for ch in range(nch):
            if prm and (ch + 1) * 128 > ncb:
                op("pool", lambda e: e.memset(vcT[:, ncb:(ch + 1) * 128], 0.0), [], [vcT])
            transpose_to(vctok[:, ch, :], vctok, vcT[:, ch * 128:(ch + 1) * 128], vcT)
        J = ncb // 2
        def select_group(g):
            if prm:
                Jn = J
                op("dve", lambda e: e.tensor_copy(impg[:, g, 0:J], impf[:, 0:J]), [impf], [impg])
                op("dve", lambda e: e.tensor_tensor(out=impg[:, g, J - 2:J], in0=impg[:, g, J - 2:J], in1=FIXK[:, :], op=ALU.mult), [impg, FIXK], [impg])
                op("dve", lambda e: e.tensor_tensor(out=impg[:, g, J - 2:J], in0=impg[:, g, J - 2:J], in1=FIXA[:, :], op=ALU.add), [impg, FIXA], [impg])
                op("dve", lambda e: e.memset(impg[:, g, 0:1], 1e4), [], [impg])
            else:
                Jn = 33
                op("dve", lambda e: e.tensor_reduce(out=impg[:, g, 0:32], in_=impf[:, :].rearrange("p (s j) -> p j s", j=32), axis=AX.X, op=ALU.add), [impf], [impg])
                op("dve", lambda e: e.memset(impg[:, g, 32:33], 2e4), [], [impg])
                op("dve", lambda e: e.memset(impg[:, g, 0:1], 1e4), [], [impg])
            op("pool", lambda e: e.memset(selm[:, :], 0.0), [], [selm])
            if Jn <= 16:
                op("dve", lambda e: e.tensor_scalar(out=selm[:, 0:Jn], in0=impg[:, g, 0:Jn], scalar1=-1e29, scalar2=None, op0=ALU.is_gt), [impg], [selm])
            else:
                op("dve", lambda e: e.max(out=m8[:, 0:8], in_=impg[:, g, 0:Jn]), [impg], [m8])
                op("dve", lambda e: e.match_replace(out=imp2[:, 0:Jn], in_to_replace=m8[:, 0:8], in_values=impg[:, g, 0:Jn], imm_value=-1e30), [impg, m8], [imp2])
                op("dve", lambda e: e.max(out=m8[:, 0:8], in_=imp2[:, 0:Jn]), [imp2], [m8])
                op("dve", lambda e: e.tensor_scalar(out=selm[:, 0:Jn], in0=impg[:, g, 0:Jn], scalar1=m8[:, 7:8], scalar2=None, op0=ALU.is_ge), [impg, m8], [selm])
            transpose_to(selT[g][:, :], selT[g], selm[:, :], selm)

        def head_a(h):
            g, r = h // 4, h % 4
            par = h % 2
            scx, pbx, mx = scH[par], pbH[par], m8H[par]
            pst = psS if par == 0 else psA
            op("pe", lambda e: e.matmul(pst[:, 0:ncb], lhsT=nqT[64 * g:64 * g + 64, r, :], rhs=kcT[64 * g:64 * g + 64, 0:ncb], start=True, stop=True), [nqT, kcT], [pst])
            op("act", lambda e: e.copy(scx[:, 0:ncb], pst[:, 0:ncb]), [pst], [scx])
            if T == 0:
                op("dve", lambda e: e.tensor_tensor(out=scx[:, 0:4], in0=scx[:, 0:4], in1=BCT[:, h, 4:8], op=ALU.add), [scx, BCT], [scx])
            else:
                op("dve", lambda e: e.tensor_tensor(out=scx[:, ncb - 8:ncb], in0=scx[:, ncb - 8:ncb], in1=BCT[:, h, :], op=ALU.add), [scx, BCT], [scx])
            op("dve", lambda e: e.tensor_reduce(out=mx[:, 0:1], in_=scx[:, 0:ncb], axis=AX.X, op=ALU.max), [scx], [mx])
            op("dve", lambda e: e.tensor_scalar(out=mx[:, 1:2], in0=mx[:, 0:1], scalar1=-1.0, scalar2=None, op0=ALU.mult), [mx], [mx])
            op("act", lambda e: e.activation(out=scx[:, 0:ncb], in_=scx[:, 0:ncb], func=AF.Exp, bias=mx[:, 1:2], accum_out=mx[:, 2:3]), [scx, mx], [scx, mx])
            op("dve", lambda e: e.reciprocal(mx[:, 3:4], mx[:, 2:3]), [mx], [mx])
            if T == 0:
                op("dve", lambda e: e.tensor_tensor(out=mx[:, 3:4], in0=mx[:, 3:4], in1=RV0[:, :], op=ALU.mult), [mx, RV0], [mx])
            op("dve", lambda e: e.tensor_scalar(out=scx[:, 0:ncb], in0=scx[:, 0:ncb], scalar1=mx[:, 3:4], scalar2=None, op0=ALU.mult), [scx, mx], [scx])
            if ncb % 128:
                op("pool", lambda e: e.memset(pbx[:, ncb:nch * 128], 0.0), [], [pbx])
            op("act", lambda e: e.copy(pbx[:, 0:ncb], scx[:, 0:ncb]), [scx], [pbx])
            ev = scx[:, 0:ncb].rearrange("p (j u) -> p j u", u=2)
            if r == 0:
                op("pool", lambda e: e.tensor_tensor(out=impf[:, 0:J], in0=ev[:, :, 0], in1=ev[:, :, 1], op=ALU.add), [scx], [impf])
            else:
                op("pool", lambda e: e.tensor_tensor(out=imt[:, 0:J], in0=ev[:, :, 0], in1=ev[:, :, 1], op=ALU.add), [scx], [imt])
                op("pool", lambda e: e.tensor_tensor(out=impf[:, 0:J], in0=impf[:, 0:J], in1=imt[:, 0:J], op=ALU.add), [impf, imt], [impf])
            if r == 3:
                select_group(g)

        def head_b(h):
            g = h // 4
            par = h % 2
            pbx, ptx = pbH[par], pTH[par]
            for ch in range(nch):
                op("pe", lambda e: e.transpose(psT[:, par * 512 + ch * 128:par * 512 + (ch + 1) * 128], pbx[:, ch * 128:(ch + 1) * 128], identb[:, :]), [pbx, identb], [psT])
            op("act", lambda e: e.copy(ptx.h.rearrange("p c q -> p (c q)")[:, 0:nch * 128], psT[:, par * 512:par * 512 + nch * 128]), [psT], [ptx])
            for ch in range(nch):
                op("pe", lambda e: e.matmul(psX[:, h * 64:(h + 1) * 64], lhsT=ptx[:, ch, :], rhs=vctok[:, ch, g * 64:(g + 1) * 64], start=(ch == 0), stop=(ch == nch - 1)), [ptx, vctok], [psX])

        if prm:
            assert ncb <= 512
            for h in range(8):
                head_a(h)
                if h >= 1:
                    head_b(h - 1)
            head_b(7)
        else:
            for h in range(8):
                g, r = h // 4, h % 4
                for c0 in range(0, ncb, 512):
                    n = min(512, ncb - c0)
                    pst = psS if c0 == 0 else psM
                    op("pe", lambda e: e.matmul(pst[:, 0:n], lhsT=nqT[64 * g:64 * g + 64, r, :], rhs=kcT[64 * g:64 * g + 64, c0:c0 + n], start=True, stop=True), [nqT, kcT], [pst])
                    op("act", lambda e: e.copy(sc[:, c0:c0 + n], pst[:, 0:n]), [pst], [sc])
                if prm:
                    if T == 0:
                        op("dve", lambda e: e.tensor_tensor(out=sc[:, 0:4], in0=sc[:, 0:4], in1=BCT[:, h, 4:8], op=ALU.add), [sc, BCT], [sc])
                    else:
                        op("dve", lambda e: e.tensor_tensor(out=sc[:, ncb - 8:ncb], in0=sc[:, ncb - 8:ncb], in1=BCT[:, h, :], op=ALU.add), [sc, BCT], [sc])
                else:
                    sc3 = sc[:, :].rearrange("p (s c) -> p s c", c=64)
                    op("dve", lambda e: e.tensor_tensor(out=sc3, in0=sc3, in1=XM[:, :].unsqueeze(2).to_broadcast([128, 16, 64]), op=ALU.add), [sc, XM], [sc])
                    op("dve", lambda e: e.tensor_tensor(out=sc3[:, :, 60:64], in0=sc3[:, :, 60:64], in1=BCS[:, h, :].unsqueeze(1).to_broadcast([128, 16, 4]), op=ALU.add), [sc, BCS], [sc])
                op("dve", lambda e: e.tensor_reduce(out=m8[:, 8:9], in_=sc[:, 0:ncb], axis=AX.X, op=ALU.max), [sc], [m8])
                op("dve", lambda e: e.tensor_scalar(out=m8[:, 9:10], in0=m8[:, 8:9], scalar1=-1.0, scalar2=None, op0=ALU.mult), [m8], [m8])
                op("act", lambda e: e.activation(out=sc[:, 0:ncb], in_=sc[:, 0:ncb], func=AF.Exp, bias=m8[:, 9:10], accum_out=m8[:, 10:11]), [sc, m8], [sc, m8])
                op("dve", lambda e: e.reciprocal(m8[:, 11:12], m8[:, 10:11]), [m8], [m8])
                if prm and T == 0:
                    op("dve", lambda e: e.tensor_tensor(out=m8[:, 11:12], in0=m8[:, 11:12], in1=RV0[:, :], op=ALU.mult), [m8, RV0], [m8])
                op("dve", lambda e: e.tensor_scalar(out=sc[:, 0:ncb], in0=sc[:, 0:ncb], scalar1=m8[:, 11:12], scalar2=None, op0=ALU.mult), [sc, m8], [sc])
                if ncb % 128:
                    op("pool", lambda e: e.memset(pb[:, ncb:nch * 128], 0.0), [], [pb])
                op("act", lambda e: e.copy(pb[:, 0:ncb], sc[:, 0:ncb]), [sc], [pb])
                ev = sc[:, 0:ncb].rearrange("p (j u) -> p j u", u=2)
                if r == 0:
                    op("pool", lambda e: e.tensor_tensor(out=impf[:, 0:J], in0=ev[:, :, 0], in1=ev[:, :, 1], op=ALU.add), [sc], [impf])
                else:
                    op("pool", lambda e: e.tensor_tensor(out=imt[:, 0:J], in0=ev[:, :, 0], in1=ev[:, :, 1], op=ALU.add), [sc], [imt])
                    op("pool", lambda e: e.tensor_tensor(out=impf[:, 0:J], in0=impf[:, 0:J], in1=imt[:, 0:J], op=ALU.add), [impf, imt], [impf])
                for ch in range(nch):
                    op("pe", lambda e: e.transpose(psT[:, ch * 128:(ch + 1) * 128], pb[:, ch * 128:(ch + 1) * 128], identb[:, :]), [pb, identb], [psT])
                op("act", lambda e: e.copy(pT.h[:].rearrange("p c q -> p (c q)")[:, 0:nch * 128], psT[:, 0:nch * 128]), [psT], [pT])
                for ch in range(nch):
                    op("pe", lambda e: e.matmul(psX[:, h * 64:(h + 1) * 64], lhsT=pT[:, ch, :], rhs=vctok[:, ch, g * 64:(g + 1) * 64], start=(ch == 0), stop=(ch == nch - 1)), [pT, vctok], [psX])
                if r == 3:
                    select_group(g)
        op("dve", lambda e: e.tensor_tensor(out=oc[:, :].rearrange("p (h c) -> p h c", c=64), in0=psX[:, :].rearrange("p (h c) -> p h c", c=64),
                                            in1=nbgs[:, 0:8].unsqueeze(2).to_broadcast([128, 8, 64]), op=ALU.mult), [psX, nbgs], [oc])
        if prm:
            dls = [d for d in range(4, -1, -1) if T - d >= 0]
            for g in range(2):
                qap = nqT[64 * g:64 * g + 64, :, :]
                for i, dl in enumerate(dls):
                    kb = (T - dl) % 8
                    masks = []
                    if dl <= 1:
                        masks = [(E01[:, g * 2 + dl, :], E01)]
                    elif dl == 4:
                        masks = [(MW4[:, :], MW4)]
                    attn_slot(g, winKT[64 * g:64 * g + 64, kb, :], winKT, qap, 512, winV[:, kb, g * 65:(g + 1) * 65], winV, masks, 0, i == 0, i == len(dls) - 1)
            finalize_branch(16, False)
            for g in range(2):
                qap = nqT[64 * g:64 * g + 64, :, :]
                for KB in range(T + 1):
                    dl = T - KB
                    masks = [(E01[:, g * 2 + dl, :], E01)] if dl <= 1 else []
                    attn_slot(g, selKT[64 * g:64 * g + 64, KB * 128:(KB + 1) * 128], selKT, qap, 512, selV[:, KB, g * 65:(g + 1) * 65], selV, masks, 0, KB == 0, KB == T,
                              selexp=(FEXP[:, KB * 128:(KB + 1) * 128], selT[g][:, :], selT[g], 128))
            finalize_branch(8, False)
        else:
            for branch in ("win", "sel"):
                for s in range(NS):
                    nb = 4 if branch == "win" else NPAGE
                    for b0 in range(0, nb, 8):
                        n8 = min(8, nb - b0)
                        rw = raw[(b0 // 8) % 2]
                        if branch == "win":
                            dma("sp", rw[:, 0:4, :], c_win[l, s * 512:(s + 1) * 512, :].rearrange("(b p) c -> p b c", p=128), w=[rw])
                        else:
                            for b in range(n8):
                                j = s * NPAGE + b0 + b
                                S.dma("pool", lambda q: q.indirect_dma_start(out=rw[:, b, :], out_offset=None, in_=c_sel[l][:, :], in_offset=bass.IndirectOffsetOnAxis(ap=idx[:, j:j + 1], axis=0)), r=[idx.r], w=[rw.r])
                        op("dve", lambda e: e.tensor_copy(kpg[:, 0:n8, :], rw[:, 0:n8, 0:128]), [rw], [kpg])
                        op("pool", lambda e: e.tensor_copy(Vs[:, b0:b0 + n8, :].rearrange("p b (g c) -> p b g c", c=65)[:, :, :, 0:64], rw[:, 0:n8, 128:256].rearrange("p b (g c) -> p b g c", c=64)), [rw], [Vs])
                        for b in range(n8):
                            op("pe", lambda e: e.transpose(psT[:, b * 128:(b + 1) * 128], kpg[:, b, :], identb[:, :]), [kpg, identb], [psT])
                        op("act", lambda e: e.copy(KTs.h[:].rearrange("p b k -> p (b k)")[:, b0 * 128:(b0 + n8) * 128], psT[:, 0:n8 * 128]), [psT], [KTs])
                    for g in range(2):
                        qap = nqT[64 * g:64 * g + 64, :, s * 8:(s + 1) * 8]
                        col0 = s * 32
                        for b in range(nb):
                            masks = []
                            sx = None
                            if branch == "win":
                                if b == 0:
                                    masks = [(MW0S[:, :], MW0S)]
                                elif b == 3:
                                    masks = [(E15[:, g, :], E15)]
                            else:
                                if b == 15:
                                    masks = [(E15[:, g, :], E15)]
                                sx = (FEXP[:, b * 128:(b + 1) * 128], selT[g][:, s * 8:(s + 1) * 8], selT[g], 8)
                            attn_slot(g, KTs[64 * g:64 * g + 64, b, :], KTs, qap, 32, Vs[:, b, g * 65:(g + 1) * 65], Vs, masks, col0, b == 0, False, selexp=sx)
                        ok, ov = (ownKw, ownVw) if branch == "win" else (ownKs, ownVs)
                        attn_slot(g, ok[64 * g:64 * g + 64, :], ok, qap, 32, ov[:, g * 65:(g + 1) * 65], ov, [(ESD[:, g * 16 + s, :], ESD)], col0, False, True)
                flush_slots()
                for g in range(2):
                    op("act", lambda e: e.copy(oTs[0:65, :].rearrange("p (r s t) -> p r s t", r=4, t=8), psO[g][0:65, :].rearrange("p (s r t) -> p r s t", r=4, t=8)), [psO[g]], [oTs])
                    for r in range(4):
                        op("pe", lambda e: e.matmul(psX[:, r * 65:(r + 1) * 65], lhsT=oTs[0:65, r * 128:(r + 1) * 128], rhs=identf[0:65, 0:65], start=True, stop=True), [oTs, identf], [psX])
                    gate0 = 16 if branch == "win" else 8
                    den = psX[:, 0:260].rearrange("p (r c) -> p r c", c=65)[:, :, 64]
                    op("dve", lambda e: e.tensor_scalar(out=wgt[:, 0:4], in0=den, scalar1=1e-30, scalar2=None, op0=ALU.max), [psX], [wgt])
                    op("dve", lambda e: e.reciprocal(wgt[:, 4:8], wgt[:, 0:4]), [wgt], [wgt])
                    op("dve", lambda e: e.tensor_tensor(out=wgt[:, 0:4], in0=wgt[:, 4:8], in1=nbgs[:, gate0 + 4 * g:gate0 + 4 * g + 4], op=ALU.mult), [wgt, nbgs], [wgt])
                    for r in range(4):
                        h = 4 * g + r
                        op("dve", lambda e: e.scalar_tensor_tensor(out=oc[:, h * 64:(h + 1) * 64], in0=psX[:, r * 65:r * 65 + 64], scalar=wgt[:, r:r + 1], in1=oc[:, h * 64:(h + 1) * 64], op0=ALU.mult, op1=ALU.add), [psX, wgt, oc], [oc])
        op("dve", lambda e: e.tensor_tensor(out=gc[:, :], in0=oc[:, :], in1=ngs[:, :], op=ALU.mult), [oc, ngs], [gc])
        for c in range(4):
            transpose_to(gcT[:, c, :], gcT, gc[:, c * 128:(c + 1) * 128], gc)

        for mg_ in range(2):
            wa_t = wload(wb[0], mg_ * 512, 512)
            wbc_t = wload(wb[1], mg_ * 512, 512)
            for mm in range(4):
                m = mg_ * 4 + mm
                pst = pp[pi[0] % 2]; pi[0] += 1
                for kc in range(8):
                    wap, wtl = wa_t.col(kc, mm * 128, (mm + 1) * 128)
                    op("pe", lambda e: e.matmul(pst[:, 0:128], lhsT=wap, rhs=gaT[:, kc, :], start=(kc == 0), stop=(kc == 7)), [wtl, gaT], [pst])
                for kc in range(4):
                    wap, wtl = wbc_t.col(kc, mm * 128, (mm + 1) * 128)
                    op("pe", lambda e: e.matmul(pst[:, 128:256], lhsT=wap, rhs=gbT[:, kc, :], start=(kc == 0), stop=(kc == 3)), [wtl, gbT], [pst])
                for kc in range(4):
                    wap, wtl = wbc_t.col(4 + kc, mm * 128, (mm + 1) * 128)
                    op("pe", lambda e: e.matmul(pst[:, 256:384], lhsT=wap, rhs=gcT[:, kc, :], start=(kc == 0), stop=(kc == 3)), [wtl, gcT], [pst])
                op("dve", lambda e: e.tensor_tensor(out=m0[:, :], in0=pst[:, 0:128], in1=mgs[:, m, :], op=ALU.mult), [pst, mgs], [m0])
                op("dve", lambda e: e.tensor_tensor(out=m1[:, :], in0=pst[:, 128:256], in1=mgs[:, 8 + m, :], op=ALU.mult), [pst, mgs], [m1])
                op("pool", lambda e: e.tensor_tensor(out=m0[:, :], in0=m0[:, :], in1=m1[:, :], op=ALU.add), [m0, m1], [m0])
                op("dve", lambda e: e.tensor_tensor(out=m1[:, :], in0=pst[:, 256:384], in1=mgs[:, 16 + m, :], op=ALU.mult), [pst, mgs], [m1])
                op("pool", lambda e: e.tensor_tensor(out=mT[:, m, :], in0=m0[:, :], in1=m1[:, :], op=ALU.add), [m0, m1], [mT])
        for hf in range(2):
            wo_t = wload(wb[2], hf * 512, 512)
            pst = pp[pi[0] % 2]; pi[0] += 1
            for a0 in (0, 256):
                for kc in range(8):
                    wap, wtl = wo_t.col(kc, a0, a0 + 256)
                    op("pe", lambda e: e.matmul(pst[:, a0:a0 + 256], lhsT=mT[:, kc, :], rhs=wap, start=(kc == 0), stop=(kc == 7)), [wtl, mT], [pst])
            op("dve", lambda e: e.tensor_tensor(out=xt[:, hf * 512:(hf + 1) * 512], in0=xt[:, hf * 512:(hf + 1) * 512], in1=pst[:, :], op=ALU.add), [xt, pst], [xt])
        if l == 0:
            dma("pool", y1p[pos0:pos0 + 128, :] if prm else y1s[:, :], xt[:, :], r=[xt], w=[Y1])
        else:
            dma("sp", sc[:, :], fgain[0:1, :].to_broadcast([128, D]), w=[sc, scH[0], scH[1]])
            rstd = rmsnorm_rstd(xt, 0)
            op("dve", lambda e: e.scalar_tensor_tensor(out=rgs[:, :], in0=xt[:, :], scalar=rstd, in1=sc[:, :], op0=ALU.mult, op1=ALU.mult), [xt, st1, sc, scH[0], scH[1]], [rgs])
            dma("pool", y_p[pos0:pos0 + 128, :] if prm else y_s[:, :], rgs[:, :], r=[rgs])

    Y1 = Tl(None)
    for l in range(DEPTH):
        op("dve", lambda e: e.memset(Sst.h[:], 0.0), [], [Sst])
        op("pool", lambda e: e.memset(Sbf.h[:], 0.0), [], [Sbf])
        op("pool", lambda e: e.memset(pext.h[:], 0.0), [], [pext])
        if l == 1:
            S.finish()
        for T in range(NTP):
            unit(l, "p", T)
        S.barrier_all()
        unit(l, "s", 0)
        S.barrier_all()
    S.finish()
    es.close()
    S.close()
    return nc


_CACHE = {}


def _prep_shared(inp, seq):
    f32 = np.float32
    ci = _col_index()
    tabs = _host_tables(seq)
    sh = {}
    w_in = np.asarray(inp["w_in"], f32)
    w_pad = np.concatenate([w_in, np.zeros(w_in.shape[:2] + (1,), f32)], axis=2)
    sh["w_all"] = np.ascontiguousarray(w_pad[:, :, ci])
    sh["w_b"] = np.ascontiguousarray(np.concatenate([np.asarray(inp[k], f32) for k in ("w_br_a", "w_br_b", "w_br_c", "w_out")], axis=1))
    g = np.asarray(inp["norm_gain"], f32)
    sh["gains_fm"] = np.ascontiguousarray(g.reshape(DEPTH, 8, 128).transpose(2, 0, 1).reshape(128, 16))
    sh["fgain"] = np.asarray(inp["final_gain"], f32).reshape(1, D)
    sh["w_pool"] = np.asarray(inp["w_pool"], f32).reshape(DEPTH, 512, 128)
    sh["pscale_fm"] = np.ascontiguousarray(np.asarray(inp["pool_scale"], f32).reshape(DEPTH, 4, 128).transpose(2, 0, 1).reshape(128, 8))
    sh["pe_cmp"] = np.asarray(inp["pe_cmp"], f32).reshape(DEPTH, 32, 256)
    sh["w_ck"] = np.asarray(inp["w_ck"], f32); sh["w_cv"] = np.asarray(inp["w_cv"], f32)
    sh["rel_bias"] = np.asarray(inp["rel_bias"], f32)
    for l in range(DEPTH):
        sh["c_cmp%d" % l] = np.asarray(inp["cache_cmp"], f32)[l].reshape(-1, 256)
        sh["c_sel%d" % l] = np.asarray(inp["cache_sel"], f32)[l].reshape(-1, 256)
    for k, shp in TABLE_SHAPES(seq).items():
        sh["t_" + k] = np.ascontiguousarray(np.asarray(tabs[k], f32).reshape(shp))
    return sh


def kernel(**inp):
    f32 = np.float32
    xpr = np.asarray(inp["x_prompt"], f32)
    B, seq, _ = xpr.shape
    n_cores = N_CORES
    nb = np.asarray(inp["x_sample"]).shape[0]
    assert nb == n_cores * NS
    if seq not in _CACHE:
        _CACHE[seq] = build(seq)
    nc = _CACHE[seq]
    sh = _prep_shared(inp, seq)
    xs = np.asarray(inp["x_sample"], f32)
    st_ret = np.asarray(inp["state_ret"], f32); st_pool = np.asarray(inp["state_pool"], f32)
    c_win = np.asarray(inp["cache_win"], f32); pt = np.asarray(inp["page_table"], np.int32)
    in_maps = []
    for c in range(n_cores):
        b = c % B
        sl = slice(c * NS, (c + 1) * NS)
        m = dict(sh)
        m["xp"] = np.ascontiguousarray(xpr[b])
        m["xs"] = np.ascontiguousarray(xs[sl].reshape(NS * TS, D))
        m["st_ret"] = np.ascontiguousarray(st_ret[:, sl].reshape(DEPTH, NS * RH * DK, DV))
        m["st_pool"] = np.ascontiguousarray(st_pool[:, sl].reshape(DEPTH, NS * 15, 512))
        m["c_win"] = np.ascontiguousarray(c_win[:, sl].reshape(DEPTH, NS * 512, 256))
        m["ptab"] = np.ascontiguousarray(pt[sl].reshape(1, NS * NPAGE))
        in_maps.append(m)
    res = run_bass_kernel_spmd(nc, in_maps, core_ids=list(range(n_cores))).results
    wk = min(512, seq)

    def cat_p(name, shape):
        return np.stack([res[b][name].reshape(shape) for b in range(B)], axis=1) if shape[0] == DEPTH else None

    y_prompt = np.stack([res[b]["y_p"] for b in range(B)]).astype(f32)
    y_sample = np.concatenate([res[c]["y_s"].reshape(NS, TS, D) for c in range(n_cores)]).astype(f32)
    ret_prompt = np.stack([res[b]["ret_p"].reshape(DEPTH, RH, DK, DV) for b in range(B)], axis=1)
    ret_sample = np.concatenate([res[c]["ret_s"].reshape(DEPTH, NS, RH, DK, DV) for c in range(n_cores)], axis=1)
    pool_prompt = np.stack([res[b]["pool_p"].reshape(DEPTH, 15, 512) for b in range(B)], axis=1)
    pool_sample = np.concatenate([res[c]["pool_s"].reshape(DEPTH, NS, 15, 512) for c in range(n_cores)], axis=1)
    win_prompt = np.stack([res[b]["win_p"].reshape(DEPTH, wk, 2, 2, 64) for b in range(B)], axis=1)
    win_sample = np.concatenate([res[c]["win_s"].reshape(DEPTH, NS, 512, 2, 2, 64) for c in range(n_cores)], axis=1)
    cmp_prompt = np.stack([res[b]["cmp_p"].reshape(DEPTH, seq, 2, 2, 64) for b in range(B)], axis=1)
    cmp_sample = np.concatenate([res[c]["cmp_s"].reshape(DEPTH, NS, TS, 2, 2, 64) for c in range(n_cores)], axis=1)
    sel_prompt = np.stack([res[b]["sel_p"].reshape(DEPTH, seq, 2, 2, 64) for b in range(B)], axis=1)
    sel_sample = np.concatenate([res[c]["sel_s"].reshape(DEPTH, NS, TS, 2, 2, 64) for c in range(n_cores)], axis=1)
    outs = (y_prompt, y_sample, ret_prompt, ret_sample, pool_prompt, pool_sample, win_prompt, win_sample,
            cmp_prompt, cmp_sample, sel_prompt, sel_sample)
    return tuple(np.ascontiguousarray(o, dtype=f32) for o in outs)
```

```python
import math
from contextlib import ExitStack
import numpy as np
import ml_dtypes
import concourse.bass as bass
import concourse.mybir as mybir
from concourse.bass_utils import run_bass_kernel_spmd

F32 = mybir.dt.float32
BF16 = mybir.dt.bfloat16
I32 = mybir.dt.int32
AF = mybir.ActivationFunctionType
ALU = mybir.AluOpType
AX = mybir.AxisListType

D = 1024
SEQ = 8192
N_CORES = 8
NS = 16
TS = 8
PAST = 2048
NPAGE = 16
PAGE = 128
NPHYS = 2560
DEPTH = 2
EPS = 1e-6
RH, DK, DV = 4, 128, 256
NEG = -30000.0

O_RQ, O_RK, O_RV, O_RG, O_PU, O_PG, O_NQ, O_KV, O_NBG, O_NG, O_MG = (
    0, 512, 1024, 2048, 3072, 3584, 4096, 4608, 5376, 5400, 5912)
NFM = 56
TM_RV, TM_RG, TM_NG, TM_KVR, TM_PUT = 7168, 8192, 9216, 9728, 10752
NCA = 11264


def _swap(idx):
    return idx.reshape(-1, 2)[:, ::-1].reshape(-1)


def _col_index():
    ar = np.arange
    fm = []
    for h in range(4):
        fm += [O_RQ + h * 128 + ar(128), _swap(O_RQ + h * 128 + ar(128))]
    for h in range(4):
        fm += [O_RK + h * 128 + ar(128), _swap(O_RK + h * 128 + ar(128))]
    fm += [O_PU + g * 128 + ar(128) for g in range(4)]
    fm += [O_PG + g * 128 + ar(128) for g in range(4)]
    for c in range(4):
        fm += [np.concatenate([O_NQ + c * 64 + ar(64), O_NQ + (4 + c) * 64 + ar(64)])]
    fm += [O_KV + 0 + ar(128), O_KV + 128 + ar(128), O_KV + 256 + ar(128), O_KV + 512 + ar(128)]
    fm += [O_MG + j * 128 + ar(128) for j in range(24)]
    tm = [O_RV + ar(1024), O_RG + ar(1024), O_NG + ar(512), O_KV + ar(768), O_NBG + ar(24), np.full(232, -1), O_PU + ar(512)]
    idx = np.concatenate(fm + tm)
    assert idx.shape[0] == NCA
    return idx


class Reg:
    __slots__ = ("w", "rs")

    def __init__(self):
        self.w = None
        self.rs = {}


class Sched:
    EPOCH = 30000

    def __init__(self, nc, n_dma_sems=40):
        self.nc = nc
        self.engs = {"pe": nc.tensor, "act": nc.scalar, "dve": nc.vector,
                     "pool": nc.gpsimd, "sp": nc.sync}
        self.sems = []
        self._ctx = []
        self.cur = {}
        self.cnt = {}
        self.last = {}
        for k in self.engs:
            self.cur[k] = self._new_sem("s_" + k)
            self.cnt[k] = 0
            self.last[k] = []
        self.dkeys, self.dval = [], []
        for i in range(n_dma_sems):
            self.dkeys.append(self._new_sem("d%d" % i))
            self.dval.append(0)
        self.dnext = 0
        self.known = {k: {} for k in self.engs}
        self.n_ins = 0
        self.n_wait = 0

    def _new_sem(self, name):
        cm = self.nc.semaphore(name + "_%d" % len(self.sems))
        self.sems.append(cm.__enter__())
        self._ctx.append(cm)
        return len(self.sems) - 1

    def close(self):
        for cm in reversed(self._ctx):
            cm.__exit__(None, None, None)

    def _wait(self, e, dep):
        if dep is None:
            return
        key, val = dep
        if key == self.cur[e] and e in ("pe", "sp"):
            return
        if self.known[e].get(key, 0) >= val:
            return
        self.engs[e].wait_ge(self.sems[key], val)
        self.known[e][key] = val
        self.n_wait += 1

    def _deps(self, e, r, w):
        for x in r:
            self._wait(e, x.w)
        for x in w:
            self._wait(e, x.w)
            for k, v in list(x.rs.items()):
                self._wait(e, (k, v))

    def _mark(self, tok, r, w):
        k, v = tok
        for x in r:
            if x.rs.get(k, 0) < v:
                x.rs[k] = v
        for x in w:
            x.w = tok
            x.rs = {}

    def op(self, e, fn, r=(), w=()):
        if self.cnt[e] >= self.EPOCH:
            self.last[e].append((self.cur[e], self.cnt[e]))
            self.cur[e] = self._new_sem("s_" + e)
            self.cnt[e] = 0
        self._deps(e, r, w)
        ins = fn(self.engs[e])
        self.cnt[e] += 1
        ins.then_inc(self.sems[self.cur[e]], 1)
        self._mark((self.cur[e], self.cnt[e]), r, w)
        self.n_ins += 1

    def dma(self, e, fn, r=(), w=()):
        i = self.dnext
        self.dnext = (self.dnext + 1) % len(self.dkeys)
        k = self.dkeys[i]
        if self.dval[i] > 0:
            self._wait(e, (k, self.dval[i]))
        self._deps(e, r, w)
        ins = fn(self.engs[e])
        self.dval[i] += 16
        ins.then_inc(self.sems[k], 16)
        self._mark((k, self.dval[i]), r, w)
        self.n_ins += 1

    def _all_tokens(self):
        toks = []
        for i, k in enumerate(self.dkeys):
            if self.dval[i] > 0:
                toks.append((k, self.dval[i]))
        for e in ("pe", "act", "dve", "pool"):
            toks += self.last[e]
            if self.cnt[e] > 0:
                toks.append((self.cur[e], self.cnt[e]))
        return toks

    def finish(self):
        for t in self._all_tokens():
            self._wait("sp", t)

    def barrier_all(self):
        toks = self._all_tokens()
        for e in ("sp", "pe", "act", "dve", "pool"):
            for t in toks:
                if t[0] != self.cur[e]:
                    self._wait(e, t)


class Tl:
    def __init__(self, h):
        self.h = h
        self.r = Reg()

    def __getitem__(self, k):
        return self.h[k]


class Sub:
    def __init__(self, ap):
        self.h = ap
        self.r = Reg()

    def __getitem__(self, k):
        return self.h[k]


class Vw:
    def __init__(self, base, ap):
        self.h = ap
        self.r = base.r

    def __getitem__(self, k):
        return self.h[k]


def _t5_bucket(dist):
    n = np.maximum(np.asarray(dist, np.int64), 0)
    nf = np.maximum(n, 1).astype(np.float32)
    lg = np.log(nf / np.float32(16)).astype(np.float32) / np.float32(math.log(128 / 16))
    large = 16 + (lg.astype(np.float32) * np.float32(16)).astype(np.int32)
    large = np.minimum(large, 31)
    return np.where(n < 16, n, large).astype(np.int32)


def _onehot_cols(dist, valid):
    b = _t5_bucket(np.asarray(dist, np.int32))
    oh = np.zeros((32,) + dist.shape, np.float32)
    for k in range(32):
        oh[k] = ((b == k) & valid).astype(np.float32)
    return oh


def _host_tables(seq):
    f32 = np.float32
    tb = {}
    inv = (10000.0 ** (-np.arange(0, DK, 2, dtype=f32) / DK)).astype(f32)

    def rot(pos):
        ang = pos.astype(f32)[None, :] * np.repeat(inv, 2)[:, None]
        c = np.cos(ang).astype(f32)
        s = np.sin(ang).astype(f32)
        sign = np.where(np.arange(DK) % 2 == 0, -1.0, 1.0).astype(f32)[:, None]
        ss = s * sign
        sc = f32(DK ** -0.5)
        return np.stack([c, ss, c * sc, ss * sc]).astype(f32)

    tb["rotp"] = rot(np.arange(seq))
    tb["rots"] = rot(PAST + (np.arange(128) % TS))
    log_g = np.log1p(-np.exp2(-5.0 - np.arange(RH, dtype=f32))).astype(f32)

    def ret_tabs(c, same):
        idx = np.arange(128)
        loc = idx % c
        rel = (loc[None, :] - loc[:, None]).astype(f32)
        dm = np.zeros((128, RH, 128), f32)
        cr = np.zeros((1, RH, 128), f32)
        tl = np.zeros((128, RH), f32)
        for h in range(RH):
            m = np.where((rel >= 0) & same, np.exp(np.maximum(rel, 0) * log_g[h]), 0.0)
            dm[:, h, :] = m
            cr[0, h, :] = np.exp((loc + 1.0) * log_g[h])
            tl[:, h] = np.exp((c - 1.0 - loc) * log_g[h])
        return dm, cr, tl

    allsame = np.ones((128, 128), bool)
    seqid = np.arange(128) // TS
    sames = seqid[:, None] == seqid[None, :]
    tb["dmt_p"], tb["cr_p"], tb["tail_p"] = ret_tabs(128, allsame)
    tb["dmt_s"], tb["cr_s"], tb["tail_s"] = ret_tabs(TS, sames)
    tb["gdec_p"] = [float(np.exp(128 * log_g[h])) for h in range(RH)]
    tb["gdec_s"] = [float(np.exp(TS * log_g[h])) for h in range(RH)]
    pin0 = np.zeros((1, 4, 128), f32)
    pin1 = np.zeros((1, 4, 128), f32)
    for g, w in enumerate((2, 4, 8, 16)):
        pin0[0, g] = 1.0 / np.minimum(np.arange(128) + 1, w)
        pin1[0, g] = 1.0 / w
    tb["pinv0"], tb["pinv1"] = pin0, pin1
    k = np.arange(128)
    cols, valid = [], []
    for dl in (0, 1):
        for q in range(128):
            d = 128 * dl + q - k
            cols.append(d); valid.append(d >= 0)
    for t in range(TS):
        d = 128 + t - k
        cols.append(d); valid.append(d >= 0)
    for c in range(128):
        d = (c % TS) - (k % TS)
        cols.append(d); valid.append((d >= 0) & ((c // TS) == (k // TS)))
    for x in range(8):
        d = k - 32 * (x - 4) - 31
        cols.append(d); valid.append(d >= 0)
    for x in range(4):
        d = PAST + (k % TS) - 32 * (60 + x) - 31
        cols.append(d); valid.append(d >= 0)
    cols = np.stack(cols); valid = np.stack(valid)
    tb["oh"] = _onehot_cols(cols, valid)
    tb["ohvalid"] = valid.T.astype(f32).copy()
    tb["mw4"] = np.tile((k[None, :] < k[:, None]).astype(f32), (1, 4))
    tb["mw0s"] = np.tile((k[:, None] > np.arange(TS)[None, :]).astype(f32), (1, 4))
    xm = np.where(seqid[:, None] == np.arange(NS)[None, :], 0.0, NEG).astype(f32)
    tb["xm"] = xm
    tb["sm"] = (seqid[:, None] == np.arange(NS)[None, :]).astype(f32)
    smrow = np.zeros((1, NS, 128), f32)
    for s in range(NS):
        smrow[0, s, s * TS:(s + 1) * TS] = 1.0
    tb["smrow"] = smrow
    rv0 = (k >= 31).astype(f32)[:, None]
    tb["rv0"] = rv0
    fixk = np.zeros((128, 2), f32); fixa = np.zeros((128, 2), f32)
    fixk[:, 0] = (k >= 64); fixa[:, 0] = np.where(k < 64, 2e4, 0.0)
    fixk[:, 1] = 0.0; fixa[:, 1] = np.where(k < 64, -1e30, 2e4)
    tb["fixk"], tb["fixa"] = fixk, fixa
    nkb = max(seq // 128, NPAGE)
    fexp = np.zeros((128, nkb * 128), f32)
    for m in range(nkb * 128):
        if m // 64 < 128:
            fexp[m // 64, m] = 1.0
    tb["fexp"] = fexp
    ind = np.zeros((128, 4), f32)
    for r in range(128):
        ind[r, r // 32] = 1.0 / 32
    tb["ind"] = ind
    tb["iota"] = np.arange(128, dtype=f32)[:, None].copy()
    tb["c32"] = np.full((32, 1), 1.0 / 32, f32)
    tb["ident"] = np.eye(128, dtype=f32)
    return tb


TABLE_SHAPES = lambda seq: {
    "rotp": [4, 128, seq], "rots": [4, 128, 128],
    "dmt_p": [128, 512], "dmt_s": [128, 512], "cr_p": [1, 512], "cr_s": [1, 512],
    "tail_p": [128, 4], "tail_s": [128, 4], "pinv0": [1, 512], "pinv1": [1, 512],
    "oh": [32, 404 * 128], "ohvalid": [128, 404], "mw4": [128, 512], "mw0s": [128, 32],
    "xm": [128, 16], "sm": [128, 16], "smrow": [1, 2048], "rv0": [128, 1],
    "fixk": [128, 2], "fixa": [128, 2], "fexp": [128, max(seq // 128, NPAGE) * 128],
    "ind": [128, 4], "iota": [128, 1], "c32": [32, 1], "ident": [128, 128],
}


def build(seq):
    NTP = seq // 128
    NKB = max(NTP, NPAGE)
    WKEEP = min(512, seq)
    tabs = _host_tables(seq)
    nc = bass.Bass("TRN2", target_bir_lowering=False)
    S = Sched(nc)
    es = ExitStack()

    def din(name, shape, dt=F32):
        return nc.dram_tensor(name, list(shape), dt, kind="ExternalInput").ap()

    def dout(name, shape):
        return nc.dram_tensor(name, list(shape), F32, kind="ExternalOutput").ap()

    def dscr(name, shape, dt):
        return nc.dram_tensor(name, list(shape), dt, kind="Internal").ap()

    def sb(name, shape, dt=F32):
        return Tl(es.enter_context(nc.sbuf_tensor(name, list(shape), dt)))

    def ps(name, shape, dt=F32):
        return Tl(es.enter_context(nc.psum_tensor(name, list(shape), dt)))

    xp = din("xp", [seq, D]); xs = din("xs", [128, D])
    st_ret = din("st_ret", [DEPTH, NS * RH * DK, DV])
    st_pool = din("st_pool", [DEPTH, NS * 15, 512])
    c_win = din("c_win", [DEPTH, NS * 512, 256])
    c_cmp = [din("c_cmp%d" % l, [NPHYS * PAGE, 256]) for l in range(DEPTH)]
    c_sel = [din("c_sel%d" % l, [NPHYS * PAGE, 256]) for l in range(DEPTH)]
    ptab = din("ptab", [1, NS * NPAGE], I32)
    w_all = din("w_all", [DEPTH, D, NCA]); w_b = din("w_b", [DEPTH, 3072, D])
    gains_fm = din("gains_fm", [128, 16]); fgain = din("fgain", [1, D])
    w_pool = din("w_pool", [DEPTH, 512, 128]); pscale_fm = din("pscale_fm", [128, 8])
    pe_cmp = din("pe_cmp", [DEPTH, 32, 256])
    w_ck = din("w_ck", [DEPTH, 64, 64]); w_cv = din("w_cv", [DEPTH, 64, 64])
    rel_bias = din("rel_bias", [32, 8])
    tin = {k: din("t_" + k, shp) for k, shp in TABLE_SHAPES(seq).items()}

    y_p = dout("y_p", [seq, D]); y_s = dout("y_s", [128, D])
    ret_p = dout("ret_p", [DEPTH, RH * DK, DV]); ret_s = dout("ret_s", [DEPTH, NS * RH * DK, DV])
    pool_p = dout("pool_p", [DEPTH, 15, 512]); pool_s = dout("pool_s", [DEPTH, NS * 15, 512])
    win_p = dout("win_p", [DEPTH, WKEEP, 256]); win_s = dout("win_s", [DEPTH, NS * 512, 256])
    cmp_p = dout("cmp_p", [DEPTH, seq, 256]); cmp_s = dout("cmp_s", [DEPTH, 128, 256])
    sel_p = dout("sel_p", [DEPTH, seq, 256]); sel_s = dout("sel_s", [DEPTH, 128, 256])

    y1p = dscr("y1p", [seq, D], F32); y1s = dscr("y1s", [128, D], F32)
    wa_bf = dscr("wa_bf", [DEPTH, NCA // 256, 128, 8, 256], BF16); wb_bf = dscr("wb_bf", [DEPTH, 3, 4, 128, 8, 256], BF16)

    psA = ps("psA", [128, 512]); psB = ps("psB", [128, 512]); psT = ps("psT", [128, 1024], BF16)
    psS = ps("psS", [128, 512]); psM = ps("psM", [128, 512])
    psO = [ps("psO0", [128, 512]), ps("psO1", [128, 512])]; psX = ps("psX", [128, 512])

    def op(e, fn, r=(), w=()):
        S.op(e, fn, r=[t.r for t in r], w=[t.r for t in w])

    def dma(e, out, in_, r=(), w=()):
        S.dma(e, lambda q: q.dma_start(out=out, in_=in_), r=[t.r for t in r], w=[t.r for t in w])

    def cast_dma(dst, src, ncols, r=(), w=()):
        for c0 in range(0, ncols, 2048):
            n = min(2048, ncols - c0)
            dma("pool", dst[:, c0:c0 + n], src[:, c0:c0 + n], r=r, w=w)

    identf = sb("identf", [128, 128]); identb = sb("identb", [128, 128], BF16)
    gains = sb("gains", [128, 16]); pscl = sb("pscl", [128, 8])
    wpool = sb("wpool", [128, DEPTH * 4, 128], BF16)
    pemean = sb("pemean", [128, 4]); W2 = sb("W2", [128, 4, 128], BF16)
    E01 = sb("E01", [128, 4, 512], BF16)
    E15 = sb("E15", [128, 2, 32], BF16)
    ESD = sb("ESD", [128, 2 * 16, 32], BF16)
    BCT = sb("BCT", [128, 8, 8]); BCS = sb("BCS", [128, 8, 4])
    idx = sb("idx", [128, NS * NPAGE], I32)

    def ctab(name, shape, key, bcast=False, dt=F32):
        t = sb(name, shape, dt)
        src = tin[key]
        n = 1
        for x in shape[1:]:
            n *= x
        src2 = src[0:1, :].to_broadcast([128, n]) if bcast else src[:, :]
        dst = t[:, :] if len(shape) == 2 else t.h[:].rearrange("p a b -> p (a b)")
        if dt == F32:
            dma("sp", dst, src2, w=[t])
        else:
            cast_dma(dst, src2, n, w=[t])
        return t

    DMT = {"p": ctab("dmt_p", [128, 4, 128], "dmt_p"), "s": ctab("dmt_s", [128, 4, 128], "dmt_s")}
    CR = {"p": ctab("cr_p", [128, 4, 128], "cr_p", True), "s": ctab("cr_s", [128, 4, 128], "cr_s", True)}
    TAIL = {"p": ctab("tail_p", [128, 4], "tail_p"), "s": ctab("tail_s", [128, 4], "tail_s")}
    PINV0 = ctab("pinv0", [128, 4, 128], "pinv0", True); PINV1 = ctab("pinv1", [128, 4, 128], "pinv1", True)
    MW4 = ctab("mw4", [128, 512], "mw4", dt=BF16); MW0S = ctab("mw0s", [128, 32], "mw0s", dt=BF16)
    XM = ctab("xm", [128, 16], "xm"); SM = ctab("sm", [128, 16], "sm")
    RV0 = ctab("rv0", [128, 1], "rv0"); FIXK = ctab("fixk", [128, 2], "fixk"); FIXA = ctab("fixa", [128, 2], "fixa")
    FEXP = ctab("fexp", [128, NKB * 128], "fexp", dt=BF16)
    IND = ctab("ind", [128, 4], "ind")

    with ExitStack() as es2:
        stg = [Tl(es2.enter_context(nc.sbuf_tensor("stg%d" % i, [128, NCA], BF16))) for i in range(2)]
        i = 0
        for l in range(DEPTH):
            for rb in range(D // 128):
                t = stg[i % 2]; i += 1
                cast_dma(t, w_all[l, rb * 128:(rb + 1) * 128, :], NCA, w=[t])
                dma("sp", wa_bf[l, :, :, rb, :].rearrange("b p c -> p b c"), t[:, :].rearrange("p (b c) -> p b c", c=256), r=[t])
            for rb in range(3):
                t = stg[i % 2]; i += 1
                for a8 in range(8):
                    cast_dma(t[:, a8 * 1024:(a8 + 1) * 1024], w_b[l, rb * 1024 + a8 * 128:rb * 1024 + (a8 + 1) * 128, :], 1024, w=[t])
                for cb in range(4):
                    dma("sp", wb_bf[l, rb, cb], t[:, 0:8192].rearrange("p (a cb c) -> p cb a c", a=8, cb=4)[:, cb, :, :], r=[t])
        S.barrier_all()
    WSCR = Tl(None)

    with ExitStack() as es3:
        def sbt(name, shape, dt=F32):
            return Tl(es3.enter_context(nc.sbuf_tensor(name, list(shape), dt)))
        dma("sp", identf[:, :], tin["ident"][:, :], w=[identf])
        op("dve", lambda e: e.tensor_copy(identb[:, :], identf[:, :]), [identf], [identb])
        dma("sp", gains[:, :], gains_fm[:, :], w=[gains])
        dma("sp", pscl[:, :], pscale_fm[:, :], w=[pscl])
        for l in range(DEPTH):
            dma("pool", wpool[:, l * 4:(l + 1) * 4, :], w_pool[l].rearrange("(g c) e -> c g e", c=128), w=[wpool])
        IOTA = sbt("iota", [128, 1]); dma("sp", IOTA[:, :], tin["iota"][:, :], w=[IOTA])
        OHV = sbt("ohv", [128, 404]); dma("sp", OHV[:, :], tin["ohvalid"][:, :], w=[OHV])
        c32 = sbt("c32", [32, 1]); dma("sp", c32[:, :], tin["c32"][:, :], w=[c32])
        pes = sbt("pes", [32, 256])
        for l in range(DEPTH):
            dma("sp", pes[:, :], pe_cmp[l], w=[pes])
            for kv in range(2):
                op("pe", lambda e: e.matmul(psX[:, 0:1], lhsT=pes[:, kv * 128:(kv + 1) * 128], rhs=c32[:, :], start=True, stop=True), [pes, c32], [psX])
                op("dve", lambda e: e.tensor_copy(pemean[:, l * 2 + kv:l * 2 + kv + 1], psX[:, 0:1]), [psX], [pemean])
        w2f = sbt("w2f", [128, 4, 128])
        op("dve", lambda e: e.memset(w2f.h[:], 0.0), [], [w2f])
        for l in range(DEPTH):
            for kv, wsrc in enumerate((w_ck, w_cv)):
                for g in range(2):
                    dma("sp", w2f[g * 64:(g + 1) * 64, l * 2 + kv, g * 64:(g + 1) * 64], wsrc[l], w=[w2f])
        op("dve", lambda e: e.tensor_copy(W2.h[:], w2f.h[:]), [w2f], [W2])
        rbs = sbt("rbs", [32, 8]); dma("sp", rbs[:, :], rel_bias[:, :], w=[rbs])
        rb31 = sbt("rb31", [128, 8]); dma("sp", rb31[:, :], rel_bias[31:32, :].to_broadcast([128, 8]), w=[rb31])
        TB = sbt("TB", [128, 404, 8])
        ohs = sbt("ohs", [32, 32, 128])
        NOH = 404
        for c0 in range(0, NOH, 32):
            n = min(32, NOH - c0)
            dma("sp", ohs.h[:].rearrange("b c k -> b (c k)")[:, 0:n * 128], tin["oh"][:, c0 * 128:(c0 + n) * 128], w=[ohs])
            for j in range(n):
                op("pe", lambda e: e.matmul(psA[:, j * 8:(j + 1) * 8], lhsT=ohs[:, j, :], rhs=rbs[:, :], start=True, stop=True), [ohs, rbs], [psA])
            op("dve", lambda e: e.tensor_copy(TB.h[:].rearrange("p c h -> p (c h)")[:, c0 * 8:(c0 + n) * 8], psA[:, 0:n * 8]), [psA], [TB])
        op("dve", lambda e: e.tensor_tensor(out=TB.h[:], in0=TB.h[:], in1=rb31[:, :].unsqueeze(1).to_broadcast([128, 404, 8]), op=ALU.subtract), [TB, rb31], [TB])
        op("dve", lambda e: e.tensor_tensor(out=TB.h[:], in0=TB.h[:], in1=OHV[:, :].unsqueeze(2).to_broadcast([128, 404, 8]), op=ALU.mult), [TB, OHV], [TB])
        TE = sbt("TE", [128, 392, 8])
        op("act", lambda e: e.activation(out=TE.h[:], in_=TB[:, 0:392, :], func=AF.Exp), [TB], [TE])
        op("dve", lambda e: e.tensor_tensor(out=TE.h[:], in0=TE.h[:], in1=OHV[:, 0:392].unsqueeze(2).to_broadcast([128, 392, 8]), op=ALU.mult), [TE, OHV], [TE])
        for g in range(2):
            for r in range(4):
                h = 4 * g + r
                for dl in range(2):
                    op("dve", lambda e: e.tensor_copy(E01[:, g * 2 + dl, r * 128:(r + 1) * 128], TE[:, dl * 128:(dl + 1) * 128, h]), [TE], [E01])
                op("dve", lambda e: e.tensor_copy(E15[:, g, r * 8:(r + 1) * 8], TE[:, 256:264, h]), [TE], [E15])
                op("dve", lambda e: e.tensor_copy(ESD[:, g * 16:(g + 1) * 16, r * 8:(r + 1) * 8],
                                                  TE[:, 264:392, h].rearrange("p (s t) -> p s t", t=8)), [TE], [ESD])
        negv = sbt("negv", [128, 12])
        op("dve", lambda e: e.tensor_scalar(out=negv[:, :], in0=OHV[:, 392:404], scalar1=-1.0, scalar2=-NEG, op0=ALU.add, op1=ALU.mult), [OHV], [negv])
        op("dve", lambda e: e.tensor_tensor(out=BCT.h[:], in0=TB[:, 392:400, :].rearrange("p c h -> p h c"),
                                            in1=negv[:, 0:8].unsqueeze(1).to_broadcast([128, 8, 8]), op=ALU.add), [TB, negv], [BCT])
        op("dve", lambda e: e.tensor_tensor(out=BCS.h[:], in0=TB[:, 400:404, :].rearrange("p c h -> p h c"),
                                            in1=negv[:, 8:12].unsqueeze(1).to_broadcast([128, 8, 4]), op=ALU.add), [TB, negv], [BCS])
        pti = sbt("pti", [128, NS * NPAGE], I32); ptf = sbt("ptf", [128, NS * NPAGE])
        dma("sp", pti[:, :], ptab[0:1, :].to_broadcast([128, NS * NPAGE]), w=[pti])
        op("dve", lambda e: e.tensor_copy(ptf[:, :], pti[:, :]), [pti], [ptf])
        op("dve", lambda e: e.tensor_scalar(out=ptf[:, :], in0=ptf[:, :], scalar1=128.0, scalar2=IOTA[:, 0:1], op0=ALU.mult, op1=ALU.add), [ptf, IOTA], [ptf])
        op("dve", lambda e: e.tensor_copy(idx[:, :], ptf[:, :]), [ptf], [idx])
        S.barrier_all()

    WB = [sb("wbuf%d" % i, [128, 8, 256], BF16) for i in range(4)]
    wbi = [0]

    class WBlk:
        def __init__(self, halves):
            self.hv = halves

        def col(self, kc, a, b):
            hh = a // 256
            assert (b - 1) // 256 == hh
            t = self.hv[hh]
            return t[:, kc, a - hh * 256:b - hh * 256], t

    def wload(src, c0, w):
        hv = []
        for hh in range(2):
            t = WB[wbi[0] % 4]; wbi[0] += 1
            hv.append(t)
            a0, a1 = hh * 256, min(w, (hh + 1) * 256)
            if a1 > a0:
                dma("sp", t[:, :, 0:a1 - a0], src[c0 // 256 + hh, :, :, 0:a1 - a0], r=[WSCR], w=[t])
        return WBlk(hv)

    xt = sb("xt", [128, D])
    st1 = sb("st1", [128, 8]); xnT = sb("xnT", [128, 8, 128], BF16)
    ROT = sb("ROT", [128, 4, 128])
    t1 = sb("t1", [128, 128]); t2 = sb("t2", [128, 128])
    qrT = sb("qrT", [128, 4, 128], BF16); qcT = sb("qcT", [128, 4, 128], BF16); krT = sb("krT", [128, 4, 128], BF16)
    ktail = sb("ktail", [128, 512], BF16); vtok = sb("vtok", [128, 1024], BF16); rgs = sb("rgs", [128, 1024])
    pext = sb("pext", [128, 4, 143])
    pw1 = sb("pw1", [128, 16 * 23]); pw2 = sb("pw2", [128, 16 * 23])
    pgsT = sb("pgsT", [128, 4, 128], BF16); nqT = sb("nqT", [128, 4, 128], BF16)
    cms = sb("cms", [128, 2, 4]); cmb = sb("cmb", [128, 2, 4], BF16)
    mgs = sb("mgs", [128, 24, 128], BF16); ngs = sb("ngs", [128, 512]); kvrow = sb("kvrow", [128, 768])
    nbgs = sb("nbgs", [128, 24])
    attT = sb("attT", [128, 128], BF16); ga = sb("ga", [128, 1024], BF16); xn = Vw(ga, ga.h[:]); gaT = sb("gaT", [128, 8, 128], BF16); junk = Vw(gaT, gaT.h[:].rearrange("p a b -> p (a b)"))
    gbT = sb("gbT", [128, 4, 128], BF16); gc = sb("gc", [128, 512], BF16); gcT = sb("gcT", [128, 4, 128], BF16)
    dT = sb("dT", [128, 128], BF16); oc = sb("oc", [128, 512]); mT = sb("mT", [128, 8, 128], BF16)
    putok = Vw(oc, oc.h[:])
    m0 = Vw(t1, t1.h[:]); m1 = Vw(t2, t2.h[:])
    Sst = sb("Sst", [128, 4, 256]); Sbf = sb("Sbf", [128, 4, 256], BF16)
    selKT = sb("selKT", [128, NTP * 128], BF16); selV = sb("selV", [128, NTP, 130], BF16)
    winKT = sb("winKT", [128, 8, 128], BF16); winV = sb("winV", [128, 8, 130], BF16)
    kcT = sb("kcT", [128, 1024], BF16); vcT = sb("vcT", [128, 1024], BF16); vctok = sb("vctok", [128, 8, 128], BF16)
    sc = sb("sc", [128, 1024]); pb = sb("pb", [128, 1024], BF16); pT = sb("pT", [128, 8, 128], BF16)
    scrB = sb("scrB", [128, 1024]); impf = Vw(scrB, scrB.h[:, 0:512]); imt = Vw(scrB, scrB.h[:, 512:1024]); impg = sb("impg", [128, 2, 128]); imp2 = sb("imp2", [128, 128])
    scH = [Sub(sc.h[:, i * 512:(i + 1) * 512]) for i in range(2)]; pbH = [Sub(pb.h[:, i * 512:(i + 1) * 512]) for i in range(2)]
    pTH = [Sub(pT.h[:, i * 4:(i + 1) * 4, :]) for i in range(2)]; m8H = [sb("m8h%d" % i, [128, 4]) for i in range(2)]
    m8 = sb("m8", [128, 16]); selm = sb("selm", [128, 128], BF16); selT = [sb("selT0", [128, 128], BF16), sb("selT1", [128, 128], BF16)]
    pexs_ = [sb("pex%d" % i, [128, 512], BF16) for i in range(3)]; pmks_ = [sb("pmk%d" % i, [128, 512], BF16) for i in range(3)]; slot_i = [0]; oTs = sb("oTs", [65, 512]); wgt = sb("wgt", [128, 8])
    raw = [sb("raw0", [128, 8, 256]), sb("raw1", [128, 8, 256])]
    pexs = Vw(raw[1], raw[1].h[:].rearrange("p a b -> p (a b)")[:, 0:4 * 368].rearrange("p (g c) -> p g c", g=4))
    kpg = sb("kpg", [128, 8, 128], BF16); KTs = sb("KTs", [128, 16, 128], BF16); Vs = sb("Vs", [128, 16, 130], BF16)
    ownKs = sb("ownKs", [128, 128], BF16); ownKw = sb("ownKw", [128, 128], BF16)
    ownVs = sb("ownVs", [128, 130], BF16); ownVw = sb("ownVw", [128, 130], BF16)
    cmbs = Vw(KTs, KTs.h[:].rearrange("p a b -> p (a b)").rearrange("p (k n) -> p k n", k=2))
    qcm = Vw(sc, sc.h[:].bitcast(BF16).rearrange("p (s n) -> p s n", s=16)); ktm = Vw(scrB, scrB.h[:].bitcast(BF16).rearrange("p (s n) -> p s n", s=16))
    s0f = [sb("s0f%d" % i, [128, 256]) for i in range(2)]; s0b = [sb("s0b%d" % i, [128, 256], BF16) for i in range(2)]
    pbst = Vw(sc, sc.h[0:120, :].rearrange("p (a c) -> p a c", a=2))
    for t in (selV, winV, Vs):
        op("pool", lambda e: e.memset(t.h[:], 1.0), [], [t])
    for t in (ownVs, ownVw):
        op("pool", lambda e: e.memset(t[:, :], 1.0), [], [t])
    op("pool", lambda e: e.memset(pb[:, :], 0.0), [], [pb])

    def rmsnorm_rstd(src, col):
        op("act", lambda e: e.activation(out=junk[:, :], in_=src[:, :], func=AF.Square, accum_out=st1[:, col:col + 1]), [src], [junk, st1])
        op("act", lambda e: e.activation(out=st1[:, col + 1:col + 2], in_=st1[:, col:col + 1], func=AF.Sqrt, scale=1.0 / D, bias=EPS), [st1], [st1])
        op("dve", lambda e: e.reciprocal(st1[:, col + 2:col + 3], st1[:, col + 1:col + 2]), [st1], [st1])
        return st1[:, col + 2:col + 3]

    def transpose_to(dst_ap, dst_t, src_ap, src_t, eng="act"):
        op("pe", lambda e: e.transpose(psT[:, 0:128], src_ap, identb[:, :]), [src_t, identb], [psT])
        if eng == "act":
            op("act", lambda e: e.copy(dst_ap, psT[:, 0:128]), [psT], [dst_t])
        else:
            op("dve", lambda e: e.tensor_copy(dst_ap, psT[:, 0:128]), [psT], [dst_t])

    def finalize_branch(gate_col0, first):
        flush_slots()
        for g in range(2):
            op("act", lambda e: e.copy(oTs[0:65, :], psO[g][0:65, :]), [psO[g]], [oTs])
            for r in range(4):
                op("pe", lambda e: e.matmul(psX[:, r * 65:(r + 1) * 65], lhsT=oTs[0:65, r * 128:(r + 1) * 128], rhs=identf[0:65, 0:65], start=True, stop=True), [oTs, identf], [psX])
            den = psX[:, 0:260].rearrange("p (r c) -> p r c", c=65)[:, :, 64]
            op("dve", lambda e: e.tensor_scalar(out=wgt[:, 0:4], in0=den, scalar1=1e-30, scalar2=None, op0=ALU.max), [psX], [wgt])
            op("dve", lambda e: e.reciprocal(wgt[:, 4:8], wgt[:, 0:4]), [wgt], [wgt])
            op("dve", lambda e: e.tensor_tensor(out=wgt[:, 0:4], in0=wgt[:, 4:8], in1=nbgs[:, gate_col0 + 4 * g:gate_col0 + 4 * g + 4], op=ALU.mult), [wgt, nbgs], [wgt])
            for r in range(4):
                h = 4 * g + r
                if first:
                    op("dve", lambda e: e.tensor_scalar(out=oc[:, h * 64:(h + 1) * 64], in0=psX[:, r * 65:r * 65 + 64], scalar1=wgt[:, r:r + 1], scalar2=None, op0=ALU.mult), [psX, wgt], [oc])
                else:
                    op("dve", lambda e: e.scalar_tensor_tensor(out=oc[:, h * 64:(h + 1) * 64], in0=psX[:, r * 65:r * 65 + 64], scalar=wgt[:, r:r + 1], in1=oc[:, h * 64:(h + 1) * 64], op0=ALU.mult, op1=ALU.add), [psX, wgt, oc], [oc])

    def attn_slot(g, kt_ap, kt_t, q_ap, ncol, v_ap, v_t, masks, col0, start, stop, selexp=None):
        i = slot_i[0]; slot_i[0] += 1
        pss, psm = (psS, psM) if i % 2 == 0 else (psA, psB)
        pex, pmk = pexs_[i % 3], pmks_[i % 3]
        op("pe", lambda e: e.matmul(pss[:, 0:ncol].rearrange("p (r q) -> p r q", r=4), lhsT=kt_ap, rhs=q_ap, start=True, stop=True), [kt_t, nqT], [pss])
        if selexp is not None:
            f_ap, st_ap, st_t, nq = selexp
            op("pe", lambda e: e.matmul(psm[:, 0:nq], lhsT=f_ap, rhs=st_ap, start=True, stop=True), [FEXP, st_t], [psm])
        op("act", lambda e: e.activation(out=pex[:, 0:ncol], in_=pss[:, 0:ncol], func=AF.Exp), [pss], [pex])
        cur = pex
        if selexp is not None:
            op("dve", lambda e: e.tensor_tensor(out=pmk[:, 0:ncol].rearrange("p (r q) -> p r q", r=4), in0=pex[:, 0:ncol].rearrange("p (r q) -> p r q", r=4),
                                                in1=psm[:, 0:nq].unsqueeze(1).to_broadcast([128, 4, nq]), op=ALU.mult), [pex, psm], [pmk])
            cur = pmk
        for m_ap, m_t in masks:
            op("dve", lambda e: e.tensor_tensor(out=pmk[:, 0:ncol], in0=cur[:, 0:ncol], in1=m_ap, op=ALU.mult), [cur, m_t], [pmk])
            cur = pmk
        pend.append((g, col0, ncol, v_ap, v_t, cur, start, stop))
        flush_slots(keep=2)

    pend = []

    def flush_slots(keep=0):
        while len(pend) > keep:
            g, col0, ncol, v_ap, v_t, cur, start, stop = pend.pop(0)
            op("pe", lambda e: e.matmul(psO[g][0:65, col0:col0 + ncol], lhsT=v_ap, rhs=cur[:, 0:ncol], start=start, stop=stop), [v_t, cur], [psO[g]])

    def unit(l, kind, T):
        prm = (kind == "p")
        last_p = prm and T == NTP - 1
        pos0 = T * 128
        wa = wa_bf[l]; wb = wb_bf[l]
        if l == 0:
            src = xp[pos0:pos0 + 128, :] if prm else xs[:, :]
        else:
            src = y1p[pos0:pos0 + 128, :] if prm else y1s[:, :]
        dma("sp", xt[:, :], src, w=[xt])
        dma("sp", ROT.h[:], (tin["rotp"][:, :, pos0:pos0 + 128] if prm else tin["rots"][:, :, :]).rearrange("t d n -> d t n"), w=[ROT])
        rstd = rmsnorm_rstd(xt, 0)
        op("dve", lambda e: e.tensor_scalar(out=xn[:, :], in0=xt[:, :], scalar1=rstd, scalar2=None, op0=ALU.mult), [xt, st1], [xn])
        for c in range(8):
            op("pe", lambda e: e.transpose(psT[:, c * 128:(c + 1) * 128], xn[:, c * 128:(c + 1) * 128], identb[:, :]), [xn, identb], [psT])
        op("dve", lambda e: e.tensor_tensor(out=xnT.h[:], in0=psT[:, :].rearrange("p (c n) -> p c n", c=8),
                                            in1=gains[:, l * 8:(l + 1) * 8].unsqueeze(2).to_broadcast([128, 8, 128]), op=ALU.mult), [psT, gains], [xnT])
        kk = "p" if prm else "s"
        pp = [psA, psB]; pi = [0]
        for blk in range(NFM // 4):
            wt = wload(wa, blk * 512, 512)
            for j in range(4):
                m = blk * 4 + j
                if (m in (28, 29)) and not prm:
                    continue
                pair = m < 16
                if not (pair and m % 2 == 1):
                    pst = pp[pi[0] % 2]; pi[0] += 1
                off = 128 if (pair and m % 2 == 1) else 0
                for kc in range(8):
                    wap, wtl = wt.col(kc, j * 128, (j + 1) * 128)
                    op("pe", lambda e: e.matmul(pst[:, off:off + 128], lhsT=wap, rhs=xnT[:, kc, :], start=(kc == 0), stop=(kc == 7)), [wtl, xnT], [pst])
                if pair:
                    if m % 2 == 0:
                        continue
                    h = (m % 8) // 2
                    isq = m < 8
                    ta, tbb = (0, 1) if isq else (2, 3)
                    op("dve", lambda e: e.tensor_tensor(out=t1[:, :], in0=pst[:, 0:128], in1=ROT[:, ta, :], op=ALU.mult), [pst, ROT], [t1])
                    op("dve", lambda e: e.tensor_tensor(out=t2[:, :], in0=pst[:, 128:256], in1=ROT[:, tbb, :], op=ALU.mult), [pst, ROT], [t2])
                    if isq:
                        op("pool", lambda e: e.tensor_tensor(out=t1[:, :], in0=t1[:, :], in1=t2[:, :], op=ALU.add), [t1, t2], [t1])
                        op("act", lambda e: e.copy(qrT[:, h, :], t1[:, :]), [t1], [qrT])
                        op("dve", lambda e: e.tensor_tensor(out=qcT[:, h, :], in0=t1[:, :], in1=CR[kk][:, h, :], op=ALU.mult), [t1, CR[kk]], [qcT])
                    else:
                        op("pool", lambda e: e.tensor_tensor(out=krT[:, h, :], in0=t1[:, :], in1=t2[:, :], op=ALU.add), [t1, t2], [krT])
                elif m < 20:
                    g = m - 16
                    if prm:
                        op("act", lambda e: e.copy(pext[:, g, 15:143], pst[:, 0:128]), [pst], [pext])
                    else:
                        op("act", lambda e: e.copy(pexs[:, g, :].rearrange("p (s c) -> p s c", c=23)[:, :, 15:23], pst[:, 0:128].rearrange("p (s t) -> p s t", t=8)), [pst], [pexs])
                elif m < 24:
                    op("act", lambda e: e.activation(out=pgsT[:, m - 20, :], in_=pst[:, 0:128], func=AF.Silu), [pst], [pgsT])
                elif m < 28:
                    op("act", lambda e: e.activation(out=nqT[:, m - 24, :], in_=pst[:, 0:128], func=AF.Copy, scale=0.125), [pst], [nqT])
                elif m < 30:
                    op("dve", lambda e: e.tensor_reduce(out=cms[:, m - 28, :], in_=pst[:, 0:128].rearrange("p (b t) -> p b t", t=32), axis=AX.X, op=ALU.add), [pst], [cms])
                elif m == 30:
                    if prm:
                        op("act", lambda e: e.copy(selKT[:, pos0:pos0 + 128], pst[:, 0:128]), [pst], [selKT])
                    else:
                        op("act", lambda e: e.copy(ownKs[:, :], pst[:, 0:128]), [pst], [ownKs])
                elif m == 31:
                    if prm:
                        op("act", lambda e: e.copy(winKT[:, T % 8, :], pst[:, 0:128]), [pst], [winKT])
                    else:
                        op("act", lambda e: e.copy(ownKw[:, :], pst[:, 0:128]), [pst], [ownKw])
                else:
                    op("act", lambda e: e.activation(out=mgs[:, m - 32, :], in_=pst[:, 0:128], func=AF.Sigmoid), [pst], [mgs])
        need_put = (not prm) or last_p
        tmb = [(TM_RV, 512, "rv0"), (TM_RV + 512, 512, "rv1"), (TM_RG, 512, "rg0"), (TM_RG + 512, 512, "rg1"),
               (TM_NG, 512, "ng"), (TM_KVR, 512, "kv0"), (TM_KVR + 512, 280, "kv1")]
        if need_put:
            tmb.append((TM_PUT, 512, "put"))
        for c0, w, nm in tmb:
            wt = wload(wa, c0, w)
            pst = pp[pi[0] % 2]; pi[0] += 1
            for a0 in range(0, w, 256):
                a1 = min(w, a0 + 256)
                for kc in range(8):
                    wap, wtl = wt.col(kc, a0, a1)
                    op("pe", lambda e: e.matmul(pst[:, a0:a1], lhsT=xnT[:, kc, :], rhs=wap, start=(kc == 0), stop=(kc == 7)), [wtl, xnT], [pst])
            if nm[:2] == "rv":
                o = 512 * int(nm[2])
                op("act", lambda e: e.copy(vtok[:, o:o + 512], pst[:, :]), [pst], [vtok])
            elif nm[:2] == "rg":
                o = 512 * int(nm[2])
                op("act", lambda e: e.activation(out=rgs[:, o:o + 512], in_=pst[:, :], func=AF.Silu), [pst], [rgs])
            elif nm == "ng":
                op("act", lambda e: e.activation(out=ngs[:, :], in_=pst[:, :], func=AF.Silu), [pst], [ngs])
            elif nm == "kv0":
                op("dve", lambda e: e.tensor_copy(kvrow[:, 0:512], pst[:, :]), [pst], [kvrow])
            elif nm == "kv1":
                op("dve", lambda e: e.tensor_copy(kvrow[:, 512:768], pst[:, 0:256]), [pst], [kvrow])
                op("act", lambda e: e.activation(out=nbgs[:, :], in_=pst[:, 256:280], func=AF.Sigmoid), [pst], [nbgs])
            else:
                op("act", lambda e: e.copy(putok[:, :], pst[:, :]), [pst], [putok])
        if prm:
            dma("pool", cmp_p[l, pos0:pos0 + 128, :], kvrow[:, 0:256], r=[kvrow])
            dma("pool", sel_p[l, pos0:pos0 + 128, :], kvrow[:, 256:512], r=[kvrow])
            if pos0 + 128 > seq - WKEEP:
                dma("pool", win_p[l, pos0 - (seq - WKEEP):pos0 - (seq - WKEEP) + 128, :], kvrow[:, 512:768], r=[kvrow])
            op("pool", lambda e: e.tensor_copy(selV[:, T, :].rearrange("p (g c) -> p g c", c=65)[:, :, 0:64], kvrow[:, 384:512].rearrange("p (g c) -> p g c", c=64)), [kvrow], [selV])
            op("pool", lambda e: e.tensor_copy(winV[:, T % 8, :].rearrange("p (g c) -> p g c", c=65)[:, :, 0:64], kvrow[:, 640:768].rearrange("p (g c) -> p g c", c=64)), [kvrow], [winV])
        else:
            dma("pool", cmp_s[l, :, :], kvrow[:, 0:256], r=[kvrow])
            dma("pool", sel_s[l, :, :], kvrow[:, 256:512], r=[kvrow])
            for s_ in range(NS):
                dma("pool", win_s[l, s_ * 512 + 504:s_ * 512 + 512, :], kvrow[s_ * 8:(s_ + 1) * 8, 512:768], r=[kvrow])
            dma("pool", win_s[l].rearrange("(s r) c -> s r c", r=512)[:, 0:504, :], c_win[l].rearrange("(s r) c -> s r c", r=512)[:, 8:512, :])
            op("pool", lambda e: e.tensor_copy(ownVs[:, :].rearrange("p (g c) -> p g c", c=65)[:, :, 0:64], kvrow[:, 384:512].rearrange("p (g c) -> p g c", c=64)), [kvrow], [ownVs])
            op("pool", lambda e: e.tensor_copy(ownVw[:, :].rearrange("p (g c) -> p g c", c=65)[:, :, 0:64], kvrow[:, 640:768].rearrange("p (g c) -> p g c", c=64)), [kvrow], [ownVw])
            for s_ in range(NS):
                dma("pool", pool_s[l, s_ * 15 + 7:s_ * 15 + 15, :], putok[s_ * 8:(s_ + 1) * 8, :], r=[putok])
            dma("pool", pool_s[l].rearrange("(s r) c -> s r c", r=15)[:, 0:7, :], st_pool[l].rearrange("(s r) c -> s r c", r=15)[:, 8:15, :])
        if last_p:
            dma("pool", pool_p[l, :, :], putok[113:128, :], r=[putok])

        for h in range(4):
            op("pe", lambda e: e.transpose(psT[:, h * 128:(h + 1) * 128], krT[:, h, :], identb[:, :]), [krT, identb], [psT])
        for h in range(4):
            op("dve", lambda e: e.tensor_scalar(out=ktail[:, h * 128:(h + 1) * 128], in0=psT[:, h * 128:(h + 1) * 128], scalar1=TAIL[kk][:, h:h + 1], scalar2=None, op0=ALU.mult), [psT, TAIL[kk]], [ktail])
        gdec = tabs["gdec_p"] if prm else tabs["gdec_s"]
        for h in range(4):
            op("pe", lambda e: e.matmul(psX[:, 0:128], lhsT=krT[:, h, :], rhs=qrT[:, h, :], start=True, stop=True), [krT, qrT], [psX])
            op("dve", lambda e: e.tensor_tensor(out=attT[:, :], in0=psX[:, 0:128], in1=DMT[kk][:, h, :], op=ALU.mult), [psX, DMT[kk]], [attT])
            if prm:
                op("pe", lambda e: e.matmul(psX[:, 128:384], lhsT=attT[:, :], rhs=vtok[:, h * 256:(h + 1) * 256], start=True, stop=False), [attT, vtok], [psX])
                op("pe", lambda e: e.matmul(psX[:, 128:384], lhsT=qcT[:, h, :], rhs=Sbf[:, h, :], start=False, stop=True), [qcT, Sbf], [psX])
                op("pe", lambda e: e.matmul(psM[:, 0:256], lhsT=ktail[:, h * 128:(h + 1) * 128], rhs=vtok[:, h * 256:(h + 1) * 256], start=True, stop=True), [ktail, vtok], [psM])
                op("dve", lambda e: e.scalar_tensor_tensor(out=Sst[:, h, :], in0=Sst[:, h, :], scalar=gdec[h], in1=psM[:, 0:256], op0=ALU.mult, op1=ALU.add), [Sst, psM], [Sst])
                op("act", lambda e: e.copy(Sbf[:, h, :], Sst[:, h, :]), [Sst], [Sbf])
            else:
                if h == 0:
                    op("pool", lambda e: e.memset(qcm.h[:], 0.0), [], [qcm])
                for s in range(NS):
                    op("pool", lambda e: e.tensor_copy(qcm[:, s, s * 8:(s + 1) * 8], qcT[:, h, s * 8:(s + 1) * 8]), [qcT], [qcm])
                op("pool", lambda e: e.tensor_tensor(out=ktm.h[:], in0=ktail[:, h * 128:(h + 1) * 128].unsqueeze(1).to_broadcast([128, 16, 128]),
                                                     in1=SM[:, :].unsqueeze(2).to_broadcast([128, 16, 128]), op=ALU.mult), [ktail, SM], [ktm])
                op("pe", lambda e: e.matmul(psX[:, 128:384], lhsT=attT[:, :], rhs=vtok[:, h * 256:(h + 1) * 256], start=True, stop=False), [attT, vtok], [psX])
                for s in range(NS):
                    b = (h * NS + s) % 2
                    row0 = (s * RH + h) * DK
                    dma("sp", s0f[b][:, :], st_ret[l, row0:row0 + DK, :], w=[s0f[b]])
                    op("act", lambda e: e.copy(s0b[b][:, :], s0f[b][:, :]), [s0f[b]], [s0b[b]])
                    op("pe", lambda e: e.matmul(psX[:, 128:384], lhsT=qcm[:, s, :], rhs=s0b[b][:, :], start=False, stop=(s == NS - 1)), [qcm, s0b[b]], [psX])
                    op("pe", lambda e: e.matmul(psM[:, 0:256], lhsT=ktm[:, s, :], rhs=vtok[:, h * 256:(h + 1) * 256], start=True, stop=True), [ktm, vtok], [psM])
                    op("dve", lambda e: e.scalar_tensor_tensor(out=s0f[b][:, :], in0=s0f[b][:, :], scalar=gdec[h], in1=psM[:, 0:256], op0=ALU.mult, op1=ALU.add), [s0f[b], psM], [s0f[b]])
                    dma("pool", ret_s[l, row0:row0 + DK, :], s0f[b][:, :], r=[s0f[b]])
            op("act", lambda e: e.activation(out=junk[:, 0:256], in_=psX[:, 128:384], func=AF.Square, accum_out=st1[:, 4:5]), [psX], [junk, st1])
            op("act", lambda e: e.activation(out=st1[:, 5:6], in_=st1[:, 4:5], func=AF.Sqrt, scale=1.0 / DV, bias=EPS), [st1], [st1])
            op("dve", lambda e: e.reciprocal(st1[:, 6:7], st1[:, 5:6]), [st1], [st1])
            op("dve", lambda e: e.scalar_tensor_tensor(out=ga[:, h * 256:(h + 1) * 256], in0=psX[:, 128:384], scalar=st1[:, 6:7], in1=rgs[:, h * 256:(h + 1) * 256], op0=ALU.mult, op1=ALU.mult), [psX, st1, rgs], [ga])
        for c in range(8):
            transpose_to(gaT[:, c, :], gaT, ga[:, c * 128:(c + 1) * 128], ga)
        if last_p:
            dma("pool", ret_p[l].rearrange("(h d) e -> d h e", d=128), Sst.h[:], r=[Sst])

        if not prm:
            dma("sp", pbst.h[:], st_pool[l].rearrange("(a p) c -> p a c", p=120), w=[pbst])
            for a in range(2):
                for g in range(4):
                    op("pe", lambda e: e.matmul(psX[:, 0:120], lhsT=pbst[:, a, g * 128:(g + 1) * 128], rhs=identf[0:120, 0:120], start=True, stop=True), [pbst, identf], [psX])
                    op("act", lambda e: e.copy(pexs[:, g, :].rearrange("p (s c) -> p s c", c=23)[:, a * 8:(a + 1) * 8, 0:15], psX[:, 0:120].rearrange("p (s b) -> p s b", b=15)), [psX], [pexs])
        nseq, L = (1, 143) if prm else (16, 23)
        ntk = L - 15
        for g in range(4):
            w = 2 << g
            base = pext[:, g, :] if prm else pexs[:, g, :]
            bt = pext if prm else pexs
            cur3 = base.rearrange("p (s c) -> p s c", c=L); curt = bt
            bufs = [pw1, pw2]; bi = 0; m = 1
            while m < w:
                nt_ = bufs[bi % 2]; bi += 1
                n3 = nt_[:, 0:nseq * L].rearrange("p (s c) -> p s c", c=L)
                op("pool", lambda e: e.tensor_tensor(out=n3[:, :, m:L], in0=cur3[:, :, m:L], in1=cur3[:, :, 0:L - m], op=ALU.add), [curt], [nt_])
                cur3, curt = n3, nt_
                m *= 2
            pin = (PINV0 if (prm and T == 0) else PINV1)
            pin3 = pin[:, g, :].rearrange("p (s c) -> p s c", c=ntk)
            u3 = base.rearrange("p (s c) -> p s c", c=L)[:, :, 15:L]
            op("dve", lambda e: e.tensor_tensor(out=m0[:, :].rearrange("p (s c) -> p s c", c=ntk), in0=cur3[:, :, 15:L], in1=pin3, op=ALU.mult), [curt, pin], [m0])
            op("dve", lambda e: e.tensor_tensor(out=dT[:, :].rearrange("p (s c) -> p s c", c=ntk), in0=m0[:, :].rearrange("p (s c) -> p s c", c=ntk), in1=u3, op=ALU.subtract), [m0, bt], [dT])
            op("pe", lambda e: e.matmul(psX[:, 0:128], lhsT=wpool[:, l * 4 + g, :], rhs=dT[:, :], start=True, stop=True), [wpool, dT], [psX])
            op("dve", lambda e: e.scalar_tensor_tensor(out=gbT[:, g, :], in0=psX[:, 0:128], scalar=pscl[:, l * 4 + g:l * 4 + g + 1], in1=pgsT[:, g, :], op0=ALU.mult, op1=ALU.mult), [psX, pscl, pgsT], [gbT])
        if prm:
            op("pool", lambda e: e.tensor_copy(pw1[:, 0:60].rearrange("p (g c) -> p g c", c=15), pext[:, :, 128:143]), [pext], [pw1])
            op("pool", lambda e: e.tensor_copy(pext[:, :, 0:15], pw1[:, 0:60].rearrange("p (g c) -> p g c", c=15)), [pw1], [pext])

        if prm:
            for kv in range(2):
                op("dve", lambda e: e.tensor_scalar(out=cmb[:, kv, :], in0=cms[:, kv, :], scalar1=1.0 / 32, scalar2=pemean[:, l * 2 + kv:l * 2 + kv + 1], op0=ALU.mult, op1=ALU.add), [cms, pemean], [cmb])
                op("pe", lambda e: e.matmul(psX[:, kv * 4:kv * 4 + 4], lhsT=W2[:, l * 2 + kv, :], rhs=cmb[:, kv, :], start=True, stop=True), [W2, cmb], [psX])
            op("act", lambda e: e.copy(kcT[:, 4 * T:4 * T + 4], psX[:, 0:4]), [psX], [kcT])
            op("act", lambda e: e.copy(vcT[:, 4 * T:4 * T + 4], psX[:, 4:8]), [psX], [vcT])
            ncb = 4 * T + 4
        else:
            for s in range(NS):
                for hf in range(2):
                    rw = raw[hf]
                    for b8 in range(8):
                        pg = hf * 8 + b8
                        j = s * NPAGE + pg
                        S.dma("pool", lambda q: q.indirect_dma_start(out=rw[:, b8, :], out_offset=None, in_=c_cmp[l][:, :], in_offset=bass.IndirectOffsetOnAxis(ap=idx[:, j:j + 1], axis=0)), r=[idx.r], w=[rw.r])
                    for b8 in range(8):
                        pg = hf * 8 + b8
                        for kv in range(2):
                            op("pe", lambda e: e.matmul(psX[:, kv * 64 + pg * 4:kv * 64 + pg * 4 + 4], lhsT=rw[:, b8, kv * 128:(kv + 1) * 128], rhs=IND[:, :], start=True, stop=True), [rw, IND], [psX])
                for kv in range(2):
                    op("dve", lambda e: e.tensor_scalar(out=cmbs[:, kv, s * 64:(s + 1) * 64], in0=psX[:, kv * 64:(kv + 1) * 64], scalar1=pemean[:, l * 2 + kv:l * 2 + kv + 1], scalar2=None, op0=ALU.add), [psX, pemean], [cmbs])
            for kv, dst in ((0, kcT), (1, vcT)):
                for hf in range(2):
                    op("pe", lambda e: e.matmul(psA[:, :], lhsT=W2[:, l * 2 + kv, :], rhs=cmbs[:, kv, hf * 512:(hf + 1) * 512], start=True, stop=True), [W2, cmbs], [psA])
                    op("act", lambda e: e.copy(dst[:, hf * 512:(hf + 1) * 512], psA[:, :]), [psA], [dst])
            ncb = 1024
        nch = (ncb + 127) // 128
        for ch in range(nch):
            if prm and (ch + 1) * 128 > ncb:
                op("pool", lambda e: e.memset(vcT[:, ncb:(ch + 1) * 128], 0.0), [], [vcT])
            transpose_to(vctok[:, ch, :], vctok, vcT[:, ch * 128:(ch + 1) * 128], vcT)
        J = ncb // 2
        def select_group(g):
            if prm:
                Jn = J
                op("dve", lambda e: e.tensor_copy(impg[:, g, 0:J], impf[:, 0:J]), [impf], [impg])
                op("dve", lambda e: e.tensor_tensor(out=impg[:, g, J - 2:J], in0=impg[:, g, J - 2:J], in1=FIXK[:, :], op=ALU.mult), [impg, FIXK], [impg])
                op("dve", lambda e: e.tensor_tensor(out=impg[:, g, J - 2:J], in0=impg[:, g, J - 2:J], in1=FIXA[:, :], op=ALU.add), [impg, FIXA], [impg])
                op("dve", lambda e: e.memset(impg[:, g, 0:1], 1e4), [], [impg])
            else:
                Jn = 33
                op("dve", lambda e: e.tensor_reduce(out=impg[:, g, 0:32], in_=impf[:, :].rearrange("p (s j) -> p j s", j=32), axis=AX.X, op=ALU.add), [impf], [impg])
                op("dve", lambda e: e.memset(impg[:, g, 32:33], 2e4), [], [impg])
                op("dve", lambda e: e.memset(impg[:, g, 0:1], 1e4), [], [impg])
            op("pool", lambda e: e.memset(selm[:, :], 0.0), [], [selm])
            if Jn <= 16:
                op("dve", lambda e: e.tensor_scalar(out=selm[:, 0:Jn], in0=impg[:, g, 0:Jn], scalar1=-1e29, scalar2=None, op0=ALU.is_gt), [impg], [selm])
            else:
                op("dve", lambda e: e.max(out=m8[:, 0:8], in_=impg[:, g, 0:Jn]), [impg], [m8])
                op("dve", lambda e: e.match_replace(out=imp2[:, 0:Jn], in_to_replace=m8[:, 0:8], in_values=impg[:, g, 0:Jn], imm_value=-1e30), [impg, m8], [imp2])
                op("dve", lambda e: e.max(out=m8[:, 0:8], in_=imp2[:, 0:Jn]), [imp2], [m8])
                op("dve", lambda e: e.tensor_scalar(out=selm[:, 0:Jn], in0=impg[:, g, 0:Jn], scalar1=m8[:, 7:8], scalar2=None, op0=ALU.is_ge), [impg, m8], [selm])
            transpose_to(selT[g][:, :], selT[g], selm[:, :], selm)

        def head_a(h):
            g, r = h // 4, h % 4
            par = h % 2
            scx, pbx, mx = scH[par], pbH[par], m8H[par]
            pst = psS if par == 0 else psA
            op("pe", lambda e: e.matmul(pst[:, 0:ncb], lhsT=nqT[64 * g:64 * g + 64, r, :], rhs=kcT[64 * g:64 * g + 64, 0:ncb], start=True, stop=True), [nqT, kcT], [pst])
            op("act", lambda e: e.copy(scx[:, 0:ncb], pst[:, 0:ncb]), [pst], [scx])
            if T == 0:
                op("dve", lambda e: e.tensor_tensor(out=scx[:, 0:4], in0=scx[:, 0:4], in1=BCT[:, h, 4:8], op=ALU.add), [scx, BCT], [scx])
            else:
                op("dve", lambda e: e.tensor_tensor(out=scx[:, ncb - 8:ncb], in0=scx[:, ncb - 8:ncb], in1=BCT[:, h, :], op=ALU.add), [scx, BCT], [scx])
            op("dve", lambda e: e.tensor_reduce(out=mx[:, 0:1], in_=scx[:, 0:ncb], axis=AX.X, op=ALU.max), [scx], [mx])
            op("dve", lambda e: e.tensor_scalar(out=mx[:, 1:2], in0=mx[:, 0:1], scalar1=-1.0, scalar2=None, op0=ALU.mult), [mx], [mx])
            op("act", lambda e: e.activation(out=scx[:, 0:ncb], in_=scx[:, 0:ncb], func=AF.Exp, bias=mx[:, 1:2], accum_out=mx[:, 2:3]), [scx, mx], [scx, mx])
            op("dve", lambda e: e.reciprocal(mx[:, 3:4], mx[:, 2:3]), [mx], [mx])
            if T == 0:
                op("dve", lambda e: e.tensor_tensor(out=mx[:, 3:4], in0=mx[:, 3:4], in1=RV0[:, :], op=ALU.mult), [mx, RV0], [mx])
            op("dve", lambda e: e.tensor_scalar(out=scx[:, 0:ncb], in0=scx[:, 0:ncb], scalar1=mx[:, 3:4], scalar2=None, op0=ALU.mult), [scx, mx], [scx])
            if ncb % 128:
                op("pool", lambda e: e.memset(pbx[:, ncb:nch * 128], 0.0), [], [pbx])
            op("act", lambda e: e.copy(pbx[:, 0:ncb], scx[:, 0:ncb]), [scx], [pbx])
            ev = scx[:, 0:ncb].rearrange("p (j u) -> p j u", u=2)
            if r == 0:
                op("pool", lambda e: e.tensor_tensor(out=impf[:, 0:J], in0=ev[:, :, 0], in1=ev[:, :, 1], op=ALU.add), [scx], [impf])
            else:
                op("pool", lambda e: e.tensor_tensor(out=imt[:, 0:J], in0=ev[:, :, 0], in1=ev[:, :, 1], op=ALU.add), [scx], [imt])
                op("pool", lambda e: e.tensor_tensor(out=impf[:, 0:J], in0=impf[:, 0:J], in1=imt[:, 0:J], op=ALU.add), [impf, imt], [impf])
            if r == 3:
                select_group(g)

        def head_b(h):
            g = h // 4
            par = h % 2
            pbx, ptx = pbH[par], pTH[par]
            for ch in range(nch):
                op("pe", lambda e: e.transpose(psT[:, par * 512 + ch * 128:par * 512 + (ch + 1) * 128], pbx[:, ch * 128:(ch + 1) * 128], identb[:, :]), [pbx, identb], [psT])
            op("act", lambda e: e.copy(ptx.h.rearrange("p c q -> p (c q)")[:, 0:nch * 128], psT[:, par * 512:par * 512 + nch * 128]), [psT], [ptx])
            for ch in range(nch):
                op("pe", lambda e: e.matmul(psX[:, h * 64:(h + 1) * 64], lhsT=ptx[:, ch, :], rhs=vctok[:, ch, g * 64:(g + 1) * 64], start=(ch == 0), stop=(ch == nch - 1)), [ptx, vctok], [psX])

        if prm:
            assert ncb <= 512
            for h in range(8):
                head_a(h)
                if h >= 1:
                    head_b(h - 1)
            head_b(7)
        else:
            for h in range(8):
                g, r = h // 4, h % 4
                for c0 in range(0, ncb, 512):
                    n = min(512, ncb - c0)
                    pst = psS if c0 == 0 else psM
                    op("pe", lambda e: e.matmul(pst[:, 0:n], lhsT=nqT[64 * g:64 * g + 64, r, :], rhs=kcT[64 * g:64 * g + 64, c0:c0 + n], start=True, stop=True), [nqT, kcT], [pst])
                    op("act", lambda e: e.copy(sc[:, c0:c0 + n], pst[:, 0:n]), [pst], [sc])
                if prm:
                    if T == 0:
                        op("dve", lambda e: e.tensor_tensor(out=sc[:, 0:4], in0=sc[:, 0:4], in1=BCT[:, h, 4:8], op=ALU.add), [sc, BCT], [sc])
                    else:
                        op("dve", lambda e: e.tensor_tensor(out=sc[:, ncb - 8:ncb], in0=sc[:, ncb - 8:ncb], in1=BCT[:, h, :], op=ALU.add), [sc, BCT], [sc])
                else:
                    sc3 = sc[:, :].rearrange("p (s c) -> p s c", c=64)
                    op("dve", lambda e: e.tensor_tensor(out=sc3, in0=sc3, in1=XM[:, :].unsqueeze(2).to_broadcast([128, 16, 64]), op=ALU.add), [sc, XM], [sc])
                    op("dve", lambda e: e.tensor_tensor(out=sc3[:, :, 60:64], in0=sc3[:, :, 60:64], in1=BCS[:, h, :].unsqueeze(1).to_broadcast([128, 16, 4]), op=ALU.add), [sc, BCS], [sc])
                op("dve", lambda e: e.tensor_reduce(out=m8[:, 8:9], in_=sc[:, 0:ncb], axis=AX.X, op=ALU.max), [sc], [m8])
                op("dve", lambda e: e.tensor_scalar(out=m8[:, 9:10], in0=m8[:, 8:9], scalar1=-1.0, scalar2=None, op0=ALU.mult), [m8], [m8])
                op("act", lambda e: e.activation(out=sc[:, 0:ncb], in_=sc[:, 0:ncb], func=AF.Exp, bias=m8[:, 9:10], accum_out=m8[:, 10:11]), [sc, m8], [sc, m8])
                op("dve", lambda e: e.reciprocal(m8[:, 11:12], m8[:, 10:11]), [m8], [m8])
                if prm and T == 0:
                    op("dve", lambda e: e.tensor_tensor(out=m8[:, 11:12], in0=m8[:, 11:12], in1=RV0[:, :], op=ALU.mult), [m8, RV0], [m8])
                op("dve", lambda e: e.tensor_scalar(out=sc[:, 0:ncb], in0=sc[:, 0:ncb], scalar1=m8[:, 11:12], scalar2=None, op0=ALU.mult), [sc, m8], [sc])
                if ncb % 128:
                    op("pool", lambda e: e.memset(pb[:, ncb:nch * 128], 0.0), [], [pb])
                op("act", lambda e: e.copy(pb[:, 0:ncb], sc[:, 0:ncb]), [sc], [pb])
                ev = sc[:, 0:ncb].rearrange("p (j u) -> p j u", u=2)
                if r == 0:
                    op("pool", lambda e: e.tensor_tensor(out=impf[:, 0:J], in0=ev[:, :, 0], in1=ev[:, :, 1], op=ALU.add), [sc], [impf])
                else:
                    op("pool", lambda e: e.tensor_tensor(out=imt[:, 0:J], in0=ev[:, :, 0], in1=ev[:, :, 1], op=ALU.add), [sc], [imt])
                    op("pool", lambda e: e.tensor_tensor(out=impf[:, 0:J], in0=impf[:, 0:J], in1=imt[:, 0:J], op=ALU.add), [impf, imt], [impf])
                for ch in range(nch):
                    op("pe", lambda e: e.transpose(psT[:, ch * 128:(ch + 1) * 128], pb[:, ch * 128:(ch + 1) * 128], identb[:, :]), [pb, identb], [psT])
                op("act", lambda e: e.copy(pT.h[:].rearrange("p c q -> p (c q)")[:, 0:nch * 128], psT[:, 0:nch * 128]), [psT], [pT])
                for ch in range(nch):
                    op("pe", lambda e: e.matmul(psX[:, h * 64:(h + 1) * 64], lhsT=pT[:, ch, :], rhs=vctok[:, ch, g * 64:(g + 1) * 64], start=(ch == 0), stop=(ch == nch - 1)), [pT, vctok], [psX])
                if r == 3:
                    select_group(g)
        op("dve", lambda e: e.tensor_tensor(out=oc[:, :].rearrange("p (h c) -> p h c", c=64), in0=psX[:, :].rearrange("p (h c) -> p h c", c=64),
                                            in1=nbgs[:, 0:8].unsqueeze(2).to_broadcast([128, 8, 64]), op=ALU.mult), [psX, nbgs], [oc])
        if prm:
            dls = [d for d in range(4, -1, -1) if T - d >= 0]
            for g in range(2):
                qap = nqT[64 * g:64 * g + 64, :, :]
                for i, dl in enumerate(dls):
                    kb = (T - dl) % 8
                    masks = []
                    if dl <= 1:
                        masks = [(E01[:, g * 2 + dl, :], E01)]
                    elif dl == 4:
                        masks = [(MW4[:, :], MW4)]
                    attn_slot(g, winKT[64 * g:64 * g + 64, kb, :], winKT, qap, 512, winV[:, kb, g * 65:(g + 1) * 65], winV, masks, 0, i == 0, i == len(dls) - 1)
            finalize_branch(16, False)
            for g in range(2):
                qap = nqT[64 * g:64 * g + 64, :, :]
                for KB in range(T + 1):
                    dl = T - KB
                    masks = [(E01[:, g * 2 + dl, :], E01)] if dl <= 1 else []
                    attn_slot(g, selKT[64 * g:64 * g + 64, KB * 128:(KB + 1) * 128], selKT, qap, 512, selV[:, KB, g * 65:(g + 1) * 65], selV, masks, 0, KB == 0, KB == T,
                              selexp=(FEXP[:, KB * 128:(KB + 1) * 128], selT[g][:, :], selT[g], 128))
            finalize_branch(8, False)
        else:
            for branch in ("win", "sel"):
                for s in range(NS):
                    nb = 4 if branch == "win" else NPAGE
                    flush_slots()
                    for b0 in range(0, nb, 8):
                        n8 = min(8, nb - b0)
                        rw = raw[(b0 // 8) % 2]
                        if branch == "win":
                            dma("sp", rw[:, 0:4, :], c_win[l, s * 512:(s + 1) * 512, :].rearrange("(b p) c -> p b c", p=128), w=[rw])
                        else:
                            for b in range(n8):
                                j = s * NPAGE + b0 + b
                                S.dma("pool", lambda q: q.indirect_dma_start(out=rw[:, b, :], out_offset=None, in_=c_sel[l][:, :], in_offset=bass.IndirectOffsetOnAxis(ap=idx[:, j:j + 1], axis=0)), r=[idx.r], w=[rw.r])
                        op("dve", lambda e: e.tensor_copy(kpg[:, 0:n8, :], rw[:, 0:n8, 0:128]), [rw], [kpg])
                        op("pool", lambda e: e.tensor_copy(Vs[:, b0:b0 + n8, :].rearrange("p b (g c) -> p b g c", c=65)[:, :, :, 0:64], rw[:, 0:n8, 128:256].rearrange("p b (g c) -> p b g c", c=64)), [rw], [Vs])
                        for b in range(n8):
                            op("pe", lambda e: e.transpose(psT[:, b * 128:(b + 1) * 128], kpg[:, b, :], identb[:, :]), [kpg, identb], [psT])
                        op("act", lambda e: e.copy(KTs.h[:].rearrange("p b k -> p (b k)")[:, b0 * 128:(b0 + n8) * 128], psT[:, 0:n8 * 128]), [psT], [KTs])
                    for g in range(2):
                        qap = nqT[64 * g:64 * g + 64, :, s * 8:(s + 1) * 8]
                        col0 = s * 32
                        for b in range(nb):
                            masks = []
                            sx = None
                            if branch == "win":
                                if b == 0:
                                    masks = [(MW0S[:, :], MW0S)]
                                elif b == 3:
                                    masks = [(E15[:, g, :], E15)]
                            else:
                                if b == 15:
                                    masks = [(E15[:, g, :], E15)]
                                sx = (FEXP[:, b * 128:(b + 1) * 128], selT[g][:, s * 8:(s + 1) * 8], selT[g], 8)
                            attn_slot(g, KTs[64 * g:64 * g + 64, b, :], KTs, qap, 32, Vs[:, b, g * 65:(g + 1) * 65], Vs, masks, col0, b == 0, False, selexp=sx)
                        ok, ov = (ownKw, ownVw) if branch == "win" else (ownKs, ownVs)
                        attn_slot(g, ok[64 * g:64 * g + 64, :], ok, qap, 32, ov[:, g * 65:(g + 1) * 65], ov, [(ESD[:, g * 16 + s, :], ESD)], col0, False, True)
                flush_slots()
                for g in range(2):
                    op("act", lambda e: e.copy(oTs[0:65, :].rearrange("p (r s t) -> p r s t", r=4, t=8), psO[g][0:65, :].rearrange("p (s r t) -> p r s t", r=4, t=8)), [psO[g]], [oTs])
                    for r in range(4):
                        op("pe", lambda e: e.matmul(psX[:, r * 65:(r + 1) * 65], lhsT=oTs[0:65, r * 128:(r + 1) * 128], rhs=identf[0:65, 0:65], start=True, stop=True), [oTs, identf], [psX])
                    gate0 = 16 if branch == "win" else 8
                    den = psX[:, 0:260].rearrange("p (r c) -> p r c", c=65)[:, :, 64]
                    op("dve", lambda e: e.tensor_scalar(out=wgt[:, 0:4], in0=den, scalar1=1e-30, scalar2=None, op0=ALU.max), [psX], [wgt])
                    op("dve", lambda e: e.reciprocal(wgt[:, 4:8], wgt[:, 0:4]), [wgt], [wgt])
                    op("dve", lambda e: e.tensor_tensor(out=wgt[:, 0:4], in0=wgt[:, 4:8], in1=nbgs[:, gate0 + 4 * g:gate0 + 4 * g + 4], op=ALU.mult), [wgt, nbgs], [wgt])
                    for r in range(4):
                        h = 4 * g + r
                        op("dve", lambda e: e.scalar_tensor_tensor(out=oc[:, h * 64:(h + 1) * 64], in0=psX[:, r * 65:r * 65 + 64], scalar=wgt[:, r:r + 1], in1=oc[:, h * 64:(h + 1) * 64], op0=ALU.mult, op1=ALU.add), [psX, wgt, oc], [oc])
        op("dve", lambda e: e.tensor_tensor(out=gc[:, :], in0=oc[:, :], in1=ngs[:, :], op=ALU.mult), [oc, ngs], [gc])
        for c in range(4):
            transpose_to(gcT[:, c, :], gcT, gc[:, c * 128:(c + 1) * 128], gc)

        for mg_ in range(2):
            wa_t = wload(wb[0], mg_ * 512, 512)
            wbc_t = wload(wb[1], mg_ * 512, 512)
            for mm in range(4):
                m = mg_ * 4 + mm
                pst = pp[pi[0] % 2]; pi[0] += 1
                for kc in range(8):
                    wap, wtl = wa_t.col(kc, mm * 128, (mm + 1) * 128)
                    op("pe", lambda e: e.matmul(pst[:, 0:128], lhsT=wap, rhs=gaT[:, kc, :], start=(kc == 0), stop=(kc == 7)), [wtl, gaT], [pst])
                for kc in range(4):
                    wap, wtl = wbc_t.col(kc, mm * 128, (mm + 1) * 128)
                    op("pe", lambda e: e.matmul(pst[:, 128:256], lhsT=wap, rhs=gbT[:, kc, :], start=(kc == 0), stop=(kc == 3)), [wtl, gbT], [pst])
                for kc in range(4):
                    wap, wtl = wbc_t.col(4 + kc, mm * 128, (mm + 1) * 128)
                    op("pe", lambda e: e.matmul(pst[:, 256:384], lhsT=wap, rhs=gcT[:, kc, :], start=(kc == 0), stop=(kc == 3)), [wtl, gcT], [pst])
                op("dve", lambda e: e.tensor_tensor(out=m0[:, :], in0=pst[:, 0:128], in1=mgs[:, m, :], op=ALU.mult), [pst, mgs], [m0])
                op("dve", lambda e: e.tensor_tensor(out=m1[:, :], in0=pst[:, 128:256], in1=mgs[:, 8 + m, :], op=ALU.mult), [pst, mgs], [m1])
                op("pool", lambda e: e.tensor_tensor(out=m0[:, :], in0=m0[:, :], in1=m1[:, :], op=ALU.add), [m0, m1], [m0])
                op("dve", lambda e: e.tensor_tensor(out=m1[:, :], in0=pst[:, 256:384], in1=mgs[:, 16 + m, :], op=ALU.mult), [pst, mgs], [m1])
                op("pool", lambda e: e.tensor_tensor(out=mT[:, m, :], in0=m0[:, :], in1=m1[:, :], op=ALU.add), [m0, m1], [mT])
        for hf in range(2):
            wo_t = wload(wb[2], hf * 512, 512)
            pst = pp[pi[0] % 2]; pi[0] += 1
            for a0 in (0, 256):
                for kc in range(8):
                    wap, wtl = wo_t.col(kc, a0, a0 + 256)
                    op("pe", lambda e: e.matmul(pst[:, a0:a0 + 256], lhsT=mT[:, kc, :], rhs=wap, start=(kc == 0), stop=(kc == 7)), [wtl, mT], [pst])
            op("dve", lambda e: e.tensor_tensor(out=xt[:, hf * 512:(hf + 1) * 512], in0=xt[:, hf * 512:(hf + 1) * 512], in1=pst[:, :], op=ALU.add), [xt, pst], [xt])
        if l == 0:
            dma("pool", y1p[pos0:pos0 + 128, :] if prm else y1s[:, :], xt[:, :], r=[xt], w=[Y1])
        else:
            dma("sp", sc[:, :], fgain[0:1, :].to_broadcast([128, D]), w=[sc, scH[0], scH[1]])
            rstd = rmsnorm_rstd(xt, 0)
            op("dve", lambda e: e.scalar_tensor_tensor(out=rgs[:, :], in0=xt[:, :], scalar=rstd, in1=sc[:, :], op0=ALU.mult, op1=ALU.mult), [xt, st1, sc, scH[0], scH[1]], [rgs])
            dma("pool", y_p[pos0:pos0 + 128, :] if prm else y_s[:, :], rgs[:, :], r=[rgs])

    Y1 = Tl(None)
    for l in range(DEPTH):
        op("dve", lambda e: e.memset(Sst.h[:], 0.0), [], [Sst])
        op("pool", lambda e: e.memset(Sbf.h[:], 0.0), [], [Sbf])
        op("pool", lambda e: e.memset(pext.h[:], 0.0), [], [pext])
        if l == 1:
            S.finish()
        for T in range(NTP):
            unit(l, "p", T)
        S.barrier_all()
        unit(l, "s", 0)
        S.barrier_all()
    S.finish()
    es.close()
    S.close()
    return nc


_CACHE = {}


def _prep_shared(inp, seq):
    f32 = np.float32
    ci = _col_index()
    tabs = _host_tables(seq)
    sh = {}
    w_in = np.asarray(inp["w_in"], f32)
    w_pad = np.concatenate([w_in, np.zeros(w_in.shape[:2] + (1,), f32)], axis=2)
    sh["w_all"] = np.ascontiguousarray(w_pad[:, :, ci])
    sh["w_b"] = np.ascontiguousarray(np.concatenate([np.asarray(inp[k], f32) for k in ("w_br_a", "w_br_b", "w_br_c", "w_out")], axis=1))
    g = np.asarray(inp["norm_gain"], f32)
    sh["gains_fm"] = np.ascontiguousarray(g.reshape(DEPTH, 8, 128).transpose(2, 0, 1).reshape(128, 16))
    sh["fgain"] = np.asarray(inp["final_gain"], f32).reshape(1, D)
    sh["w_pool"] = np.asarray(inp["w_pool"], f32).reshape(DEPTH, 512, 128)
    sh["pscale_fm"] = np.ascontiguousarray(np.asarray(inp["pool_scale"], f32).reshape(DEPTH, 4, 128).transpose(2, 0, 1).reshape(128, 8))
    sh["pe_cmp"] = np.asarray(inp["pe_cmp"], f32).reshape(DEPTH, 32, 256)
    sh["w_ck"] = np.asarray(inp["w_ck"], f32); sh["w_cv"] = np.asarray(inp["w_cv"], f32)
    sh["rel_bias"] = np.asarray(inp["rel_bias"], f32)
    for l in range(DEPTH):
        sh["c_cmp%d" % l] = np.asarray(inp["cache_cmp"], f32)[l].reshape(-1, 256)
        sh["c_sel%d" % l] = np.asarray(inp["cache_sel"], f32)[l].reshape(-1, 256)
    for k, shp in TABLE_SHAPES(seq).items():
        sh["t_" + k] = np.ascontiguousarray(np.asarray(tabs[k], f32).reshape(shp))
    return sh


def kernel(**inp):
    f32 = np.float32
    xpr = np.asarray(inp["x_prompt"], f32)
    B, seq, _ = xpr.shape
    n_cores = N_CORES
    nb = np.asarray(inp["x_sample"]).shape[0]
    assert nb == n_cores * NS
    if seq not in _CACHE:
        _CACHE[seq] = build(seq)
    nc = _CACHE[seq]
    sh = _prep_shared(inp, seq)
    xs = np.asarray(inp["x_sample"], f32)
    st_ret = np.asarray(inp["state_ret"], f32); st_pool = np.asarray(inp["state_pool"], f32)
    c_win = np.asarray(inp["cache_win"], f32); pt = np.asarray(inp["page_table"], np.int32)
    in_maps = []
    for c in range(n_cores):
        b = c % B
        sl = slice(c * NS, (c + 1) * NS)
        m = dict(sh)
        m["xp"] = np.ascontiguousarray(xpr[b])
        m["xs"] = np.ascontiguousarray(xs[sl].reshape(NS * TS, D))
        m["st_ret"] = np.ascontiguousarray(st_ret[:, sl].reshape(DEPTH, NS * RH * DK, DV))
        m["st_pool"] = np.ascontiguousarray(st_pool[:, sl].reshape(DEPTH, NS * 15, 512))
        m["c_win"] = np.ascontiguousarray(c_win[:, sl].reshape(DEPTH, NS * 512, 256))
        m["ptab"] = np.ascontiguousarray(pt[sl].reshape(1, NS * NPAGE))
        in_maps.append(m)
    res = run_bass_kernel_spmd(nc, in_maps, core_ids=list(range(n_cores))).results
    wk = min(512, seq)

    def cat_p(name, shape):
        return np.stack([res[b][name].reshape(shape) for b in range(B)], axis=1) if shape[0] == DEPTH else None

    y_prompt = np.stack([res[b]["y_p"] for b in range(B)]).astype(f32)
    y_sample = np.concatenate([res[c]["y_s"].reshape(NS, TS, D) for c in range(n_cores)]).astype(f32)
    ret_prompt = np.stack([res[b]["ret_p"].reshape(DEPTH, RH, DK, DV) for b in range(B)], axis=1)
    ret_sample = np.concatenate([res[c]["ret_s"].reshape(DEPTH, NS, RH, DK, DV) for c in range(n_cores)], axis=1)
    pool_prompt = np.stack([res[b]["pool_p"].reshape(DEPTH, 15, 512) for b in range(B)], axis=1)
    pool_sample = np.concatenate([res[c]["pool_s"].reshape(DEPTH, NS, 15, 512) for c in range(n_cores)], axis=1)
    win_prompt = np.stack([res[b]["win_p"].reshape(DEPTH, wk, 2, 2, 64) for b in range(B)], axis=1)
    win_sample = np.concatenate([res[c]["win_s"].reshape(DEPTH, NS, 512, 2, 2, 64) for c in range(n_cores)], axis=1)
    cmp_prompt = np.stack([res[b]["cmp_p"].reshape(DEPTH, seq, 2, 2, 64) for b in range(B)], axis=1)
    cmp_sample = np.concatenate([res[c]["cmp_s"].reshape(DEPTH, NS, TS, 2, 2, 64) for c in range(n_cores)], axis=1)
    sel_prompt = np.stack([res[b]["sel_p"].reshape(DEPTH, seq, 2, 2, 64) for b in range(B)], axis=1)
    sel_sample = np.concatenate([res[c]["sel_s"].reshape(DEPTH, NS, TS, 2, 2, 64) for c in range(n_cores)], axis=1)
    outs = (y_prompt, y_sample, ret_prompt, ret_sample, pool_prompt, pool_sample, win_prompt, win_sample,
            cmp_prompt, cmp_sample, sel_prompt, sel_sample)
    return tuple(np.ascontiguousarray(o, dtype=f32) for o in outs)
```
